# Optimizing a Trainium2 kernel written in Bass

```python
import jax, jax.numpy as jnp
from jax import lax
import numpy as np

D_MODEL = 1024
BATCH = 4
SEQ = 4096
DEPTH = 2
DEC_BATCH = 32
DEC_SEQ = 32
PAST_LEN = 4096

CHUNK = 64
N_MIXERS = 2
N_ATTN_LAYERS = (DEPTH + 1) // 2
N_GMLP_LAYERS = DEPTH // 2
HEAD_DIM = 64
N_HEADS = D_MODEL // HEAD_DIM
N_KV_HEADS = 4
GQA_GROUP = N_HEADS // N_KV_HEADS
WINDOW = 128
WIN_CHUNKS = WINDOW // CHUNK
QKV_DIM = (N_HEADS + 2 * N_KV_HEADS) * HEAD_DIM
GMLP_CHUNK = 128
GMLP_HALF = 3 * D_MODEL
GMLP_GROUPS = 8
GMLP_GROUP_DIM = GMLP_HALF // GMLP_GROUPS
D_FF = 4 * D_MODEL
PLE_DIM = 256
EPS = 1e-6
NEG_INF = -1e30

kernel_name = 'hybrid_swa_sink_gmlp_stream_step'


def rms_norm(x, g):
    xf = x.astype(jnp.float32)
    y = xf * lax.rsqrt(jnp.mean(xf * xf, axis=-1, keepdims=True) + EPS)
    return (y * g.astype(jnp.float32)).astype(x.dtype)


def alibi_slopes():
    h = jnp.arange(1, N_HEADS + 1, dtype=jnp.float32)
    return jnp.exp2(-8.0 * h / N_HEADS).reshape(N_KV_HEADS, GQA_GROUP)


def sink_alibi_attention(q, k, v, dist, valid, sinks):
    s = jnp.einsum('bnqkgd,bnskd->bnkgqs', q, k, preferred_element_type=jnp.float32)
    s = s * (HEAD_DIM ** -0.5)
    s = s - alibi_slopes()[:, :, None, None] * dist[None, None]
    if valid is not None:
        s = jnp.where(valid[None, :, None, None, None, :], s, NEG_INF)
    sink = sinks.astype(jnp.float32).reshape(N_KV_HEADS, GQA_GROUP)[None, None, :, :, None, None]
    m = jnp.maximum(jnp.max(s, axis=-1, keepdims=True), sink)
    e = jnp.exp(s - m)
    probs = e / (jnp.sum(e, axis=-1, keepdims=True) + jnp.exp(sink - m))
    return jnp.einsum('bnkgqs,bnskd->bnqkgd', probs.astype(v.dtype), v)


def attention_mixer(h, w_qkv, q_g, k_g, sinks, w_o, cache_k, cache_v):
    B, T, _ = h.shape
    qkv = h @ w_qkv
    q = qkv[..., :N_HEADS * HEAD_DIM].reshape(B, T, N_KV_HEADS, GQA_GROUP, HEAD_DIM)
    k = qkv[..., N_HEADS * HEAD_DIM:(N_HEADS + N_KV_HEADS) * HEAD_DIM].reshape(B, T, N_KV_HEADS, HEAD_DIM)
    v = qkv[..., (N_HEADS + N_KV_HEADS) * HEAD_DIM:].reshape(B, T, N_KV_HEADS, HEAD_DIM)
    q = rms_norm(q, q_g)
    k = rms_norm(k, k_g)
    if cache_k is None:
        nc = T // CHUNK
        pad = ((0, 0), (WINDOW, 0), (0, 0), (0, 0))
        kp = jnp.pad(k, pad).reshape(B, nc + WIN_CHUNKS, CHUNK, N_KV_HEADS, HEAD_DIM)
        vp = jnp.pad(v, pad).reshape(B, nc + WIN_CHUNKS, CHUNK, N_KV_HEADS, HEAD_DIM)
        kb = jnp.concatenate([kp[:, j:j + nc] for j in range(WIN_CHUNKS + 1)], axis=2)
        vb = jnp.concatenate([vp[:, j:j + nc] for j in range(WIN_CHUNKS + 1)], axis=2)
        qb = q.reshape(B, nc, CHUNK, N_KV_HEADS, GQA_GROUP, HEAD_DIM)
        qi = jnp.arange(CHUNK)[:, None]
        kj = jnp.arange(WINDOW + CHUNK)[None, :]
        dist = jnp.abs(qi + WINDOW - kj).astype(jnp.float32)
        key_pos = jnp.arange(nc)[:, None] * CHUNK - WINDOW + kj
        o = sink_alibi_attention(qb, kb, vb, dist, key_pos >= 0, sinks)
        new_k, new_v = k[:, T - WINDOW:], v[:, T - WINDOW:]
    else:
        W = cache_k.shape[1]
        kc = jnp.concatenate([cache_k.astype(k.dtype), k], axis=1)
        vc = jnp.concatenate([cache_v.astype(v.dtype), v], axis=1)
        qi = jnp.arange(T)[:, None]
        kj = jnp.arange(W + T)[None, :]
        dist = jnp.abs(qi + W - kj).astype(jnp.float32)
        o = sink_alibi_attention(q[:, None], kc[:, None], vc[:, None], dist, None, sinks)
        new_k, new_v = kc[:, T:], vc[:, T:]
    y = o.reshape(B, T, N_HEADS * HEAD_DIM) @ w_o
    return y, new_k, new_v


def gmlp_mixer(h, w_uv, v_g, w_s, b_s, w_out):
    B, T, _ = h.shape
    z = jax.nn.gelu(h @ w_uv)
    u, v = z[..., :GMLP_HALF], z[..., GMLP_HALF:]
    v = rms_norm(v, v_g)
    L = min(T, GMLP_CHUNK)
    nc = T // L
    ws = w_s[:, :L, :L] * jnp.tril(jnp.ones((L, L), w_s.dtype))
    vb = v.reshape(B, nc, L, GMLP_GROUPS, GMLP_GROUP_DIM)
    s = jnp.einsum('gij,bnjgc->bnigc', ws, vb) + b_s[:, :L].T[None, None, :, :, None]
    y = (u * s.reshape(B, T, GMLP_HALF)) @ w_out
    return y, v


def trunk(x, p, cache_k, cache_v, g_mix, g_ffn, g_ple,
          attn_w_qkv, attn_q_norm, attn_k_norm, attn_sinks, attn_w_o,
          gmlp_w_uv, gmlp_v_norm, gmlp_w_s, gmlp_b_s, gmlp_w_out,
          ffn_w1, ffn_w2, ple_w_proj, ple_w_gate):
    h = x
    ks, vs, vrows = [], [], []
    for i in range(DEPTH):
        n = rms_norm(h, g_mix[i])
        if i % N_MIXERS == 0:
            a = i // N_MIXERS
            ck = None if cache_k is None else cache_k[a]
            cv = None if cache_v is None else cache_v[a]
            y, nk, nv = attention_mixer(n, attn_w_qkv[a], attn_q_norm[a], attn_k_norm[a],
                                        attn_sinks[a], attn_w_o[a], ck, cv)
            ks.append(nk)
            vs.append(nv)
        else:
            a = i // N_MIXERS
            y, vr = gmlp_mixer(n, gmlp_w_uv[a], gmlp_v_norm[a], gmlp_w_s[a], gmlp_b_s[a], gmlp_w_out[a])
            vrows.append(vr)
        h = h + y
        n = rms_norm(h, g_ffn[i])
        h = h + jnp.square(jax.nn.relu(n @ ffn_w1[i])) @ ffn_w2[i]
        gate = jax.nn.sigmoid(rms_norm(h, g_ple[i]) @ ple_w_gate[i])
        h = h + gate * (p[i] @ ple_w_proj[i])
    return h, jnp.stack(ks), jnp.stack(vs), jnp.stack(vrows)


def setup_inputs(seed: int = 0) -> dict:
    key = jax.random.key(seed)
    ks = jax.random.split(key, 24)
    f32 = jnp.float32
    nrm = lambda k, shape, scale: jax.random.normal(k, shape, f32) * scale
    win_rows = min(WINDOW, PAST_LEN)
    return {
        'x_prompt': nrm(ks[0], (BATCH, SEQ, D_MODEL), 1.0),
        'x_sample': nrm(ks[1], (DEC_BATCH, DEC_SEQ, D_MODEL), 1.0),
        'p_prompt': nrm(ks[2], (DEPTH, BATCH, SEQ, PLE_DIM), 1.0),
        'p_sample': nrm(ks[3], (DEPTH, DEC_BATCH, DEC_SEQ, PLE_DIM), 1.0),
        'cache_k': nrm(ks[4], (N_ATTN_LAYERS, DEC_BATCH, win_rows, N_KV_HEADS, HEAD_DIM), 1.0),
        'cache_v': nrm(ks[5], (N_ATTN_LAYERS, DEC_BATCH, win_rows, N_KV_HEADS, HEAD_DIM), 1.0),
        'g_mix': 1.0 + nrm(ks[6], (DEPTH, D_MODEL), 0.05),
        'g_ffn': 1.0 + nrm(ks[7], (DEPTH, D_MODEL), 0.05),
        'g_ple': 1.0 + nrm(ks[8], (DEPTH, D_MODEL), 0.05),
        'attn_w_qkv': nrm(ks[9], (N_ATTN_LAYERS, D_MODEL, QKV_DIM), D_MODEL ** -0.5),
        'attn_q_norm': 1.0 + nrm(ks[10], (N_ATTN_LAYERS, HEAD_DIM), 0.05),
        'attn_k_norm': 1.0 + nrm(ks[11], (N_ATTN_LAYERS, HEAD_DIM), 0.05),
        'attn_sinks': nrm(ks[12], (N_ATTN_LAYERS, N_HEADS), 1.0),
        'attn_w_o': nrm(ks[13], (N_ATTN_LAYERS, N_HEADS * HEAD_DIM, D_MODEL), (N_HEADS * HEAD_DIM) ** -0.5),
        'gmlp_w_uv': nrm(ks[14], (N_GMLP_LAYERS, D_MODEL, 2 * GMLP_HALF), D_MODEL ** -0.5),
        'gmlp_v_norm': 1.0 + nrm(ks[15], (N_GMLP_LAYERS, GMLP_HALF), 0.05),
        'gmlp_w_s': nrm(ks[16], (N_GMLP_LAYERS, GMLP_GROUPS, GMLP_CHUNK, GMLP_CHUNK), 0.5 * GMLP_CHUNK ** -0.5),
        'gmlp_b_s': 1.0 + nrm(ks[17], (N_GMLP_LAYERS, GMLP_GROUPS, GMLP_CHUNK), 0.02),
        'gmlp_w_out': nrm(ks[18], (N_GMLP_LAYERS, GMLP_HALF, D_MODEL), GMLP_HALF ** -0.5),
        'ffn_w1': nrm(ks[19], (DEPTH, D_MODEL, D_FF), D_MODEL ** -0.5),
        'ffn_w2': nrm(ks[20], (DEPTH, D_FF, D_MODEL), D_FF ** -0.5),
        'ple_w_proj': nrm(ks[21], (DEPTH, PLE_DIM, D_MODEL), PLE_DIM ** -0.5),
        'ple_w_gate': nrm(ks[22], (DEPTH, D_MODEL, D_MODEL), D_MODEL ** -0.5),
    }


def reference(x_prompt, x_sample, p_prompt, p_sample, cache_k, cache_v,
              g_mix, g_ffn, g_ple,
              attn_w_qkv, attn_q_norm, attn_k_norm, attn_sinks, attn_w_o,
              gmlp_w_uv, gmlp_v_norm, gmlp_w_s, gmlp_b_s, gmlp_w_out,
              ffn_w1, ffn_w2, ple_w_proj, ple_w_gate):
    weights = (g_mix, g_ffn, g_ple,
               attn_w_qkv, attn_q_norm, attn_k_norm, attn_sinks, attn_w_o,
               gmlp_w_uv, gmlp_v_norm, gmlp_w_s, gmlp_b_s, gmlp_w_out,
               ffn_w1, ffn_w2, ple_w_proj, ple_w_gate)
    y_prompt, new_k_prompt, new_v_prompt, _ = trunk(x_prompt, p_prompt, None, None, *weights)
    y_sample, new_k_sample, new_v_sample, new_gmlp_v_sample = trunk(x_sample, p_sample, cache_k, cache_v, *weights)
    return (y_prompt, y_sample, new_k_prompt, new_v_prompt, new_k_sample, new_v_sample, new_gmlp_v_sample)
```

```python
from contextlib import ExitStack
import numpy as np
import concourse.bass as bass
import concourse.mybir as mybir
from concourse.bass_utils import run_bass_kernel_spmd

F32 = mybir.dt.float32
BF16 = mybir.dt.bfloat16
AF = mybir.ActivationFunctionType
ALU = mybir.AluOpType
AX = mybir.AxisListType

EPS = 1e-6
N_HEADS = 16
SLOPES = [2.0 ** (-8.0 * (h + 1) / N_HEADS) for h in range(N_HEADS)]
NEG = -32768.0


class Buf:
    __slots__ = ("name", "t", "lw", "rd", "dsem", "psum")

    def __init__(self, name, t=None, psum=False):
        self.name = name
        self.t = t
        self.psum = psum
        self.lw = []
        self.rd = []
        self.dsem = None


class Sched:
    def __init__(self, nc):
        self.nc = nc
        self.engs = {"pe": nc.tensor, "act": nc.scalar, "dve": nc.vector, "pool": nc.gpsimd, "sp": nc.sync}
        self.sem, self.cnt, self.unit = {}, {}, {}
        self.known = {e: {} for e in self.engs}
        self.clock = {}
        self.stack = []
        for e in self.engs:
            self._mksrc(e, 1)
        self.n_wait = 0
        self.n_inst = 0
        self.phase = "setup"
        self.pe_phase = []
        self.pe_pending = []
        self.pe_waits = []

    def _mksrc(self, name, unit):
        cm = self.nc.semaphore("s_" + name)
        self.sem[name] = cm.__enter__()
        self.stack.append(cm)
        self.cnt[name] = 0
        self.unit[name] = unit

    def close(self):
        for cm in reversed(self.stack):
            cm.__exit__(None, None, None)

    def _need(self, eng, deps):
        kn = self.known[eng]
        best = {}
        for (s, i) in deps:
            if kn.get(s, 0) >= i:
                continue
            if best.get(s, 0) < i:
                best[s] = i
        for s, i in sorted(best.items(), key=lambda x: -x[1]):
            if kn.get(s, 0) >= i:
                continue
            self.engs[eng].wait_ge(self.sem[s], i * self.unit[s])
            self.n_wait += 1
            if eng == "pe":
                self.pe_pending.append(s)
            kn[s] = i
            ck = self.clock.get((s, i))
            if ck:
                for s2, i2 in ck.items():
                    if kn.get(s2, 0) < i2:
                        kn[s2] = i2

    def _deps(self, eng, reads, writes):
        deps = []
        for b in reads:
            deps.extend(b.lw)
            if b.psum:
                deps.extend(d for d in b.rd if d[0] != eng)
        for b in writes:
            deps.extend(b.lw)
            deps.extend(b.rd)
        if eng == "pe":
            deps = [d for d in deps if d[0] != "pe"]
        return deps

    def op(self, eng, fn, reads=(), writes=(), inc=True):
        self._need(eng, self._deps(eng, reads, writes))
        ins = fn()
        self.n_inst += 1
        if eng == "pe":
            self.pe_phase.append(self.phase)
            self.pe_waits.append(tuple(self.pe_pending))
            self.pe_pending = []
        idx = self.cnt[eng] + 1
        if inc:
            ins.then_inc(self.sem[eng], 1)
            self.cnt[eng] = idx
            self.clock[(eng, idx)] = dict(self.known[eng])
        tag = (eng, idx)
        for b in reads:
            b.rd.append(tag)
        for b in writes:
            b.lw = [tag]
            b.rd = []
        return ins

    def dma(self, q, out_ap, in_ap, dst=None, src=None, more=False):
        owner = dst if dst is not None else src
        reads = [src] if src is not None else []
        writes = [dst] if dst is not None else []
        if not more:
            self._need(q, self._deps(q, reads, writes))
        if owner.dsem is None:
            owner.dsem = "d_" + owner.name
            self._mksrc(owner.dsem, 16)
        s = owner.dsem
        ins = self.engs[q].dma_start(out=out_ap, in_=in_ap)
        ins.then_inc(self.sem[s], 16)
        self.n_inst += 1
        idx = self.cnt[s] + 1
        self.cnt[s] = idx
        self.clock[(s, idx)] = dict(self.known[q])
        tag = (s, idx)
        for b in reads:
            b.rd.append(tag)
        for b in writes:
            b.lw = [tag]
            if not more:
                b.rd = []
        return ins

    def inherit(self, new_bufs, old_bufs):
        acc = []
        for b in old_bufs:
            acc.extend(b.lw)
            acc.extend(b.rd)
        for nb in new_bufs:
            nb.rd = list(nb.rd) + acc

    def finish(self, eng, bufs):
        deps = []
        for b in bufs:
            deps.extend(b.lw)
            deps.extend(b.rd)
        self._need(eng, deps)


class _Stop(Exception):
    pass


XIN = 2


def build_program(NST, TPS, stop=None, skip=()):
    nc = bass.Bass("TRN2", target_bir_lowering=False)
    NTP = NST * TPS * 128
    TP = TPS * 128
    TMAX = TP + 128
    KSL = TMAX + 128

    def din(name, shape):
        return nc.dram_tensor(name, list(shape), F32, kind="ExternalInput").ap()

    def dout(name, shape):
        return nc.dram_tensor(name, list(shape), F32, kind="ExternalOutput").ap()

    xm = din("xm", [NTP, 1024]); xh = din("xh", [128, 1024]); xs = din("xs", [128, 1024])
    pm = din("pm", [2, NTP, 256]); psm = din("psm", [2, 128, 256])
    ck = din("ck", [4, 128, 256]); cv = din("cv", [4, 128, 256])
    gvec = din("gvec", [48, 128]); qg = din("qg", [64]); kg = din("kg", [64])
    qscale = din("qscale", [128, 8]); sinks = din("sinks", [16])
    w_qkv = din("w_qkv", [1024, 1536]); w_o = din("w_o", [1024, 1024])
    w_uv = din("w_uv", [1024, 6144]); vg = din("vg", [3072])
    w_s = din("w_s", [8, 128, 128]); b_s = din("b_s", [8, 128]); w_out = din("w_out", [3072, 1024])
    w1 = din("w1", [2, 1024, 4096]); w2 = din("w2", [2, 4096, 1024])
    wp = din("wp", [2, 256, 1024]); wg = din("wg", [2, 1024, 1024])
    tril = din("tril", [128, 128]); cbias = din("cbias", [9, 128, 128])

    ym = dout("ym", [NTP, 1024]); ys = dout("ys", [128, 1024])
    nkp = dout("nkp", [128, 256]); nvp = dout("nvp", [128, 256])
    nks = dout("nks", [4, 128, 256]); nvs = dout("nvs", [4, 128, 256])
    gvs = dout("gvs", [128, 3072])

    S = Sched(nc)
    es = ExitStack()

    def sb(name, shape, dt):
        return Buf(name, es.enter_context(nc.sbuf_tensor(name, list(shape), dt)))

    hT = [sb(f"hT{c}", [128, TMAX], F32) for c in range(8)]
    nT = sb("nT", [128, 8, TMAX], BF16)
    nTc = [Buf(f"nTc{c}", nT.t) for c in range(8)]
    sqr = [sb(f"sq{i}", [128, 512], BF16) for i in range(3)]
    rstd = sb("rstd", [128, TMAX], F32)
    rq = [sb(f"rq{i}", [128, 512], F32) for i in range(2)]
    stage = [sb(f"stage{i}", [128, 1024], F32) for i in range(3)]
    xin = [sb(f"xin{i}", [128, 1024], F32) for i in range(XIN)]
    pstage = [sb(f"pstage{i}", [128, 256], F32) for i in range(TPS + 1)]
    kTe = sb("kTe", [128, 4, KSL], BF16)
    kTo = sb("kTo", [128, 4, KSL], BF16)
    vaug = sb("vaug", [128, TPS + 2, 4, 65], BF16)
    rcs = [sb(f"rcs{i}", [128, 4], F32) for i in range(3)]
    pT = sb("pT", [128, 2, TMAX], BF16)
    identf = sb("identf", [128, 128], F32)
    identb = sb("identb", [128, 128], BF16)
    ones_d = sb("ones_d", [128, 128], BF16)
    ones_b = sb("ones_b", [128, 128], BF16)
    blk1 = sb("blk1", [128, 128], BF16)
    epsb = sb("epsb", [128, 1], F32)
    gT = sb("gT", [128, 48], F32)
    gq = sb("gq", [128, 8], F32)
    qgk = sb("qgk", [128, 2], F32)
    qsc = sb("qsc", [128, 8], F32)
    esink = sb("esink", [128, 16], F32)
    vgb = sb("vgb", [128, 3072], F32)
    bias4 = sb("bias4", [128, 9 * 128], BF16)
    WtT = sb("WtT", [128, 2, 8, 128], BF16)
    bbc = sb("bbc", [128, 1024], F32)
    ssq = sb("ssq", [128, (TPS + 1) * 6], F32)
    rsv = sb("rsv", [128, TPS + 1], F32)
    NWB = 5
    wring = [sb(f"wr{i}", [128, 4096], BF16) for i in range(NWB)]
    arena = es.enter_context(nc.sbuf_tensor("arena", [128, 30 * 1024], BF16))
    AW = 30 * 1024

    class Carve:
        def __init__(self):
            self.off = 0

        def take(self, name, shape, dt):
            n = int(np.prod(shape[1:]))
            if dt == F32:
                self.off = (self.off + 1) // 2 * 2
                ap = arena[:, self.off:self.off + 2 * n].bitcast(F32)
                self.off += 2 * n
            else:
                ap = arena[:, self.off:self.off + n]
                self.off += n
            assert self.off <= AW, (name, self.off)
            if len(shape) == 3:
                ap = ap.rearrange("p (a b) -> p a b", a=shape[1])
            elif len(shape) == 4:
                ap = ap.rearrange("p (a b c) -> p a b c", a=shape[1], b=shape[2])
            return Buf(name, ap)

    cA = Carve()
    qT = cA.take("qT", [128, 8, TMAX], BF16)
    oT = cA.take("oT", [128, 8, TMAX], BF16)
    Pb = [cA.take(f"P{i}", [128, 512], BF16) for i in range(5)]
    Pu = [cA.take(f"Pu{i}", [128, 2, 512], BF16) for i in range(4)]
    otok = [cA.take(f"otok{i}", [128, 1024], BF16) for i in range(2)]
    kf32 = cA.take("kf32", [128, 2, 4, 128], F32)
    hTh = cA.take("hTh", [128, 8, 128], F32)
    nTh = cA.take("nTh", [128, 8, 128], BF16)
    kcTe = cA.take("kcTe", [128, 4, 4, 128], BF16)
    kcTo = cA.take("kcTo", [128, 4, 4, 128], BF16)
    vcaug = cA.take("vcaug", [128, 4, 4, 65], BF16)
    attn_bufs = [qT, oT] + Pb + Pu + otok + [kf32, hTh, nTh, kcTe, kcTo, vcaug]
    cF = Carve()
    hidT = cF.take("hidT", [128, 32, TMAX], BF16)
    hq = [Buf(f"hq{j}", hidT.t) for j in range(8)]
    gsb = [cF.take(f"gsb{i}", [128, 512], F32) for i in range(2)]
    ffn_bufs = hq + gsb
    cG = Carve()
    uT = cG.take("uT", [128, 24, TMAX], BF16)
    gv = cG.take("gv", [128, TPS + 1, 3072], BF16)
    ug = [Buf(f"ug{g}", uT.t) for g in range(8)]
    gvt = [Buf(f"gvt{t}", gv.t) for t in range(TPS + 1)]
    gm_bufs = ug + gvt
    cS = Carve()
    wsf = cS.take("wsf", [128, 2, 8, 128], F32)
    cbf = cS.take("cbf", [128, 9, 128], F32)
    gvs_ = cS.take("gvs_", [128, 128], F32)
    trl = cS.take("trl", [128, 128], F32)
    sinkb = cS.take("sinkb", [128, 16], F32)
    setup_bufs = [wsf, cbf, gvs_, trl, sinkb]

    pbank = [Buf(f"pb{i}", es.enter_context(nc.psum_tensor(f"pb{i}", [128, 512], F32)), psum=True) for i in range(8)]
    poolA = pbank[0:3]
    poolB = pbank[3:7]
    bankC = pbank[7]
    rr = {"A": 0, "B": 0, "sq": 0, "rq": 0, "st": 0, "pst": 0, "P": 0, "rcs": 0, "otok": 0, "gsb": 0, "QK": 0, "QS": 0, "xin": 0, "sqk": 0}

    def nxt(key, lst):
        i = rr[key]
        rr[key] = i + 1
        return lst[i % len(lst)]

    def A():
        return nxt("A", poolA)

    V = nc.vector
    ACT = nc.scalar
    PE = nc.tensor
    POOL = nc.gpsimd

    wspecs = []

    def wv(ap2d, c0, c1, r0=None, r1=None):
        v = ap2d.rearrange("(c p) f -> p c f", p=128)
        if r0 is not None:
            v = v[:, r0:r1, :]
        return v[:, :, c0:c1]

    def plan_weights():
        for st in range(NST):
            wspecs.append(("wqA", [(None, wv(w_qkv, 0, 512))], 8, 512))
            wspecs.append(("wqB", [(None, wv(w_qkv, 512, 1024))], 8, 512))
            wspecs.append(("wkD", "dupk", 8, 512))
            wspecs.append(("wvD", [(None, wv(w_qkv, 1280, 1536))], 8, 256))
            wspecs.append(("woA", [(None, wv(w_o, 0, 512))], 8, 512))
            wspecs.append(("woB", [(None, wv(w_o, 512, 1024))], 8, 512))
            for l in range(2):
                if l == 1:
                    for j in range(6):
                        wspecs.append((f"wvv{j}", [(None, wv(w_uv, 3072 + j * 512, 3072 + (j + 1) * 512))], 8, 512))
                    for j in range(6):
                        wspecs.append((f"wu{j}", [(None, wv(w_uv, j * 512, (j + 1) * 512))], 8, 512))
                    for half in range(2):
                        for rb in range(3):
                            wspecs.append((f"wout{half}{rb}", [(None, wv(w_out, half * 512, (half + 1) * 512, rb * 8, rb * 8 + 8))], 8, 512))
                for j in range(8):
                    wspecs.append((f"w1_{l}_{j}", [(None, wv(w1[l], j * 512, (j + 1) * 512))], 8, 512))
                for half in range(2):
                    for rb in range(4):
                        wspecs.append((f"w2_{l}_{half}{rb}", [(None, wv(w2[l], half * 512, (half + 1) * 512, rb * 8, rb * 8 + 8))], 8, 512))
                wspecs.append((f"wp_{l}", [(None, wv(wp[l], 0, 1024))], 2, 1024))
                wspecs.append((f"wgA_{l}", [(None, wv(wg[l], 0, 512))], 8, 512))
                wspecs.append((f"wgB_{l}", [(None, wv(wg[l], 512, 1024))], 8, 512))

    plan_weights()
    wstate = {"issued": 0, "used": 0, "rel": 0}
    NBLK = len(wspecs) // NST
    wscr = nc.dram_tensor("wscr", [NBLK, 128, 4096], BF16).ap()
    scr = [Buf(f"scr{b}") for b in range(NBLK)]
    USE_SCR = NST > 1

    def w_issue(i):
        tag, srcs, kc, cols = wspecs[i]
        buf = wring[i % NWB]
        view = buf.t[:, 0:kc * cols].rearrange("p (c f) -> p c f", c=kc)
        if USE_SCR and i >= NBLK:
            b = i % NBLK
            S.dma("pool", buf.t[:, 0:kc * cols], wscr[b, :, 0:kc * cols], dst=buf, src=scr[b])
        elif srcs == "dupk" or srcs == "dupv":
            base = 1024 if srcs == "dupk" else 1280
            src = w_qkv.rearrange("(c p) f -> p c f", p=128)[:, :, base:base + 256].rearrange("p c (g d) -> p c g d", g=4)
            v4 = view.rearrange("p c (g two d) -> p c g two d", g=4, two=2)
            for kc_ in range(8):
                for two in range(2):
                    S.dma("pool", v4[:, kc_, :, two, :], src[:, kc_, :, :], dst=buf, more=not (kc_ == 0 and two == 0))
        else:
            first = True
            for _, sap in srcs:
                S.dma("pool", view, sap, dst=buf, more=not first)
                first = False

    def w_pump():
        while wstate["issued"] < min(len(wspecs), wstate["rel"] + NWB):
            w_issue(wstate["issued"])
            wstate["issued"] += 1

    def wdone(n=1):
        for r in range(wstate["rel"], wstate["rel"] + n):
            if USE_SCR and r < NBLK:
                _, _, kc, cols = wspecs[r]
                buf = wring[r % NWB]
                S.dma("sp", wscr[r, :, 0:kc * cols], buf.t[:, 0:kc * cols], dst=scr[r], src=buf)
        wstate["rel"] += n
        assert wstate["rel"] <= wstate["used"]
        w_pump()

    def wget(tag):
        i = wstate["used"]
        assert wspecs[i][0] == tag, (wspecs[i][0], tag)
        wstate["used"] = i + 1
        w_pump()
        assert wstate["issued"] > i, (tag, wstate)
        _, _, kc, cols = wspecs[i]
        buf = wring[i % NWB]
        return buf, buf.t[:, 0:kc * cols].rearrange("p (c f) -> p c f", c=kc)

    xpre = {}

    def x_issue(st_, t):
        if (st_, t) in xpre or st_ >= NST or t >= TPS:
            return
        stg = nxt("xin", xin)
        S.dma("sp", stg.t[:], xm[st_ * TP + t * 128: st_ * TP + (t + 1) * 128, :], dst=stg)
        xpre[(st_, t)] = stg

    for t_ in range(XIN):
        x_issue(0, t_)
    S.op("pool", lambda: POOL.memset(identf.t[:], 0.0), writes=[identf])
    S.op("pool", lambda: POOL.affine_select(out=identf.t[:], in_=identf.t[:], pattern=[[-1, 128]], compare_op=ALU.not_equal, fill=1.0, base=0, channel_multiplier=1), reads=[identf], writes=[identf])
    S.op("dve", lambda: V.tensor_copy(identb.t[:], identf.t[:]), reads=[identf], writes=[identb])
    S.op("dve", lambda: V.memset(ones_d.t[:], 1.0 / 1024), writes=[ones_d])
    S.op("dve", lambda: V.memset(ones_b.t[:], 1.0), writes=[ones_b])
    S.op("dve", lambda: V.memset(blk1.t[:], 0.0), writes=[blk1])
    S.op("dve", lambda: V.memset(blk1.t[0:64, 0:64], 1.0 / 64), writes=[blk1])
    S.op("dve", lambda: V.memset(blk1.t[64:128, 64:128], 1.0 / 64), writes=[blk1])
    S.op("dve", lambda: V.memset(epsb.t[:], EPS), writes=[epsb])
    S.op("dve", lambda: V.memset(kTe.t[:], 0.0), writes=[kTe])
    S.op("dve", lambda: V.memset(kTo.t[:], 0.0), writes=[kTo])
    S.op("dve", lambda: V.memset(vaug.t[:], 1.0), writes=[vaug])
    S.dma("sp", gvs_.t[0:48, :], gvec, dst=gvs_)
    S.dma("sp", trl.t[:], tril, dst=trl)
    S.dma("sp", cbf.t[:], cbias.rearrange("k p q -> p k q"), dst=cbf)
    S.dma("sp", qsc.t[:], qscale, dst=qsc)
    S.dma("sp", qgk.t[0:64, 0:1], qg.rearrange("(p o) -> p o", o=1), dst=qgk)
    S.dma("sp", qgk.t[64:128, 0:1], qg.rearrange("(p o) -> p o", o=1), dst=qgk, more=True)
    S.dma("sp", qgk.t[0:64, 1:2], kg.rearrange("(p o) -> p o", o=1), dst=qgk, more=True)
    S.dma("sp", qgk.t[64:128, 1:2], kg.rearrange("(p o) -> p o", o=1), dst=qgk, more=True)
    S.dma("sp", sinkb.t[:], sinks.partition_broadcast(128), dst=sinkb)
    bk = A()
    S.op("pe", lambda: PE.transpose(bk.t[:, 0:48], gvs_.t[0:48, :], identf.t[0:48, 0:48]), reads=[gvs_, identf], writes=[bk])
    S.op("dve", lambda: V.tensor_copy(gT.t[:], bk.t[:, 0:48]), reads=[bk], writes=[gT])
    S.op("dve", lambda: V.tensor_scalar(gq.t[:], qsc.t[:], qgk.t[:, 0:1], None, ALU.mult), reads=[qsc, qgk], writes=[gq])
    S.op("act", lambda: ACT.activation(esink.t[:], sinkb.t[:], AF.Exp), reads=[sinkb], writes=[esink])
    S.op("dve", lambda: V.tensor_copy(bias4.t[:].rearrange("p (k q) -> p k q", k=9), cbf.t[:]), reads=[cbf], writes=[bias4])

    def setup_spatial():
        S.dma("sp", wsf.t[:, 0, :, :], w_s.rearrange("g i j -> i g j"), dst=wsf)
        S.op("dve", lambda: V.memset(wsf.t[:, 1, :, :], 0.0), writes=[wsf])
        for b in range(4):
            S.dma("sp", wsf.t[32 * b:32 * b + 32, 1, :, 32 * b:32 * b + 32], w_s[:, 0:32, 0:32].rearrange("g i j -> i g j"), dst=wsf, more=(b > 0))
        S.dma("sp", bbc.t[:, :], b_s.rearrange("g i -> (g i)").partition_broadcast(128), dst=bbc)
        S.dma("sp", vgb.t[:], vg.partition_broadcast(128), dst=vgb)
        for st_ in range(2):
            for g in range(8):
                S.op("dve", lambda: V.tensor_tensor(wsf.t[:, st_, g, :], wsf.t[:, st_, g, :], trl.t[:], ALU.mult), reads=[wsf, trl], writes=[wsf])
            for q in range(2):
                bk = A()
                for j in range(4):
                    g = q * 4 + j
                    S.op("pe", lambda: PE.transpose(bk.t[:, j * 128:(j + 1) * 128], wsf.t[:, st_, g, :], identf.t[:]), reads=[wsf, identf], writes=[bk], inc=(j == 3))
                S.op("dve", lambda: V.tensor_copy(WtT.t[:, st_, q * 4:(q + 1) * 4, :], bk.t[:].rearrange("p (g i) -> p g i", g=4)), reads=[bk], writes=[WtT])

    dk = Buf("dk"); dv = Buf("dv")
    S.dma("sp", nks[:, 0:96, :], ck[:, 32:128, :], dst=dk)
    S.dma("sp", nvs[:, 0:96, :], cv[:, 32:128, :], dst=dv)
    out_bufs = [dk, dv]

    def load_tiles_T(src_rows_fn, ntiles, dst_fn, pre=None):
        for t in range(ntiles):
            if pre is not None and t < TPS:
                x_issue(pre, t)
                stg = xpre.pop((pre, t))
            else:
                stg = nxt("xin", xin)
                S.dma("sp", stg.t[:], src_rows_fn(t), dst=stg)
            for half in range(2):
                bk = A()
                for j in range(4):
                    c = half * 4 + j
                    S.op("pe", lambda: PE.transpose(bk.t[:, j * 128:(j + 1) * 128], stg.t[:, c * 128:(c + 1) * 128], identf.t[:]), reads=[stg, identf], writes=[bk], inc=(j == 3))
                for j in range(4):
                    c = half * 4 + j
                    db, dap = dst_fn(c, t)
                    eng = "act" if half == 0 else "dve"
                    if eng == "act":
                        S.op("act", lambda: ACT.copy(dap, bk.t[:, j * 128:(j + 1) * 128]), reads=[bk], writes=[db])
                    else:
                        S.op("dve", lambda: V.tensor_copy(dap, bk.t[:, j * 128:(j + 1) * 128]), reads=[bk], writes=[db])

    def rmsnorm(h_fn, groups, gcol, dbuf, d_fn, rbuf, r_fn):
        for (t0, n) in groups:
            bk = bankC
            for c in range(8):
                hb, hap = h_fn(c, t0, n)
                sq = nxt("sq", sqr)
                S.op("act", lambda: ACT.activation(sq.t[:, 0:n], hap, AF.Square), reads=[hb], writes=[sq])
                S.op("pe", lambda: PE.matmul(bk.t[:, 0:n], ones_d.t[:], sq.t[:, 0:n], start=(c == 0), stop=(c == 7)), reads=[ones_d, sq], writes=[bk], inc=True)
            S.op("act", lambda: ACT.activation(r_fn(t0, n), bk.t[:, 0:n], AF.Ln, bias=epsb.t[:], scale=1.0), reads=[bk, epsb], writes=[rbuf])
            S.op("act", lambda: ACT.activation(r_fn(t0, n), r_fn(t0, n), AF.Exp, scale=-0.5), reads=[rbuf], writes=[rbuf])
            for c in range(8):
                hb, hap = h_fn(c, t0, n)
                S.op("dve", lambda: V.scalar_tensor_tensor(d_fn(c, t0, n), hap, gT.t[:, gcol + c:gcol + c + 1], r_fn(t0, n), ALU.mult, ALU.mult), reads=[hb, gT, rbuf], writes=[dbuf[c]])

    def rmsnorm_deferred(h_fn, groups, gcol, dbuf, d_fn, rbuf, r_fn):
        for (t0, n) in groups:
            bk = bankC
            for c in range(8):
                hb, hap = h_fn(c, t0, n)
                S.op("act", lambda: ACT.activation(d_fn(c, t0, n), hap, AF.Copy, scale=gT.t[:, gcol + c:gcol + c + 1]), reads=[hb, gT], writes=[dbuf[c]])
                sq = nxt("sq", sqr)
                S.op("act", lambda: ACT.activation(sq.t[:, 0:n], hap, AF.Square), reads=[hb], writes=[sq])
                S.op("pe", lambda: PE.matmul(bk.t[:, 0:n], ones_d.t[:], sq.t[:, 0:n], start=(c == 0), stop=(c == 7)), reads=[ones_d, sq], writes=[bk], inc=True)
            S.op("act", lambda: ACT.activation(r_fn(t0, n), bk.t[:, 0:n], AF.Ln, bias=epsb.t[:], scale=1.0), reads=[bk, epsb], writes=[rbuf])
            S.op("act", lambda: ACT.activation(r_fn(t0, n), r_fn(t0, n), AF.Exp, scale=-0.5), reads=[rbuf], writes=[rbuf])

    def h_main(c, t0, n):
        return hT[c], hT[c].t[:, t0:t0 + n]

    def n_main(c, t0, n):
        return nT.t[:, c, t0:t0 + n]

    def r_main(t0, n):
        return rstd.t[:, t0:t0 + n]

    def acc_phase(tag_fn, nrb, rhs_buf, rhs_fn, groups):
        for half in range(2):
            first_c = True
            for rb in range(nrb):
                wb, wvw = wget(tag_fn(half, rb))
                for dcl in range(4):
                    for fc in range(8):
                        for gi, (t0, n) in enumerate(groups):
                            if t0 != TP:
                                bk = poolB[dcl]
                                oap = bk.t[:, 0:n]
                                st = (rb == 0 and fc == 0)
                                sk = False
                            else:
                                bk = bankC
                                oap = bk.t[:, dcl * 128:dcl * 128 + n]
                                st = first_c
                                first_c = False
                                sk = True
                            last = (rb == nrb - 1 and fc == 7)
                            S.op("pe", lambda: PE.matmul(oap, wvw[:, fc, dcl * 128:(dcl + 1) * 128], rhs_fn(rb * 8 + fc, t0, n), start=st, stop=last, skip_group_check=sk),
                                 reads=[wb, rhs_buf(rb * 8 + fc)], writes=[bk], inc=last)
                wdone()
            for dcl in range(4):
                dc = half * 4 + dcl
                for gi, (t0, n) in enumerate(groups):
                    if t0 != TP:
                        bk = poolB[dcl]; iap = bk.t[:, 0:n]
                    else:
                        bk = bankC; iap = bk.t[:, dcl * 128:dcl * 128 + n]
                    S.op("dve", lambda: V.tensor_tensor(hT[dc].t[:, t0:t0 + n], iap, hT[dc].t[:, t0:t0 + n], ALU.add), reads=[bk, hT[dc]], writes=[hT[dc]])

    def ffn(l, groups, mid=None):
        S.phase = f"F{l}norm"
        rmsnorm_deferred(h_main, groups, 16 + l * 8, nTc, n_main, rstd, r_main)
        S.phase = f"F{l}p1"
        for j in range(8):
            wb, wvw = wget(f"w1_{l}_{j}")
            for jj in range(4):
                f = j * 4 + jj
                for (t0, n) in groups:
                    bk = A()
                    for kc in range(8):
                        S.op("pe", lambda: PE.matmul(bk.t[:, 0:n], wvw[:, kc, jj * 128:(jj + 1) * 128], nT.t[:, kc, t0:t0 + n], start=(kc == 0), stop=(kc == 7)), reads=[wb, nTc[kc]], writes=[bk], inc=(kc == 7))
                    rl = nxt("gsb", gsb)
                    S.op("act", lambda: ACT.activation(rl.t[:, 0:n], bk.t[:, 0:n], AF.Relu), reads=[bk], writes=[rl])
                    S.op("dve", lambda: V.tensor_tensor(rl.t[:, 0:n], rl.t[:, 0:n], rstd.t[:, t0:t0 + n], ALU.mult), reads=[rl, rstd], writes=[rl])
                    S.op("dve", lambda: V.tensor_tensor(hidT.t[:, f, t0:t0 + n], rl.t[:, 0:n], rl.t[:, 0:n], ALU.mult), reads=[rl], writes=[hq[j]])
            wdone()
        if mid is not None:
            mid()
        S.phase = f"F{l}p2"
        acc_phase(lambda half, rb: f"w2_{l}_{half}{rb}", 4, lambda f: hq[f // 4], lambda f, t0, n: hidT.t[:, f, t0:t0 + n], groups)

    def ple_load(l, st, ntiles):
        for t in range(ntiles):
            if t < TPS:
                src = pm[l, st * TP + t * 128: st * TP + (t + 1) * 128, :]
            else:
                src = psm[l, :, :]
            S.dma("sp", pstage[t].t[:], src, dst=pstage[t])

    def ple_prep(l, st, ntiles):
        for t in range(ntiles):
            pst = pstage[t]
            bk = A()
            for j in range(2):
                S.op("pe", lambda: PE.transpose(bk.t[:, j * 128:(j + 1) * 128], pst.t[:, j * 128:(j + 1) * 128], identf.t[:]), reads=[pst, identf], writes=[bk], inc=(j == 1))
            S.op("act", lambda: ACT.copy(pT.t[:, :, t * 128:(t + 1) * 128], bk.t[:, 0:256].rearrange("p (c t) -> p c t", c=2)), reads=[bk], writes=[pT])

    def ple(l, st, groups, ntiles, is_last):
        S.phase = f"P{l}"
        rmsnorm(h_main, groups, 32 + l * 8, nTc, n_main, rstd, r_main)
        pbanks = pbank[0:7]
        pr = [0]

        def PB():
            b = pbanks[pr[0] % 7]
            pr[0] += 1
            return b
        wpb, wpv = wget(f"wp_{l}")
        for hf, nm in enumerate(["wgA", "wgB"]):
            wb, wvw = wget(f"{nm}_{l}")
            for dcl in range(4):
                dc = hf * 4 + dcl
                for (t0, n) in groups:
                    bp = PB()
                    for kc in range(2):
                        S.op("pe", lambda: PE.matmul(bp.t[:, 0:n], wpv[:, kc, dc * 128:(dc + 1) * 128], pT.t[:, kc, t0:t0 + n], start=(kc == 0), stop=(kc == 1)), reads=[wpb, pT], writes=[bp], inc=(kc == 1))
                    bg = PB()
                    for kc in range(8):
                        S.op("pe", lambda: PE.matmul(bg.t[:, 0:n], wvw[:, kc, dcl * 128:(dcl + 1) * 128], nT.t[:, kc, t0:t0 + n], start=(kc == 0), stop=(kc == 7)), reads=[wb, nTc[kc]], writes=[bg], inc=(kc == 7))
                    gs = nxt("gsb", gsb)
                    S.op("act", lambda: ACT.activation(gs.t[:, 0:n], bg.t[:, 0:n], AF.Sigmoid), reads=[bg], writes=[gs])
                    S.op("dve", lambda: V.tensor_tensor(gs.t[:, 0:n], bp.t[:, 0:n], gs.t[:, 0:n], ALU.mult), reads=[bp, gs], writes=[gs])
                    S.op("dve", lambda: V.tensor_tensor(hT[dc].t[:, t0:t0 + n], hT[dc].t[:, t0:t0 + n], gs.t[:, 0:n], ALU.add), reads=[hT[dc], gs], writes=[hT[dc]])
        wdone(3)

    def qk_chunk_proj(wb, lhs_fn, rhs_buf, rhs_fn, n):
        bk = nxt("QK", pbank[0:5])
        for kc in range(8):
            S.op("pe", lambda: PE.matmul(bk.t[:, 0:n], lhs_fn(kc), rhs_fn(kc), start=(kc == 0), stop=(kc == 7)), reads=[wb, rhs_buf(kc)], writes=[bk], inc=(kc == 7))
        sq = nxt("sq", sqr)
        S.op("act", lambda: ACT.activation(sq.t[:, 0:n], bk.t[:, 0:n], AF.Square), reads=[bk], writes=[sq])
        return (bk, sq)

    def qk_chunk_norm(bksq, n, writes_fn):
        bk, sq = bksq
        b2 = nxt("QS", pbank[5:7])
        S.op("pe", lambda: PE.matmul(b2.t[:, 0:n], blk1.t[:], sq.t[:, 0:n], start=True, stop=True), reads=[blk1, sq], writes=[b2])
        r = nxt("rq", rq)
        S.op("act", lambda: ACT.activation(r.t[:, 0:n], b2.t[:, 0:n], AF.Ln, bias=epsb.t[:], scale=1.0), reads=[b2, epsb], writes=[r])
        S.op("act", lambda: ACT.activation(r.t[:, 0:n], r.t[:, 0:n], AF.Exp, scale=-0.5), reads=[r], writes=[r])
        writes_fn(bk, r)

    def pipeline(items, proj_fn, norm_fn, depth=2):
        pend = []
        for it in items:
            pend.append((it, proj_fn(it)))
            if len(pend) > depth:
                norm_fn(*pend.pop(0))
        while pend:
            norm_fn(*pend.pop(0))

    def finish_tile(ot, tq, bank):
        bv = bank.t[:].bitcast(BF16)
        for c in range(8):
            S.op("pe", lambda: PE.transpose(bv[:, c * 128:(c + 1) * 128], ot.t[:, c * 128:(c + 1) * 128], identb.t[:]), reads=[ot, identb], writes=[bank], inc=(c == 7))
        S.op("dve", lambda: V.tensor_copy(oT.t[:, :, tq:tq + 128], bv.rearrange("p (c q) -> p c q", c=8)), reads=[bank], writes=[oT])

    def pv_norm(ob, g, ot, mm_list):
        for hh in range(4):
            n_ = len(mm_list[hh])
            for j, (lap, rap, rb) in enumerate(mm_list[hh]):
                last = (hh == 3 and j == n_ - 1)
                S.op("pe", lambda: PE.matmul(ob.t[:, hh * 65:(hh + 1) * 65], lap, rap, start=(j == 0), stop=(j == n_ - 1)), reads=rb, writes=[ob], inc=last)
        rc = nxt("rcs", rcs)
        o3 = ob.t[:, 0:260].rearrange("p (h e) -> p h e", e=65)
        S.op("dve", lambda: V.tensor_tensor(rc.t[:, 0:4], o3[:, :, 64], esink.t[:, 4 * g:4 * g + 4], ALU.add), reads=[ob, esink], writes=[rc])
        S.op("dve", lambda: V.reciprocal(rc.t[:, 0:4], rc.t[:, 0:4]), reads=[rc], writes=[rc])
        S.op("dve", lambda: V.tensor_tensor(ot.t[:, g * 256:(g + 1) * 256].rearrange("p (h d) -> p h d", h=4), o3[:, :, 0:64], rc.t[:, 0:4].unsqueeze(2).to_broadcast([128, 4, 64]), ALU.mult), reads=[ob, rc], writes=[ot])

    def attention_prompt(st, tiles):
        Sx = [poolA[0], poolA[1], poolA[2], bankC]
        units = [(t, g) for t in tiles for g in range(4)]

        def front(i):
            t, g = units[i]
            tq = t * 128
            pbi = 2 if (st == 0 and t == 0) else 0
            pu = Pu[i % 4]
            for half in range(2):
                sbk = Sx[2 * (i % 2) + half]
                S.op("pe", lambda: PE.matmul(sbk.t[:, 0:512], identb.t[:], bias4.t[:, pbi * 128:(pbi + 2) * 128].unsqueeze(1).to_broadcast([128, 2, 256]), start=True, stop=False), reads=[identb, bias4], writes=[sbk], inc=False)
                for hl in range(2):
                    h = 4 * g + 2 * half + hl
                    c = h // 2
                    kb = kTe if h % 2 == 0 else kTo
                    for kt in range(2):
                        last = (hl == 1 and kt == 1)
                        S.op("pe", lambda: PE.matmul(sbk.t[:, hl * 256 + kt * 128:hl * 256 + (kt + 1) * 128], kb.t[:, g, (t + kt) * 128:(t + kt + 1) * 128], qT.t[:, c, tq:tq + 128], start=False, stop=last), reads=[kTe, kTo, qT], writes=[sbk], inc=last)
                for hl in range(2):
                    h = 4 * g + 2 * half + hl
                    S.op("act", lambda: ACT.activation(pu.t[:, half, hl * 256:(hl + 1) * 256], sbk.t[:, hl * 256:(hl + 1) * 256], AF.Exp, scale=float(SLOPES[h])), reads=[sbk], writes=[pu])

        def back(i):
            t, g = units[i]
            tq = t * 128
            pu = Pu[i % 4]
            ob = poolB[i % 2]
            ot = otok[t % 2]
            mm = []
            for hh in range(4):
                half, hl = hh // 2, hh % 2
                mm.append([(pu.t[:, half, hl * 256 + kt * 128:hl * 256 + (kt + 1) * 128], vaug.t[:, t + kt, g, :], [pu, vaug]) for kt in range(2)])
            pv_norm(ob, g, ot, mm)
            if g == 3:
                finish_tile(ot, tq, poolB[2 + (t % 2)])

        LAG = 2
        for i in range(len(units) + LAG):
            if i < len(units):
                front(i)
            if i >= LAG:
                back(i - LAG)

    def attention_tile(tq, keytiles):
        nkt = len(keytiles)
        for g in range(4):
            Ps = []
            for (kbufs, ke_fn, ko_fn, vbuf, v_fn, bidx) in keytiles:
                sbk = A()
                S.op("pe", lambda: PE.matmul(sbk.t[:, 0:512], identb.t[:], bias4.t[:, bidx * 128:(bidx + 1) * 128].unsqueeze(1).to_broadcast([128, 4, 128]), start=True, stop=False), reads=[identb, bias4], writes=[sbk], inc=False)
                for hh in range(4):
                    h = 4 * g + hh
                    c = h // 2
                    kap = ke_fn(g) if h % 2 == 0 else ko_fn(g)
                    S.op("pe", lambda: PE.matmul(sbk.t[:, hh * 128:(hh + 1) * 128], kap, qT.t[:, c, tq:tq + 128], start=False, stop=(hh == 3)), reads=list(kbufs) + [qT], writes=[sbk], inc=(hh == 3))
                pb_ = nxt("P", Pb)
                for hh in range(4):
                    h = 4 * g + hh
                    S.op("act", lambda: ACT.activation(pb_.t[:, hh * 128:(hh + 1) * 128], sbk.t[:, hh * 128:(hh + 1) * 128], AF.Exp, scale=float(SLOPES[h])), reads=[sbk], writes=[pb_])
                Ps.append(pb_)
            ob = poolB[g % 2]
            mm = []
            for hh in range(4):
                mm.append([(Ps[kt].t[:, hh * 128:(hh + 1) * 128], keytiles[kt][4](g), [Ps[kt], keytiles[kt][3]]) for kt in range(nkt)])
            pv_norm(ob, g, otok[0], mm)
        finish_tile(otok[0], tq, poolB[2])

    def chk(name):
        S.phase = name
        if stop == name:
            raise _Stop()

    cur_view = attn_bufs
    try:
      chk("setup")
      for st in range(NST):
          is_last = (st == NST - 1)
          ntiles = TPS + (1 if is_last else 0)
          T = ntiles * 128
          groups = []
          t0 = 0
          while t0 < TP:
              n = min(512, TP - t0)
              groups.append((t0, n))
              t0 += n
          if is_last:
              groups.append((TP, 128))

          if st > 0:
              S.inherit(attn_bufs, cur_view)
              cur_view = attn_bufs

          def xrows(t):
              if t < TPS:
                  return xm[st * TP + t * 128: st * TP + (t + 1) * 128, :]
              return xs[:, :]
          load_tiles_T(xrows, ntiles, lambda c, t: (hT[c], hT[c].t[:, t * 128:(t + 1) * 128]), pre=st)
          if st == 0:
              setup_spatial()
              S.inherit(attn_bufs, setup_bufs)
          chk("load")
          if st == 0:
              load_tiles_T(lambda t: xh[:, :], 1, lambda c, t: (hTh, hTh.t[:, c, :]))
              rmsnorm(lambda c, t0, n: (hTh, hTh.t[:, c, :]), [(0, 128)], 0, [nTh] * 8, lambda c, t0, n: nTh.t[:, c, :], rq[0], lambda t0, n: rq[0].t[:, 0:128])
          if is_last:
              S.op("dve", lambda: V.memset(kcTe.t[:], 0.0), writes=[kcTe])
              S.op("dve", lambda: V.memset(kcTo.t[:], 0.0), writes=[kcTo])
              S.op("dve", lambda: V.memset(vcaug.t[:], 1.0), writes=[vcaug])
              for b in range(4):
                  S.dma("pool", vcaug.t[:, b, :, 0:64], cv[b].rearrange("r (g d) -> r g d", g=4), dst=vcaug, more=(b > 0))
              for b in range(4):
                  stg = nxt("st", stage)
                  for two in range(2):
                      S.dma("sp", stg.t[:, 0:512].rearrange("p (g two d) -> p g two d", g=4, two=2)[:, :, two, :], ck[b].rearrange("r (g d) -> r g d", g=4), dst=stg, more=(two > 0))
                  bk = A()
                  for g in range(4):
                      S.op("pe", lambda: PE.transpose(bk.t[:, g * 128:(g + 1) * 128], stg.t[:, g * 128:(g + 1) * 128], identf.t[:]), reads=[stg, identf], writes=[bk], inc=(g == 3))
                  S.op("dve", lambda: V.tensor_copy(kcTe.t[0:64, b, :, :], bk.t[0:64, :].rearrange("p (g k) -> p g k", g=4)), reads=[bk], writes=[kcTe])
                  S.op("act", lambda: ACT.copy(kcTo.t[64:128, b, :, :], bk.t[64:128, :].rearrange("p (g k) -> p g k", g=4)), reads=[bk], writes=[kcTo])

          chk("prep")
          rmsnorm(h_main, groups, 0, nTc, n_main, rstd, r_main)
          chk("norm0")
          wq = {}
          items = [(c, t0, n) for c in range(8) for (t0, n) in groups]

          def qproj(it):
              c, t0, n = it
              if c == 0 and "a" not in wq:
                  wq["a"] = wget("wqA")
              if c == 4 and "b" not in wq:
                  wdone()
                  wq["b"] = wget("wqB")
              wb, wvw = wq["a"] if c < 4 else wq["b"]
              return qk_chunk_proj(wb, lambda kc: wvw[:, kc, (c % 4) * 128:(c % 4 + 1) * 128], (lambda kc: nTc[kc]), lambda kc: nT.t[:, kc, t0:t0 + n], n)

          def qnorm(it, bk):
              c, t0, n = it

              def wr(bk, r):
                  S.op("dve", lambda: V.scalar_tensor_tensor(qT.t[:, c, t0:t0 + n], bk.t[:, 0:n], gq.t[:, c:c + 1], r.t[:, 0:n], ALU.mult, ALU.mult), reads=[bk, gq, r], writes=[qT])
              qk_chunk_norm(bk, n, wr)
          pipeline(items, qproj, qnorm)
          wdone()
          chk("q")
          wkb, wkv = wget("wkD")
          kgroups = [("m", t0, n) for (t0, n) in groups] + ([("h", 0, 128)] if st == 0 else [])
          items = [(g, kind, t0, n) for g in range(4) for (kind, t0, n) in kgroups]
          need_kout = is_last

          def kproj(it):
              g, kind, t0, n = it
              if kind == "m":
                  return qk_chunk_proj(wkb, lambda kc: wkv[:, kc, g * 128:(g + 1) * 128], (lambda kc: nTc[kc]), lambda kc: nT.t[:, kc, t0:t0 + n], n)
              return qk_chunk_proj(wkb, lambda kc: wkv[:, kc, g * 128:(g + 1) * 128], (lambda kc: nTh), lambda kc: nTh.t[:, kc, :], n)

          def knorm(it, bk):
              g, kind, t0, n = it
              k0 = 128 + t0 if kind == "m" else 0

              def wr(bk, r):
                  S.op("dve", lambda: V.scalar_tensor_tensor(kTe.t[0:64, g, k0:k0 + n], bk.t[0:64, 0:n], qgk.t[0:64, 1:2], r.t[0:64, 0:n], ALU.mult, ALU.mult), reads=[bk, qgk, r], writes=[kTe])
                  S.op("dve", lambda: V.scalar_tensor_tensor(kTo.t[64:128, g, k0:k0 + n], bk.t[64:128, 0:n], qgk.t[64:128, 1:2], r.t[64:128, 0:n], ALU.mult, ALU.mult), reads=[bk, qgk, r], writes=[kTo])
                  if need_kout and kind == "m":
                      outs = []
                      if t0 <= (TPS - 1) * 128 < t0 + n:
                          outs.append(((TPS - 1) * 128 - t0, 0))
                      if t0 == TP:
                          outs.append((0, 1))
                      for (off, which) in outs:
                          S.op("dve", lambda: V.scalar_tensor_tensor(kf32.t[:, which, g, :], bk.t[:, off:off + 128], qgk.t[:, 1:2], r.t[:, off:off + 128], ALU.mult, ALU.mult), reads=[bk, qgk, r], writes=[kf32])
              qk_chunk_norm(bk, n, wr)

          pipeline(items, kproj, knorm)
          wdone()
          if need_kout:
              for which in range(2):
                  bk = A()
                  for g in range(4):
                      S.op("pe", lambda: PE.transpose(bk.t[:, g * 128:(g + 1) * 128], kf32.t[:, which, g, :], identf.t[:]), reads=[kf32, identf], writes=[bk], inc=(g == 3))
                  stg = nxt("st", stage)
                  S.op("act", lambda: ACT.copy(stg.t[:, 0:256].rearrange("p (g d) -> p g d", g=4), bk.t[:].rearrange("p (g x) -> p g x", g=4)[:, :, 0:64]), reads=[bk], writes=[stg])
                  if which == 0:
                      S.dma("sp", nkp[:, :], stg.t[:, 0:256], src=stg)
                  else:
                      for b in range(4):
                          S.dma("sp", nks[b, 96:128, :], stg.t[32 * b:32 * b + 32, 0:256], src=stg, more=(b > 0))
                  out_bufs.append(stg)
          chk("k")
          wvb, wvv_ = wget("wvD")
          vt = [("m", t) for t in range(ntiles)] + ([("h", 0)] if st == 0 else [])
          for (kind, t) in vt:
              bk = A()
              for kc in range(8):
                  lhs = nT.t[:, kc, t * 128:(t + 1) * 128] if kind == "m" else nTh.t[:, kc, :]
                  S.op("pe", lambda: PE.matmul(bk.t[:, 0:256], lhs, wvv_[:, kc, :], start=(kc == 0), stop=(kc == 7)), reads=[wvb, nTc[kc] if kind == "m" else nTh], writes=[bk], inc=(kc == 7))
              slot = t + 1 if kind == "m" else 0
              S.op("act", lambda: ACT.copy(vaug.t[:, slot, :, 0:64], bk.t[:, 0:256].rearrange("p (g d) -> p g d", g=4)), reads=[bk], writes=[vaug])
              if is_last and kind == "m" and t >= TPS - 1:
                  stg = nxt("st", stage)
                  S.op("act", lambda: ACT.copy(stg.t[:, 0:256], bk.t[:, 0:256]), reads=[bk], writes=[stg])
                  if t == TPS - 1:
                      S.dma("sp", nvp[:, :], stg.t[:, 0:256], src=stg)
                  else:
                      for b in range(4):
                          S.dma("sp", nvs[b, 96:128, :], stg.t[32 * b:32 * b + 32, 0:256], src=stg, more=(b > 0))
                  out_bufs.append(stg)
          wdone()
          chk("v")
          attention_prompt(st, list(range(TPS)))
          for t in range(TPS, ntiles):
              kts = []
              for b in range(4):
                  kts.append(([kcTe, kcTo], (lambda g, b=b: kcTe.t[:, b, g, :]), (lambda g, b=b: kcTo.t[:, b, g, :]), vcaug, (lambda g, b=b: vcaug.t[:, b, g, :]), 5 + b))
              kts.append(([kTe, kTo], (lambda g, t=t: kTe.t[:, g, (t + 1) * 128:(t + 2) * 128]), (lambda g, t=t: kTo.t[:, g, (t + 1) * 128:(t + 2) * 128]), vaug, (lambda g, t=t: vaug.t[:, t + 1, g, :]), 4))
              attention_tile(t * 128, kts)
          chk("attn")
          if not is_last:
              S.op("dve", lambda: V.tensor_copy(kTe.t[0:64, :, 0:128], kTe.t[0:64, :, TP:TP + 128]), reads=[kTe], writes=[kTe])
              S.op("dve", lambda: V.tensor_copy(kTo.t[64:128, :, 0:128], kTo.t[64:128, :, TP:TP + 128]), reads=[kTo], writes=[kTo])
              S.op("dve", lambda: V.tensor_copy(vaug.t[:, 0, :, :], vaug.t[:, TPS, :, :]), reads=[vaug], writes=[vaug])
          for hf, nm in enumerate(["woA", "woB"]):
              wb, wvw = wget(nm)
              for dcl in range(4):
                  dc = hf * 4 + dcl
                  for (t0, n) in groups:
                      bk = A()
                      for kc in range(8):
                          S.op("pe", lambda: PE.matmul(bk.t[:, 0:n], wvw[:, kc, dcl * 128:(dcl + 1) * 128], oT.t[:, kc, t0:t0 + n], start=(kc == 0), stop=(kc == 7)), reads=[wb, oT], writes=[bk], inc=(kc == 7))
                      S.op("dve", lambda: V.tensor_tensor(hT[dc].t[:, t0:t0 + n], bk.t[:, 0:n], hT[dc].t[:, t0:t0 + n], ALU.add), reads=[bk, hT[dc]], writes=[hT[dc]])
              wdone()
          chk("wo")
          S.inherit(ffn_bufs, cur_view); cur_view = ffn_bufs
          ple_load(0, st, ntiles)
          ffn(0, groups, mid=lambda: ple_prep(0, st, ntiles))
          chk("ffn0")
          ple(0, st, groups, ntiles, is_last)
          chk("ple0")

          S.inherit(gm_bufs, cur_view); cur_view = gm_bufs
          S.phase = "Gnorm"
          rmsnorm(h_main, groups, 8, nTc, n_main, rstd, r_main)
          S.phase = "Gv"
          S.op("dve", lambda: V.memset(ssq.t[:], 0.0), writes=[ssq])
          for j in range(6):
              wb, wvw = wget(f"wvv{j}")
              for t in range(ntiles):
                  bk = A()
                  for kc in range(8):
                      S.op("pe", lambda: PE.matmul(bk.t[:, 0:512], nT.t[:, kc, t * 128:(t + 1) * 128], wvw[:, kc, :], start=(kc == 0), stop=(kc == 7)), reads=[wb, nTc[kc]], writes=[bk], inc=(kc == 7))
                  S.op("act", lambda: ACT.activation(gv.t[:, t, j * 512:(j + 1) * 512], bk.t[:, 0:512], AF.Gelu_apprx_tanh), reads=[bk], writes=[gvt[t]])
                  sq = nxt("sq", sqr)
                  S.op("dve", lambda: V.scalar_tensor_tensor(sq.t[:, 0:512], gv.t[:, t, j * 512:(j + 1) * 512], 1.0, gv.t[:, t, j * 512:(j + 1) * 512], ALU.mult, ALU.mult, accum_out=ssq.t[:, t * 6 + j:t * 6 + j + 1]), reads=[gvt[t]], writes=[sq, ssq])
              wdone()
          for t in range(ntiles):
              S.op("dve", lambda: V.reduce_sum(rsv.t[:, t:t + 1], ssq.t[:, t * 6:(t + 1) * 6], axis=AX.X), reads=[ssq], writes=[rsv])
          S.op("act", lambda: ACT.activation(rsv.t[:, 0:ntiles], rsv.t[:, 0:ntiles], AF.Ln, bias=epsb.t[:], scale=1.0 / 3072), reads=[rsv, epsb], writes=[rsv])
          S.op("act", lambda: ACT.activation(rsv.t[:, 0:ntiles], rsv.t[:, 0:ntiles], AF.Exp, scale=-0.5), reads=[rsv], writes=[rsv])
          for t in range(ntiles):
              if t == TPS:
                  for q in range(3):
                      stg = nxt("st", stage)
                      S.op("dve", lambda: V.scalar_tensor_tensor(stg.t[:, :], gv.t[:, t, q * 1024:(q + 1) * 1024], rsv.t[:, t:t + 1], vgb.t[:, q * 1024:(q + 1) * 1024], ALU.mult, ALU.mult), reads=[gvt[t], rsv, vgb], writes=[stg])
                      S.dma("sp", gvs[:, q * 1024:(q + 1) * 1024], stg.t[:, :], src=stg)
                      out_bufs.append(stg)
              S.op("dve", lambda: V.scalar_tensor_tensor(gv.t[:, t, :], gv.t[:, t, :], rsv.t[:, t:t + 1], vgb.t[:, :], ALU.mult, ALU.mult), reads=[gvt[t], rsv, vgb], writes=[gvt[t]])
          S.phase = "Gu"
          sp_pend = []

          def spatial(g):
              sp_pend.extend((g, t) for t in range(ntiles))

          def sp_flush(k):
              for _ in range(min(k, len(sp_pend))):
                  spatial1(*sp_pend.pop(0))

          def spatial1(g, t):
              if True:
                  wset = 1 if t == TPS else 0
                  bk = nxt("B", poolB)
                  for j in range(3):
                      cc = g * 3 + j
                      S.op("pe", lambda: PE.matmul(bk.t[:, j * 128:(j + 1) * 128], gv.t[:, t, cc * 128:(cc + 1) * 128], WtT.t[:, wset, g, :], start=True, stop=True), reads=[gvt[t], WtT], writes=[bk], inc=(j == 2))
                  sg = nxt("sq", sqr)
                  if wset == 0:
                      S.op("dve", lambda: V.tensor_tensor(sg.t[:, 0:384].rearrange("p (j i) -> p j i", j=3), bk.t[:, 0:384].rearrange("p (j i) -> p j i", j=3), bbc.t[:, g * 128:(g + 1) * 128].unsqueeze(1).to_broadcast([128, 3, 128]), ALU.add), reads=[bk, bbc], writes=[sg])
                  else:
                      S.op("dve", lambda: V.tensor_tensor(sg.t[:, 0:384].rearrange("p (j r i) -> p j r i", j=3, r=4), bk.t[:, 0:384].rearrange("p (j r i) -> p j r i", j=3, r=4), bbc.t[:, g * 128:g * 128 + 32].unsqueeze(1).unsqueeze(1).to_broadcast([128, 3, 4, 32]), ALU.add), reads=[bk, bbc], writes=[sg])
                  S.op("dve", lambda: V.tensor_tensor(uT.t[:, g * 3:(g + 1) * 3, t * 128:(t + 1) * 128], uT.t[:, g * 3:(g + 1) * 3, t * 128:(t + 1) * 128], sg.t[:, 0:384].rearrange("p (j i) -> p j i", j=3), ALU.mult), reads=[ug[g], sg], writes=[ug[g]])
          gdone = 0
          for j in range(6):
              wb, wvw = wget(f"wu{j}")
              for jj in range(4):
                  uc = j * 4 + jj
                  for (t0, n) in groups:
                      bk = A()
                      for kc in range(8):
                          S.op("pe", lambda: PE.matmul(bk.t[:, 0:n], wvw[:, kc, jj * 128:(jj + 1) * 128], nT.t[:, kc, t0:t0 + n], start=(kc == 0), stop=(kc == 7)), reads=[wb, nTc[kc]], writes=[bk], inc=(kc == 7))
                      S.op("act", lambda: ACT.activation(uT.t[:, uc, t0:t0 + n], bk.t[:, 0:n], AF.Gelu_apprx_tanh), reads=[bk], writes=[ug[uc // 3]])
                  while gdone < 8 and 3 * gdone + 2 < uc - 8:
                      spatial(gdone)
                      gdone += 1
                  sp_flush(2)
              wdone()
          while gdone < 8:
              spatial(gdone)
              gdone += 1
          sp_flush(len(sp_pend))
          S.phase = "Gout"
          acc_phase(lambda half, rb: f"wout{half}{rb}", 3, lambda f: ug[f // 3], lambda f, t0, n: uT.t[:, f, t0:t0 + n], groups)
          chk("gmlp")
          S.inherit(ffn_bufs, cur_view); cur_view = ffn_bufs
          ple_load(1, st, ntiles)
          ffn(1, groups, mid=lambda: ple_prep(1, st, ntiles))
          chk("ffn1")
          ple(1, st, groups, ntiles, is_last)
          chk("ple1")
          S.phase = "out"
          for t_ in range(XIN):
              x_issue(st + 1, t_)
          for t in range(ntiles):
              stg = nxt("st", stage)
              for half in range(2):
                  bk = A()
                  for j in range(4):
                      c = half * 4 + j
                      S.op("pe", lambda: PE.transpose(bk.t[:, j * 128:(j + 1) * 128], hT[c].t[:, t * 128:(t + 1) * 128], identf.t[:]), reads=[hT[c], identf], writes=[bk], inc=(j == 3))
                  if half == 0:
                      S.op("act", lambda: ACT.copy(stg.t[:, 0:512], bk.t[:, 0:512]), reads=[bk], writes=[stg])
                  else:
                      S.op("dve", lambda: V.tensor_copy(stg.t[:, 512:1024], bk.t[:, 0:512]), reads=[bk], writes=[stg])
              if t < TPS:
                  S.dma("sp", ym[st * TP + t * 128: st * TP + (t + 1) * 128, :], stg.t[:, :], src=stg)
              else:
                  S.dma("sp", ys[:, :], stg.t[:, :], src=stg)
              out_bufs.append(stg)

    except _Stop:
        pass
    if stop is None:
        assert wstate["used"] == len(wspecs), (wstate, len(wspecs))
    S.finish("sp", out_bufs)
    stats = (S.n_inst, S.n_wait, len(S.sem))
    nc._pe_phase = S.pe_phase
    nc._pe_waits = S.pe_waits
    S.close()
    es.close()
    return nc, stats


def _const_inputs(is_second_half):
    kk = np.arange(128)[:, None]
    qq = np.arange(128)[None, :]
    own = np.where((kk // 64) <= (qq // 64), -np.abs(qq - kk), NEG).astype(np.float32)
    prev = np.where((qq < 64) | (kk >= 64), -(qq + 128 - kk), NEG).astype(np.float32)
    prev0 = prev if is_second_half else np.full((128, 128), NEG, np.float32)
    s_own = np.where((kk // 32) == (qq // 32), -np.abs(qq % 32 - kk % 32), NEG).astype(np.float32)
    sc = [np.where((qq // 32) == b, -((qq % 32) + 128 - kk), NEG).astype(np.float32) for b in range(4)]
    cbias = np.stack([prev, own, prev0, own, s_own] + sc).astype(np.float32)
    tril = (np.arange(128)[None, :] <= np.arange(128)[:, None]).astype(np.float32)
    p = np.arange(128)[:, None]
    c = np.arange(8)[None, :]
    h = 2 * c + p // 64
    qscale = (1.0 / (8.0 * np.array(SLOPES, np.float64)[h])).astype(np.float32)
    return cbias, tril, qscale


_PROG = {}


def make_in_maps(inp, n_cores, NST, TPS, seq):
    f = lambda a: np.ascontiguousarray(np.asarray(a, dtype=np.float32))
    half = seq // 2
    NTP = NST * TPS * 128
    assert NTP == half
    xp = f(inp["x_prompt"]); xs_ = f(inp["x_sample"]); pp = f(inp["p_prompt"]); ps_ = f(inp["p_sample"])
    ckk = f(inp["cache_k"]); cvv = f(inp["cache_v"])
    gvec = np.concatenate([f(inp["g_mix"]).reshape(16, 128), f(inp["g_ffn"]).reshape(16, 128), f(inp["g_ple"]).reshape(16, 128)], 0)
    shared = dict(
        gvec=np.ascontiguousarray(gvec), qg=f(inp["attn_q_norm"])[0], kg=f(inp["attn_k_norm"])[0],
        sinks=f(inp["attn_sinks"])[0], w_qkv=f(inp["attn_w_qkv"])[0], w_o=f(inp["attn_w_o"])[0],
        w_uv=f(inp["gmlp_w_uv"])[0], vg=f(inp["gmlp_v_norm"])[0], w_s=f(inp["gmlp_w_s"])[0], b_s=f(inp["gmlp_b_s"])[0],
        w_out=f(inp["gmlp_w_out"])[0], w1=f(inp["ffn_w1"]), w2=f(inp["ffn_w2"]), wp=f(inp["ple_w_proj"]), wg=f(inp["ple_w_gate"]),
    )
    maps = []
    for core in range(n_cores):
        b, hf = core // 2, core % 2
        cb, tril, qscale = _const_inputs(hf == 1)
        m = dict(shared)
        m["xm"] = np.ascontiguousarray(xp[b, hf * half:(hf + 1) * half])
        m["xh"] = np.ascontiguousarray(xp[b, half - 128:half]) if hf == 1 else np.zeros((128, 1024), np.float32)
        m["xs"] = np.ascontiguousarray(xs_[4 * core:4 * core + 4].reshape(128, 1024))
        m["pm"] = np.ascontiguousarray(pp[:, b, hf * half:(hf + 1) * half])
        m["psm"] = np.ascontiguousarray(ps_[:, 4 * core:4 * core + 4].reshape(2, 128, 256))
        m["ck"] = np.ascontiguousarray(ckk[0, 4 * core:4 * core + 4].reshape(4, 128, 256))
        m["cv"] = np.ascontiguousarray(cvv[0, 4 * core:4 * core + 4].reshape(4, 128, 256))
        m["cbias"] = cb; m["tril"] = tril; m["qscale"] = qscale
        maps.append(m)
    return maps


def assemble(results, n_cores, seq):
    nb = n_cores // 2
    half = seq // 2
    y_prompt = np.zeros((nb, seq, 1024), np.float32)
    y_sample = np.zeros((4 * n_cores, 32, 1024), np.float32)
    nkp = np.zeros((1, nb, 128, 4, 64), np.float32); nvp = np.zeros_like(nkp)
    nks = np.zeros((1, 4 * n_cores, 128, 4, 64), np.float32); nvs = np.zeros_like(nks)
    gvs = np.zeros((1, 4 * n_cores, 32, 3072), np.float32)
    for core in range(n_cores):
        r = results[core]
        b, hf = core // 2, core % 2
        y_prompt[b, hf * half:(hf + 1) * half] = r["ym"]
        y_sample[4 * core:4 * core + 4] = r["ys"].reshape(4, 32, 1024)
        if hf == 1:
            nkp[0, b] = r["nkp"].reshape(128, 4, 64)
            nvp[0, b] = r["nvp"].reshape(128, 4, 64)
        nks[0, 4 * core:4 * core + 4] = r["nks"].reshape(4, 128, 4, 64)
        nvs[0, 4 * core:4 * core + 4] = r["nvs"].reshape(4, 128, 4, 64)
        gvs[0, 4 * core:4 * core + 4] = r["gvs"].reshape(4, 32, 3072)
    return (y_prompt, y_sample, nkp, nvp, nks, nvs, gvs)


def kernel(**inputs):
    n_cores = 8
    NST, TPS, seq = 4, 4, 4096
    key = (NST, TPS)
    if key not in _PROG:
        _PROG[key] = build_program(NST, TPS)[0]
    nc = _PROG[key]
    in_maps = make_in_maps(inputs, n_cores, NST, TPS, seq)
    res = run_bass_kernel_spmd(nc, in_maps, core_ids=list(range(n_cores)))
    return assemble(res.results, n_cores, seq)
```

```python
from contextlib import ExitStack
import numpy as np
import concourse.bass as bass
import concourse.mybir as mybir
from concourse.bass_utils import run_bass_kernel_spmd

F32 = mybir.dt.float32
BF16 = mybir.dt.bfloat16
AF = mybir.ActivationFunctionType
ALU = mybir.AluOpType
AX = mybir.AxisListType

EPS = 1e-6
N_HEADS = 16
SLOPES = [2.0 ** (-8.0 * (h + 1) / N_HEADS) for h in range(N_HEADS)]
NEG = -32768.0


class Buf:
    __slots__ = ("name", "t", "lw", "rd", "dsem", "psum")

    def __init__(self, name, t=None, psum=False):
        self.name = name
        self.t = t
        self.psum = psum
        self.lw = []
        self.rd = []
        self.dsem = None


class Sched:
    def __init__(self, nc):
        self.nc = nc
        self.engs = {"pe": nc.tensor, "act": nc.scalar, "dve": nc.vector, "pool": nc.gpsimd, "sp": nc.sync}
        self.sem, self.cnt, self.unit = {}, {}, {}
        self.known = {e: {} for e in self.engs}
        self.clock = {}
        self.stack = []
        for e in self.engs:
            self._mksrc(e, 1)
        self.n_wait = 0
        self.n_inst = 0
        self.phase = "setup"
        self.pe_phase = []
        self.pe_pending = []
        self.pe_waits = []

    def _mksrc(self, name, unit):
        cm = self.nc.semaphore("s_" + name)
        self.sem[name] = cm.__enter__()
        self.stack.append(cm)
        self.cnt[name] = 0
        self.unit[name] = unit

    def close(self):
        for cm in reversed(self.stack):
            cm.__exit__(None, None, None)

    def _need(self, eng, deps):
        kn = self.known[eng]
        best = {}
        for (s, i) in deps:
            if kn.get(s, 0) >= i:
                continue
            if best.get(s, 0) < i:
                best[s] = i
        for s, i in sorted(best.items(), key=lambda x: -x[1]):
            if kn.get(s, 0) >= i:
                continue
            self.engs[eng].wait_ge(self.sem[s], i * self.unit[s])
            self.n_wait += 1
            if eng == "pe":
                self.pe_pending.append(s)
            kn[s] = i
            ck = self.clock.get((s, i))
            if ck:
                for s2, i2 in ck.items():
                    if kn.get(s2, 0) < i2:
                        kn[s2] = i2

    def _deps(self, eng, reads, writes):
        deps = []
        for b in reads:
            deps.extend(b.lw)
            if b.psum:
                deps.extend(d for d in b.rd if d[0] != eng)
        for b in writes:
            deps.extend(b.lw)
            deps.extend(b.rd)
        if eng == "pe":
            deps = [d for d in deps if d[0] != "pe"]
        return deps

    def op(self, eng, fn, reads=(), writes=(), inc=True):
        self._need(eng, self._deps(eng, reads, writes))
        ins = fn()
        self.n_inst += 1
        if eng == "pe":
            self.pe_phase.append(self.phase)
            self.pe_waits.append(tuple(self.pe_pending))
            self.pe_pending = []
        idx = self.cnt[eng] + 1
        if inc:
            ins.then_inc(self.sem[eng], 1)
            self.cnt[eng] = idx
            self.clock[(eng, idx)] = dict(self.known[eng])
        tag = (eng, idx)
        for b in reads:
            b.rd.append(tag)
        for b in writes:
            b.lw = [tag]
            b.rd = []
        return ins

    def dma(self, q, out_ap, in_ap, dst=None, src=None, more=False):
        owner = dst if dst is not None else src
        reads = [src] if src is not None else []
        writes = [dst] if dst is not None else []
        if not more:
            self._need(q, self._deps(q, reads, writes))
        if owner.dsem is None:
            owner.dsem = "d_" + owner.name
            self._mksrc(owner.dsem, 16)
        s = owner.dsem
        ins = self.engs[q].dma_start(out=out_ap, in_=in_ap)
        ins.then_inc(self.sem[s], 16)
        self.n_inst += 1
        idx = self.cnt[s] + 1
        self.cnt[s] = idx
        self.clock[(s, idx)] = dict(self.known[q])
        tag = (s, idx)
        for b in reads:
            b.rd.append(tag)
        for b in writes:
            b.lw = [tag]
            if not more:
                b.rd = []
        return ins

    def inherit(self, new_bufs, old_bufs):
        acc = []
        for b in old_bufs:
            acc.extend(b.lw)
            acc.extend(b.rd)
        for nb in new_bufs:
            nb.rd = list(nb.rd) + acc

    def finish(self, eng, bufs):
        deps = []
        for b in bufs:
            deps.extend(b.lw)
            deps.extend(b.rd)
        self._need(eng, deps)


class _Stop(Exception):
    pass


XIN = 2


def build_program(NST, TPS, stop=None, skip=()):
    nc = bass.Bass("TRN2", target_bir_lowering=False)
    NTP = NST * TPS * 128
    TP = TPS * 128
    TMAX = TP + 128
    KSL = TMAX + 128

    def din(name, shape):
        return nc.dram_tensor(name, list(shape), F32, kind="ExternalInput").ap()

    def dout(name, shape):
        return nc.dram_tensor(name, list(shape), F32, kind="ExternalOutput").ap()

    xm = din("xm", [NTP, 1024]); xh = din("xh", [128, 1024]); xs = din("xs", [128, 1024])
    pm = din("pm", [2, NTP, 256]); psm = din("psm", [2, 128, 256])
    ck = din("ck", [4, 128, 256]); cv = din("cv", [4, 128, 256])
    gvec = din("gvec", [48, 128]); qg = din("qg", [64]); kg = din("kg", [64])
    qscale = din("qscale", [128, 8]); sinks = din("sinks", [16])
    w_qkv = din("w_qkv", [1024, 1536]); w_o = din("w_o", [1024, 1024])
    w_uv = din("w_uv", [1024, 6144]); vg = din("vg", [3072])
    w_s = din("w_s", [8, 128, 128]); b_s = din("b_s", [8, 128]); w_out = din("w_out", [3072, 1024])
    w1 = din("w1", [2, 1024, 4096]); w2 = din("w2", [2, 4096, 1024])
    wp = din("wp", [2, 256, 1024]); wg = din("wg", [2, 1024, 1024])
    tril = din("tril", [128, 128]); cbias = din("cbias", [9, 128, 128])

    ym = dout("ym", [NTP, 1024]); ys = dout("ys", [128, 1024])
    nkp = dout("nkp", [128, 256]); nvp = dout("nvp", [128, 256])
    nks = dout("nks", [4, 128, 256]); nvs = dout("nvs", [4, 128, 256])
    gvs = dout("gvs", [128, 3072])

    S = Sched(nc)
    es = ExitStack()

    def sb(name, shape, dt):
        return Buf(name, es.enter_context(nc.sbuf_tensor(name, list(shape), dt)))

    hT = [sb(f"hT{c}", [128, TMAX], F32) for c in range(8)]
    nT = sb("nT", [128, 8, TMAX], BF16)
    nTc = [Buf(f"nTc{c}", nT.t) for c in range(8)]
    sqr = [sb(f"sq{i}", [128, 512], BF16) for i in range(3)]
    rstd = sb("rstd", [128, TMAX], F32)
    rq = [sb(f"rq{i}", [128, 512], F32) for i in range(2)]
    stage = [sb(f"stage{i}", [128, 1024], F32) for i in range(3)]
    xin = [sb(f"xin{i}", [128, 1024], F32) for i in range(XIN)]
    pstage = [sb(f"pstage{i}", [128, 256], F32) for i in range(TPS + 1)]
    kTe = sb("kTe", [128, 4, KSL], BF16)
    kTo = sb("kTo", [128, 4, KSL], BF16)
    vaug = sb("vaug", [128, TPS + 2, 4, 65], BF16)
    rcs = [sb(f"rcs{i}", [128, 4], F32) for i in range(3)]
    pT = sb("pT", [128, 2, TMAX], BF16)
    identf = sb("identf", [128, 128], F32)
    identb = sb("identb", [128, 128], BF16)
    ones_d = sb("ones_d", [128, 128], BF16)
    ones_b = sb("ones_b", [128, 128], BF16)
    blk1 = sb("blk1", [128, 128], BF16)
    epsb = sb("epsb", [128, 1], F32)
    gT = sb("gT", [128, 48], F32)
    gq = sb("gq", [128, 8], F32)
    qgk = sb("qgk", [128, 2], F32)
    qsc = sb("qsc", [128, 8], F32)
    esink = sb("esink", [128, 16], F32)
    vgb = sb("vgb", [128, 3072], F32)
    bias4 = sb("bias4", [128, 9 * 128], BF16)
    WtT = sb("WtT", [128, 2, 8, 128], BF16)
    bbc = sb("bbc", [128, 1024], F32)
    ssq = sb("ssq", [128, (TPS + 1) * 6], F32)
    rsv = sb("rsv", [128, TPS + 1], F32)
    NWB = 5
    wring = [sb(f"wr{i}", [128, 4096], BF16) for i in range(NWB)]
    arena = es.enter_context(nc.sbuf_tensor("arena", [128, 30 * 1024], BF16))
    AW = 30 * 1024

    class Carve:
        def __init__(self):
            self.off = 0

        def take(self, name, shape, dt):
            n = int(np.prod(shape[1:]))
            if dt == F32:
                self.off = (self.off + 1) // 2 * 2
                ap = arena[:, self.off:self.off + 2 * n].bitcast(F32)
                self.off += 2 * n
            else:
                ap = arena[:, self.off:self.off + n]
                self.off += n
            assert self.off <= AW, (name, self.off)
            if len(shape) == 3:
                ap = ap.rearrange("p (a b) -> p a b", a=shape[1])
            elif len(shape) == 4:
                ap = ap.rearrange("p (a b c) -> p a b c", a=shape[1], b=shape[2])
            return Buf(name, ap)

    cA = Carve()
    qT = cA.take("qT", [128, 8, TMAX], BF16)
    oT = cA.take("oT", [128, 8, TMAX], BF16)
    Pb = [cA.take(f"P{i}", [128, 512], BF16) for i in range(5)]
    Pu = [cA.take(f"Pu{i}", [128, 2, 512], BF16) for i in range(4)]
    otok = [cA.take(f"otok{i}", [128, 1024], BF16) for i in range(2)]
    kf32 = cA.take("kf32", [128, 2, 4, 128], F32)
    hTh = cA.take("hTh", [128, 8, 128], F32)
    nTh = cA.take("nTh", [128, 8, 128], BF16)
    kcTe = cA.take("kcTe", [128, 4, 4, 128], BF16)
    kcTo = cA.take("kcTo", [128, 4, 4, 128], BF16)
    vcaug = cA.take("vcaug", [128, 4, 4, 65], BF16)
    attn_bufs = [qT, oT] + Pb + Pu + otok + [kf32, hTh, nTh, kcTe, kcTo, vcaug]
    cF = Carve()
    hidT = cF.take("hidT", [128, 32, TMAX], BF16)
    hq = [Buf(f"hq{j}", hidT.t) for j in range(8)]
    gsb = [cF.take(f"gsb{i}", [128, 512], F32) for i in range(2)]
    ffn_bufs = hq + gsb
    cG = Carve()
    uT = cG.take("uT", [128, 24, TMAX], BF16)
    gv = cG.take("gv", [128, TPS + 1, 3072], BF16)
    ug = [Buf(f"ug{g}", uT.t) for g in range(8)]
    gvt = [Buf(f"gvt{t}", gv.t) for t in range(TPS + 1)]
    gm_bufs = ug + gvt
    cS = Carve()
    wsf = cS.take("wsf", [128, 2, 8, 128], F32)
    cbf = cS.take("cbf", [128, 9, 128], F32)
    gvs_ = cS.take("gvs_", [128, 128], F32)
    trl = cS.take("trl", [128, 128], F32)
    sinkb = cS.take("sinkb", [128, 16], F32)
    setup_bufs = [wsf, cbf, gvs_, trl, sinkb]

    pbank = [Buf(f"pb{i}", es.enter_context(nc.psum_tensor(f"pb{i}", [128, 512], F32)), psum=True) for i in range(8)]
    poolA = pbank[0:3]
    poolB = pbank[3:7]
    bankC = pbank[7]
    rr = {"A": 0, "B": 0, "sq": 0, "rq": 0, "st": 0, "pst": 0, "P": 0, "rcs": 0, "otok": 0, "gsb": 0, "QK": 0, "QS": 0, "xin": 0, "sqk": 0}

    def nxt(key, lst):
        i = rr[key]
        rr[key] = i + 1
        return lst[i % len(lst)]

    def A():
        return nxt("A", poolA)

    V = nc.vector
    ACT = nc.scalar
    PE = nc.tensor
    POOL = nc.gpsimd

    wspecs = []

    def wv(ap2d, c0, c1, r0=None, r1=None):
        v = ap2d.rearrange("(c p) f -> p c f", p=128)
        if r0 is not None:
            v = v[:, r0:r1, :]
        return v[:, :, c0:c1]

    def plan_weights():
        for st in range(NST):
            wspecs.append(("wqA", [(None, wv(w_qkv, 0, 512))], 8, 512))
            wspecs.append(("wqB", [(None, wv(w_qkv, 512, 1024))], 8, 512))
            wspecs.append(("wkD", "dupk", 8, 512))
            wspecs.append(("wvD", [(None, wv(w_qkv, 1280, 1536))], 8, 256))
            wspecs.append(("woA", [(None, wv(w_o, 0, 512))], 8, 512))
            wspecs.append(("woB", [(None, wv(w_o, 512, 1024))], 8, 512))
            for l in range(2):
                if l == 1:
                    for j in range(6):
                        wspecs.append((f"wvv{j}", [(None, wv(w_uv, 3072 + j * 512, 3072 + (j + 1) * 512))], 8, 512))
                    for j in range(6):
                        wspecs.append((f"wu{j}", [(None, wv(w_uv, j * 512, (j + 1) * 512))], 8, 512))
                    for half in range(2):
                        for rb in range(3):
                            wspecs.append((f"wout{half}{rb}", [(None, wv(w_out, half * 512, (half + 1) * 512, rb * 8, rb * 8 + 8))], 8, 512))
                for j in range(8):
                    wspecs.append((f"w1_{l}_{j}", [(None, wv(w1[l], j * 512, (j + 1) * 512))], 8, 512))
                for half in range(2):
                    for rb in range(4):
                        wspecs.append((f"w2_{l}_{half}{rb}", [(None, wv(w2[l], half * 512, (half + 1) * 512, rb * 8, rb * 8 + 8))], 8, 512))
                wspecs.append((f"wp_{l}", [(None, wv(wp[l], 0, 1024))], 2, 1024))
                wspecs.append((f"wgA_{l}", [(None, wv(wg[l], 0, 512))], 8, 512))
                wspecs.append((f"wgB_{l}", [(None, wv(wg[l], 512, 1024))], 8, 512))

    plan_weights()
    wstate = {"issued": 0, "used": 0, "rel": 0}
    NBLK = len(wspecs) // NST
    wscr = nc.dram_tensor("wscr", [NBLK, 128, 4096], BF16).ap()
    scr = [Buf(f"scr{b}") for b in range(NBLK)]
    USE_SCR = NST > 1

    def w_issue(i):
        tag, srcs, kc, cols = wspecs[i]
        buf = wring[i % NWB]
        view = buf.t[:, 0:kc * cols].rearrange("p (c f) -> p c f", c=kc)
        if USE_SCR and i >= NBLK:
            b = i % NBLK
            S.dma("pool", buf.t[:, 0:kc * cols], wscr[b, :, 0:kc * cols], dst=buf, src=scr[b])
        elif srcs == "dupk" or srcs == "dupv":
            base = 1024 if srcs == "dupk" else 1280
            src = w_qkv.rearrange("(c p) f -> p c f", p=128)[:, :, base:base + 256].rearrange("p c (g d) -> p c g d", g=4)
            v4 = view.rearrange("p c (g two d) -> p c g two d", g=4, two=2)
            for kc_ in range(8):
                for two in range(2):
                    S.dma("pool", v4[:, kc_, :, two, :], src[:, kc_, :, :], dst=buf, more=not (kc_ == 0 and two == 0))
        else:
            first = True
            for _, sap in srcs:
                S.dma("pool", view, sap, dst=buf, more=not first)
                first = False

    def w_pump():
        while wstate["issued"] < min(len(wspecs), wstate["rel"] + NWB):
            w_issue(wstate["issued"])
            wstate["issued"] += 1

    def wdone(n=1):
        for r in range(wstate["rel"], wstate["rel"] + n):
            if USE_SCR and r < NBLK:
                _, _, kc, cols = wspecs[r]
                buf = wring[r % NWB]
                S.dma("sp", wscr[r, :, 0:kc * cols], buf.t[:, 0:kc * cols], dst=scr[r], src=buf)
        wstate["rel"] += n
        assert wstate["rel"] <= wstate["used"]
        w_pump()

    def wget(tag):
        i = wstate["used"]
        assert wspecs[i][0] == tag, (wspecs[i][0], tag)
        wstate["used"] = i + 1
        w_pump()
        assert wstate["issued"] > i, (tag, wstate)
        _, _, kc, cols = wspecs[i]
        buf = wring[i % NWB]
        return buf, buf.t[:, 0:kc * cols].rearrange("p (c f) -> p c f", c=kc)

    xpre = {}

    def x_issue(st_, t):
        if (st_, t) in xpre or st_ >= NST or t >= TPS:
            return
        stg = nxt("xin", xin)
        S.dma("sp", stg.t[:], xm[st_ * TP + t * 128: st_ * TP + (t + 1) * 128, :], dst=stg)
        xpre[(st_, t)] = stg

    for t_ in range(XIN):
        x_issue(0, t_)
    S.op("pool", lambda: POOL.memset(identf.t[:], 0.0), writes=[identf])
    S.op("pool", lambda: POOL.affine_select(out=identf.t[:], in_=identf.t[:], pattern=[[-1, 128]], compare_op=ALU.not_equal, fill=1.0, base=0, channel_multiplier=1), reads=[identf], writes=[identf])
    S.op("dve", lambda: V.tensor_copy(identb.t[:], identf.t[:]), reads=[identf], writes=[identb])
    S.op("dve", lambda: V.memset(ones_d.t[:], 1.0 / 1024), writes=[ones_d])
    S.op("dve", lambda: V.memset(ones_b.t[:], 1.0), writes=[ones_b])
    S.op("dve", lambda: V.memset(blk1.t[:], 0.0), writes=[blk1])
    S.op("dve", lambda: V.memset(blk1.t[0:64, 0:64], 1.0 / 64), writes=[blk1])
    S.op("dve", lambda: V.memset(blk1.t[64:128, 64:128], 1.0 / 64), writes=[blk1])
    S.op("dve", lambda: V.memset(epsb.t[:], EPS), writes=[epsb])
    S.op("dve", lambda: V.memset(kTe.t[:], 0.0), writes=[kTe])
    S.op("dve", lambda: V.memset(kTo.t[:], 0.0), writes=[kTo])
    S.op("dve", lambda: V.memset(vaug.t[:], 1.0), writes=[vaug])
    S.dma("sp", gvs_.t[0:48, :], gvec, dst=gvs_)
    S.dma("sp", trl.t[:], tril, dst=trl)
    S.dma("sp", cbf.t[:], cbias.rearrange("k p q -> p k q"), dst=cbf)
    S.dma("sp", qsc.t[:], qscale, dst=qsc)
    S.dma("sp", qgk.t[0:64, 0:1], qg.rearrange("(p o) -> p o", o=1), dst=qgk)
    S.dma("sp", qgk.t[64:128, 0:1], qg.rearrange("(p o) -> p o", o=1), dst=qgk, more=True)
    S.dma("sp", qgk.t[0:64, 1:2], kg.rearrange("(p o) -> p o", o=1), dst=qgk, more=True)
    S.dma("sp", qgk.t[64:128, 1:2], kg.rearrange("(p o) -> p o", o=1), dst=qgk, more=True)
    S.dma("sp", sinkb.t[:], sinks.partition_broadcast(128), dst=sinkb)
    bk = A()
    S.op("pe", lambda: PE.transpose(bk.t[:, 0:48], gvs_.t[0:48, :], identf.t[0:48, 0:48]), reads=[gvs_, identf], writes=[bk])
    S.op("dve", lambda: V.tensor_copy(gT.t[:], bk.t[:, 0:48]), reads=[bk], writes=[gT])
    S.op("dve", lambda: V.tensor_scalar(gq.t[:], qsc.t[:], qgk.t[:, 0:1], None, ALU.mult), reads=[qsc, qgk], writes=[gq])
    S.op("act", lambda: ACT.activation(esink.t[:], sinkb.t[:], AF.Exp), reads=[sinkb], writes=[esink])
    S.op("dve", lambda: V.tensor_copy(bias4.t[:].rearrange("p (k q) -> p k q", k=9), cbf.t[:]), reads=[cbf], writes=[bias4])

    def setup_spatial_dma():
        S.dma("sp", wsf.t[:, 0, :, :], w_s.rearrange("g i j -> i g j"), dst=wsf)
        S.op("dve", lambda: V.memset(wsf.t[:, 1, :, :], 0.0), writes=[wsf])
        for b in range(4):
            S.dma("sp", wsf.t[32 * b:32 * b + 32, 1, :, 32 * b:32 * b + 32], w_s[:, 0:32, 0:32].rearrange("g i j -> i g j"), dst=wsf, more=(b > 0))
        S.dma("sp", bbc.t[:, :], b_s.rearrange("g i -> (g i)").partition_broadcast(128), dst=bbc)
        S.dma("sp", vgb.t[:], vg.partition_broadcast(128), dst=vgb)

    def setup_spatial():
        for st_ in range(2):
            for g in range(8):
                S.op("dve", lambda: V.tensor_tensor(wsf.t[:, st_, g, :], wsf.t[:, st_, g, :], trl.t[:], ALU.mult), reads=[wsf, trl], writes=[wsf])
            for q in range(2):
                bk = A()
                for j in range(4):
                    g = q * 4 + j
                    S.op("pe", lambda: PE.transpose(bk.t[:, j * 128:(j + 1) * 128], wsf.t[:, st_, g, :], identf.t[:]), reads=[wsf, identf], writes=[bk], inc=(j == 3))
                S.op("dve", lambda: V.tensor_copy(WtT.t[:, st_, q * 4:(q + 1) * 4, :], bk.t[:].rearrange("p (g i) -> p g i", g=4)), reads=[bk], writes=[WtT])

    setup_spatial_dma()
    dk = Buf("dk"); dv = Buf("dv")
    S.dma("sp", nks[:, 0:96, :], ck[:, 32:128, :], dst=dk)
    S.dma("sp", nvs[:, 0:96, :], cv[:, 32:128, :], dst=dv)
    out_bufs = [dk, dv]

    def load_tiles_T(src_rows_fn, ntiles, dst_fn, pre=None):
        for t in range(ntiles):
            if pre is not None and t < TPS:
                x_issue(pre, t)
                stg = xpre.pop((pre, t))
            else:
                stg = nxt("xin", xin)
                S.dma("sp", stg.t[:], src_rows_fn(t), dst=stg)
            for half in range(2):
                bk = A()
                for j in range(4):
                    c = half * 4 + j
                    S.op("pe", lambda: PE.transpose(bk.t[:, j * 128:(j + 1) * 128], stg.t[:, c * 128:(c + 1) * 128], identf.t[:]), reads=[stg, identf], writes=[bk], inc=(j == 3))
                for j in range(4):
                    c = half * 4 + j
                    db, dap = dst_fn(c, t)
                    eng = "act" if half == 0 else "dve"
                    if eng == "act":
                        S.op("act", lambda: ACT.copy(dap, bk.t[:, j * 128:(j + 1) * 128]), reads=[bk], writes=[db])
                    else:
                        S.op("dve", lambda: V.tensor_copy(dap, bk.t[:, j * 128:(j + 1) * 128]), reads=[bk], writes=[db])

    def rmsnorm(h_fn, groups, gcol, dbuf, d_fn, rbuf, r_fn):
        for (t0, n) in groups:
            bk = bankC
            for c in range(8):
                hb, hap = h_fn(c, t0, n)
                sq = nxt("sq", sqr)
                S.op("act", lambda: ACT.activation(sq.t[:, 0:n], hap, AF.Square), reads=[hb], writes=[sq])
                S.op("pe", lambda: PE.matmul(bk.t[:, 0:n], ones_d.t[:], sq.t[:, 0:n], start=(c == 0), stop=(c == 7)), reads=[ones_d, sq], writes=[bk], inc=True)
            S.op("act", lambda: ACT.activation(r_fn(t0, n), bk.t[:, 0:n], AF.Ln, bias=epsb.t[:], scale=1.0), reads=[bk, epsb], writes=[rbuf])
            S.op("act", lambda: ACT.activation(r_fn(t0, n), r_fn(t0, n), AF.Exp, scale=-0.5), reads=[rbuf], writes=[rbuf])
            for c in range(8):
                hb, hap = h_fn(c, t0, n)
                S.op("dve", lambda: V.scalar_tensor_tensor(d_fn(c, t0, n), hap, gT.t[:, gcol + c:gcol + c + 1], r_fn(t0, n), ALU.mult, ALU.mult), reads=[hb, gT, rbuf], writes=[dbuf[c]])

    def rmsnorm_deferred(h_fn, groups, gcol, dbuf, d_fn, rbuf, r_fn):
        for (t0, n) in groups:
            bk = bankC
            for c in range(8):
                hb, hap = h_fn(c, t0, n)
                S.op("act", lambda: ACT.activation(d_fn(c, t0, n), hap, AF.Copy, scale=gT.t[:, gcol + c:gcol + c + 1]), reads=[hb, gT], writes=[dbuf[c]])
                sq = nxt("sq", sqr)
                S.op("act", lambda: ACT.activation(sq.t[:, 0:n], hap, AF.Square), reads=[hb], writes=[sq])
                S.op("pe", lambda: PE.matmul(bk.t[:, 0:n], ones_d.t[:], sq.t[:, 0:n], start=(c == 0), stop=(c == 7)), reads=[ones_d, sq], writes=[bk], inc=True)
            S.op("act", lambda: ACT.activation(r_fn(t0, n), bk.t[:, 0:n], AF.Ln, bias=epsb.t[:], scale=1.0), reads=[bk, epsb], writes=[rbuf])
            S.op("act", lambda: ACT.activation(r_fn(t0, n), r_fn(t0, n), AF.Exp, scale=-0.5), reads=[rbuf], writes=[rbuf])

    def h_main(c, t0, n):
        return hT[c], hT[c].t[:, t0:t0 + n]

    def n_main(c, t0, n):
        return nT.t[:, c, t0:t0 + n]

    def r_main(t0, n):
        return rstd.t[:, t0:t0 + n]

    def acc_phase(tag_fn, nrb, rhs_buf, rhs_fn, groups):
        for half in range(2):
            first_c = True
            for rb in range(nrb):
                wb, wvw = wget(tag_fn(half, rb))
                for dcl in range(4):
                    for fc in range(8):
                        for gi, (t0, n) in enumerate(groups):
                            if t0 != TP:
                                bk = poolB[dcl]
                                oap = bk.t[:, 0:n]
                                st = (rb == 0 and fc == 0)
                                sk = False
                            else:
                                bk = bankC
                                oap = bk.t[:, dcl * 128:dcl * 128 + n]
                                st = first_c
                                first_c = False
                                sk = True
                            last = (rb == nrb - 1 and fc == 7)
                            S.op("pe", lambda: PE.matmul(oap, wvw[:, fc, dcl * 128:(dcl + 1) * 128], rhs_fn(rb * 8 + fc, t0, n), start=st, stop=last, skip_group_check=sk),
                                 reads=[wb, rhs_buf(rb * 8 + fc)], writes=[bk], inc=last)
                wdone()
            for dcl in range(4):
                dc = half * 4 + dcl
                for gi, (t0, n) in enumerate(groups):
                    if t0 != TP:
                        bk = poolB[dcl]; iap = bk.t[:, 0:n]
                    else:
                        bk = bankC; iap = bk.t[:, dcl * 128:dcl * 128 + n]
                    S.op("dve", lambda: V.tensor_tensor(hT[dc].t[:, t0:t0 + n], iap, hT[dc].t[:, t0:t0 + n], ALU.add), reads=[bk, hT[dc]], writes=[hT[dc]])

    def ffn(l, groups, mid=None):
        S.phase = f"F{l}norm"
        rmsnorm_deferred(h_main, groups, 16 + l * 8, nTc, n_main, rstd, r_main)
        S.phase = f"F{l}p1"
        for j in range(8):
            wb, wvw = wget(f"w1_{l}_{j}")
            for jj in range(4):
                f = j * 4 + jj
                for (t0, n) in groups:
                    bk = A()
                    for kc in range(8):
                        S.op("pe", lambda: PE.matmul(bk.t[:, 0:n], wvw[:, kc, jj * 128:(jj + 1) * 128], nT.t[:, kc, t0:t0 + n], start=(kc == 0), stop=(kc == 7)), reads=[wb, nTc[kc]], writes=[bk], inc=(kc == 7))
                    rl = nxt("gsb", gsb)
                    S.op("act", lambda: ACT.activation(rl.t[:, 0:n], bk.t[:, 0:n], AF.Relu), reads=[bk], writes=[rl])
                    S.op("dve", lambda: V.tensor_tensor(rl.t[:, 0:n], rl.t[:, 0:n], rstd.t[:, t0:t0 + n], ALU.mult), reads=[rl, rstd], writes=[rl])
                    S.op("dve", lambda: V.tensor_tensor(hidT.t[:, f, t0:t0 + n], rl.t[:, 0:n], rl.t[:, 0:n], ALU.mult), reads=[rl], writes=[hq[j]])
            wdone()
        if mid is not None:
            mid()
        S.phase = f"F{l}p2"
        acc_phase(lambda half, rb: f"w2_{l}_{half}{rb}", 4, lambda f: hq[f // 4], lambda f, t0, n: hidT.t[:, f, t0:t0 + n], groups)

    def ple_load(l, st, ntiles):
        for t in range(ntiles):
            if t < TPS:
                src = pm[l, st * TP + t * 128: st * TP + (t + 1) * 128, :]
            else:
                src = psm[l, :, :]
            S.dma("sp", pstage[t].t[:], src, dst=pstage[t])

    def ple_prep(l, st, ntiles):
        for t in range(ntiles):
            pst = pstage[t]
            bk = A()
            for j in range(2):
                S.op("pe", lambda: PE.transpose(bk.t[:, j * 128:(j + 1) * 128], pst.t[:, j * 128:(j + 1) * 128], identf.t[:]), reads=[pst, identf], writes=[bk], inc=(j == 1))
            S.op("act", lambda: ACT.copy(pT.t[:, :, t * 128:(t + 1) * 128], bk.t[:, 0:256].rearrange("p (c t) -> p c t", c=2)), reads=[bk], writes=[pT])

    def ple(l, st, groups, ntiles, is_last):
        S.phase = f"P{l}"
        rmsnorm(h_main, groups, 32 + l * 8, nTc, n_main, rstd, r_main)
        pbanks = pbank[0:7]
        pr = [0]

        def PB():
            b = pbanks[pr[0] % 7]
            pr[0] += 1
            return b
        wpb, wpv = wget(f"wp_{l}")
        for hf, nm in enumerate(["wgA", "wgB"]):
            wb, wvw = wget(f"{nm}_{l}")
            for dcl in range(4):
                dc = hf * 4 + dcl
                for (t0, n) in groups:
                    bp = PB()
                    for kc in range(2):
                        S.op("pe", lambda: PE.matmul(bp.t[:, 0:n], wpv[:, kc, dc * 128:(dc + 1) * 128], pT.t[:, kc, t0:t0 + n], start=(kc == 0), stop=(kc == 1)), reads=[wpb, pT], writes=[bp], inc=(kc == 1))
                    bg = PB()
                    for kc in range(8):
                        S.op("pe", lambda: PE.matmul(bg.t[:, 0:n], wvw[:, kc, dcl * 128:(dcl + 1) * 128], nT.t[:, kc, t0:t0 + n], start=(kc == 0), stop=(kc == 7)), reads=[wb, nTc[kc]], writes=[bg], inc=(kc == 7))
                    gs = nxt("gsb", gsb)
                    S.op("act", lambda: ACT.activation(gs.t[:, 0:n], bg.t[:, 0:n], AF.Sigmoid), reads=[bg], writes=[gs])
                    S.op("dve", lambda: V.tensor_tensor(gs.t[:, 0:n], bp.t[:, 0:n], gs.t[:, 0:n], ALU.mult), reads=[bp, gs], writes=[gs])
                    S.op("dve", lambda: V.tensor_tensor(hT[dc].t[:, t0:t0 + n], hT[dc].t[:, t0:t0 + n], gs.t[:, 0:n], ALU.add), reads=[hT[dc], gs], writes=[hT[dc]])
        wdone(3)

    def qk_chunk_proj(wb, lhs_fn, rhs_buf, rhs_fn, n):
        bk = nxt("QK", pbank[0:5])
        for kc in range(8):
            S.op("pe", lambda: PE.matmul(bk.t[:, 0:n], lhs_fn(kc), rhs_fn(kc), start=(kc == 0), stop=(kc == 7)), reads=[wb, rhs_buf(kc)], writes=[bk], inc=(kc == 7))
        sq = nxt("sq", sqr)
        S.op("act", lambda: ACT.activation(sq.t[:, 0:n], bk.t[:, 0:n], AF.Square), reads=[bk], writes=[sq])
        return (bk, sq)

    def qk_chunk_norm(bksq, n, writes_fn):
        bk, sq = bksq
        b2 = nxt("QS", pbank[5:7])
        S.op("pe", lambda: PE.matmul(b2.t[:, 0:n], blk1.t[:], sq.t[:, 0:n], start=True, stop=True), reads=[blk1, sq], writes=[b2])
        r = nxt("rq", rq)
        S.op("act", lambda: ACT.activation(r.t[:, 0:n], b2.t[:, 0:n], AF.Ln, bias=epsb.t[:], scale=1.0), reads=[b2, epsb], writes=[r])
        S.op("act", lambda: ACT.activation(r.t[:, 0:n], r.t[:, 0:n], AF.Exp, scale=-0.5), reads=[r], writes=[r])
        writes_fn(bk, r)

    def pipeline(items, proj_fn, norm_fn, depth=2):
        pend = []
        for it in items:
            pend.append((it, proj_fn(it)))
            if len(pend) > depth:
                norm_fn(*pend.pop(0))
        while pend:
            norm_fn(*pend.pop(0))

    def finish_tile(ot, tq, bank):
        bv = bank.t[:].bitcast(BF16)
        for c in range(8):
            S.op("pe", lambda: PE.transpose(bv[:, c * 128:(c + 1) * 128], ot.t[:, c * 128:(c + 1) * 128], identb.t[:]), reads=[ot, identb], writes=[bank], inc=(c == 7))
        S.op("dve", lambda: V.tensor_copy(oT.t[:, :, tq:tq + 128], bv.rearrange("p (c q) -> p c q", c=8)), reads=[bank], writes=[oT])

    def pv_norm(ob, g, ot, mm_list):
        for hh in range(4):
            n_ = len(mm_list[hh])
            for j, (lap, rap, rb) in enumerate(mm_list[hh]):
                last = (hh == 3 and j == n_ - 1)
                S.op("pe", lambda: PE.matmul(ob.t[:, hh * 65:(hh + 1) * 65], lap, rap, start=(j == 0), stop=(j == n_ - 1)), reads=rb, writes=[ob], inc=last)
        rc = nxt("rcs", rcs)
        o3 = ob.t[:, 0:260].rearrange("p (h e) -> p h e", e=65)
        S.op("dve", lambda: V.tensor_tensor(rc.t[:, 0:4], o3[:, :, 64], esink.t[:, 4 * g:4 * g + 4], ALU.add), reads=[ob, esink], writes=[rc])
        S.op("dve", lambda: V.reciprocal(rc.t[:, 0:4], rc.t[:, 0:4]), reads=[rc], writes=[rc])
        S.op("dve", lambda: V.tensor_tensor(ot.t[:, g * 256:(g + 1) * 256].rearrange("p (h d) -> p h d", h=4), o3[:, :, 0:64], rc.t[:, 0:4].unsqueeze(2).to_broadcast([128, 4, 64]), ALU.mult), reads=[ob, rc], writes=[ot])

    def attention_prompt(st, tiles):
        Sx = [poolA[0], poolA[1], poolA[2], bankC]
        units = [(t, g) for t in tiles for g in range(4)]

        def front(i):
            t, g = units[i]
            tq = t * 128
            pbi = 2 if (st == 0 and t == 0) else 0
            pu = Pu[i % 4]
            for half in range(2):
                sbk = Sx[2 * (i % 2) + half]
                S.op("pe", lambda: PE.matmul(sbk.t[:, 0:512], identb.t[:], bias4.t[:, pbi * 128:(pbi + 2) * 128].unsqueeze(1).to_broadcast([128, 2, 256]), start=True, stop=False), reads=[identb, bias4], writes=[sbk], inc=False)
                for hl in range(2):
                    h = 4 * g + 2 * half + hl
                    c = h // 2
                    kb = kTe if h % 2 == 0 else kTo
                    for kt in range(2):
                        last = (hl == 1 and kt == 1)
                        S.op("pe", lambda: PE.matmul(sbk.t[:, hl * 256 + kt * 128:hl * 256 + (kt + 1) * 128], kb.t[:, g, (t + kt) * 128:(t + kt + 1) * 128], qT.t[:, c, tq:tq + 128], start=False, stop=last), reads=[kTe, kTo, qT], writes=[sbk], inc=last)
                for hl in range(2):
                    h = 4 * g + 2 * half + hl
                    S.op("act", lambda: ACT.activation(pu.t[:, half, hl * 256:(hl + 1) * 256], sbk.t[:, hl * 256:(hl + 1) * 256], AF.Exp, scale=float(SLOPES[h])), reads=[sbk], writes=[pu])

        def back(i):
            t, g = units[i]
            tq = t * 128
            pu = Pu[i % 4]
            ob = poolB[i % 2]
            ot = otok[t % 2]
            mm = []
            for hh in range(4):
                half, hl = hh // 2, hh % 2
                mm.append([(pu.t[:, half, hl * 256 + kt * 128:hl * 256 + (kt + 1) * 128], vaug.t[:, t + kt, g, :], [pu, vaug]) for kt in range(2)])
            pv_norm(ob, g, ot, mm)
            if g == 3:
                finish_tile(ot, tq, poolB[2 + (t % 2)])

        LAG = 2
        for i in range(len(units) + LAG):
            if i < len(units):
                front(i)
            if i >= LAG:
                back(i - LAG)

    def attention_tile(tq, keytiles):
        nkt = len(keytiles)
        for g in range(4):
            Ps = []
            for (kbufs, ke_fn, ko_fn, vbuf, v_fn, bidx) in keytiles:
                sbk = A()
                S.op("pe", lambda: PE.matmul(sbk.t[:, 0:512], identb.t[:], bias4.t[:, bidx * 128:(bidx + 1) * 128].unsqueeze(1).to_broadcast([128, 4, 128]), start=True, stop=False), reads=[identb, bias4], writes=[sbk], inc=False)
                for hh in range(4):
                    h = 4 * g + hh
                    c = h // 2
                    kap = ke_fn(g) if h % 2 == 0 else ko_fn(g)
                    S.op("pe", lambda: PE.matmul(sbk.t[:, hh * 128:(hh + 1) * 128], kap, qT.t[:, c, tq:tq + 128], start=False, stop=(hh == 3)), reads=list(kbufs) + [qT], writes=[sbk], inc=(hh == 3))
                pb_ = nxt("P", Pb)
                for hh in range(4):
                    h = 4 * g + hh
                    S.op("act", lambda: ACT.activation(pb_.t[:, hh * 128:(hh + 1) * 128], sbk.t[:, hh * 128:(hh + 1) * 128], AF.Exp, scale=float(SLOPES[h])), reads=[sbk], writes=[pb_])
                Ps.append(pb_)
            ob = poolB[g % 2]
            mm = []
            for hh in range(4):
                mm.append([(Ps[kt].t[:, hh * 128:(hh + 1) * 128], keytiles[kt][4](g), [Ps[kt], keytiles[kt][3]]) for kt in range(nkt)])
            pv_norm(ob, g, otok[0], mm)
        finish_tile(otok[0], tq, poolB[2])

    def chk(name):
        S.phase = name
        if stop == name:
            raise _Stop()

    cur_view = attn_bufs
    try:
      chk("setup")
      for st in range(NST):
          is_last = (st == NST - 1)
          ntiles = TPS + (1 if is_last else 0)
          T = ntiles * 128
          groups = []
          t0 = 0
          while t0 < TP:
              n = min(512, TP - t0)
              groups.append((t0, n))
              t0 += n
          if is_last:
              groups.append((TP, 128))

          if st > 0:
              S.inherit(attn_bufs, cur_view)
              cur_view = attn_bufs

          def xrows(t):
              if t < TPS:
                  return xm[st * TP + t * 128: st * TP + (t + 1) * 128, :]
              return xs[:, :]
          load_tiles_T(xrows, ntiles, lambda c, t: (hT[c], hT[c].t[:, t * 128:(t + 1) * 128]), pre=st)
          if st == 0:
              setup_spatial()
              S.inherit(attn_bufs, setup_bufs)
          chk("load")
          if st == 0:
              load_tiles_T(lambda t: xh[:, :], 1, lambda c, t: (hTh, hTh.t[:, c, :]))
              rmsnorm(lambda c, t0, n: (hTh, hTh.t[:, c, :]), [(0, 128)], 0, [nTh] * 8, lambda c, t0, n: nTh.t[:, c, :], rq[0], lambda t0, n: rq[0].t[:, 0:128])
          if is_last:
              S.op("dve", lambda: V.memset(kcTe.t[:], 0.0), writes=[kcTe])
              S.op("dve", lambda: V.memset(kcTo.t[:], 0.0), writes=[kcTo])
              S.op("dve", lambda: V.memset(vcaug.t[:], 1.0), writes=[vcaug])
              for b in range(4):
                  S.dma("pool", vcaug.t[:, b, :, 0:64], cv[b].rearrange("r (g d) -> r g d", g=4), dst=vcaug, more=(b > 0))
              for b in range(4):
                  stg = nxt("st", stage)
                  for two in range(2):
                      S.dma("sp", stg.t[:, 0:512].rearrange("p (g two d) -> p g two d", g=4, two=2)[:, :, two, :], ck[b].rearrange("r (g d) -> r g d", g=4), dst=stg, more=(two > 0))
                  bk = A()
                  for g in range(4):
                      S.op("pe", lambda: PE.transpose(bk.t[:, g * 128:(g + 1) * 128], stg.t[:, g * 128:(g + 1) * 128], identf.t[:]), reads=[stg, identf], writes=[bk], inc=(g == 3))
                  S.op("dve", lambda: V.tensor_copy(kcTe.t[0:64, b, :, :], bk.t[0:64, :].rearrange("p (g k) -> p g k", g=4)), reads=[bk], writes=[kcTe])
                  S.op("act", lambda: ACT.copy(kcTo.t[64:128, b, :, :], bk.t[64:128, :].rearrange("p (g k) -> p g k", g=4)), reads=[bk], writes=[kcTo])

          chk("prep")
          rmsnorm(h_main, groups, 0, nTc, n_main, rstd, r_main)
          chk("norm0")
          wq = {}
          items = [(c, t0, n) for c in range(8) for (t0, n) in groups]

          def qproj(it):
              c, t0, n = it
              if c == 0 and "a" not in wq:
                  wq["a"] = wget("wqA")
              if c == 4 and "b" not in wq:
                  wdone()
                  wq["b"] = wget("wqB")
              wb, wvw = wq["a"] if c < 4 else wq["b"]
              return qk_chunk_proj(wb, lambda kc: wvw[:, kc, (c % 4) * 128:(c % 4 + 1) * 128], (lambda kc: nTc[kc]), lambda kc: nT.t[:, kc, t0:t0 + n], n)

          def qnorm(it, bk):
              c, t0, n = it

              def wr(bk, r):
                  S.op("dve", lambda: V.scalar_tensor_tensor(qT.t[:, c, t0:t0 + n], bk.t[:, 0:n], gq.t[:, c:c + 1], r.t[:, 0:n], ALU.mult, ALU.mult), reads=[bk, gq, r], writes=[qT])
              qk_chunk_norm(bk, n, wr)
          pipeline(items, qproj, qnorm)
          wdone()
          chk("q")
          wkb, wkv = wget("wkD")
          kgroups = [("m", t0, n) for (t0, n) in groups] + ([("h", 0, 128)] if st == 0 else [])
          items = [(g, kind, t0, n) for g in range(4) for (kind, t0, n) in kgroups]
          need_kout = is_last

          def kproj(it):
              g, kind, t0, n = it
              if kind == "m":
                  return qk_chunk_proj(wkb, lambda kc: wkv[:, kc, g * 128:(g + 1) * 128], (lambda kc: nTc[kc]), lambda kc: nT.t[:, kc, t0:t0 + n], n)
              return qk_chunk_proj(wkb, lambda kc: wkv[:, kc, g * 128:(g + 1) * 128], (lambda kc: nTh), lambda kc: nTh.t[:, kc, :], n)

          def knorm(it, bk):
              g, kind, t0, n = it
              k0 = 128 + t0 if kind == "m" else 0

              def wr(bk, r):
                  S.op("dve", lambda: V.scalar_tensor_tensor(kTe.t[0:64, g, k0:k0 + n], bk.t[0:64, 0:n], qgk.t[0:64, 1:2], r.t[0:64, 0:n], ALU.mult, ALU.mult), reads=[bk, qgk, r], writes=[kTe])
                  S.op("dve", lambda: V.scalar_tensor_tensor(kTo.t[64:128, g, k0:k0 + n], bk.t[64:128, 0:n], qgk.t[64:128, 1:2], r.t[64:128, 0:n], ALU.mult, ALU.mult), reads=[bk, qgk, r], writes=[kTo])
                  if need_kout and kind == "m":
                      outs = []
                      if t0 <= (TPS - 1) * 128 < t0 + n:
                          outs.append(((TPS - 1) * 128 - t0, 0))
                      if t0 == TP:
                          outs.append((0, 1))
                      for (off, which) in outs:
                          S.op("dve", lambda: V.scalar_tensor_tensor(kf32.t[:, which, g, :], bk.t[:, off:off + 128], qgk.t[:, 1:2], r.t[:, off:off + 128], ALU.mult, ALU.mult), reads=[bk, qgk, r], writes=[kf32])
              qk_chunk_norm(bk, n, wr)

          pipeline(items, kproj, knorm)
          wdone()
          if need_kout:
              for which in range(2):
                  bk = A()
                  for g in range(4):
                      S.op("pe", lambda: PE.transpose(bk.t[:, g * 128:(g + 1) * 128], kf32.t[:, which, g, :], identf.t[:]), reads=[kf32, identf], writes=[bk], inc=(g == 3))
                  stg = nxt("st", stage)
                  S.op("act", lambda: ACT.copy(stg.t[:, 0:256].rearrange("p (g d) -> p g d", g=4), bk.t[:].rearrange("p (g x) -> p g x", g=4)[:, :, 0:64]), reads=[bk], writes=[stg])
                  if which == 0:
                      S.dma("sp", nkp[:, :], stg.t[:, 0:256], src=stg)
                  else:
                      for b in range(4):
                          S.dma("sp", nks[b, 96:128, :], stg.t[32 * b:32 * b + 32, 0:256], src=stg, more=(b > 0))
                  out_bufs.append(stg)
          chk("k")
          wvb, wvv_ = wget("wvD")
          vt = [("m", t) for t in range(ntiles)] + ([("h", 0)] if st == 0 else [])
          for (kind, t) in vt:
              bk = A()
              for kc in range(8):
                  lhs = nT.t[:, kc, t * 128:(t + 1) * 128] if kind == "m" else nTh.t[:, kc, :]
                  S.op("pe", lambda: PE.matmul(bk.t[:, 0:256], lhs, wvv_[:, kc, :], start=(kc == 0), stop=(kc == 7)), reads=[wvb, nTc[kc] if kind == "m" else nTh], writes=[bk], inc=(kc == 7))
              slot = t + 1 if kind == "m" else 0
              S.op("act", lambda: ACT.copy(vaug.t[:, slot, :, 0:64], bk.t[:, 0:256].rearrange("p (g d) -> p g d", g=4)), reads=[bk], writes=[vaug])
              if is_last and kind == "m" and t >= TPS - 1:
                  stg = nxt("st", stage)
                  S.op("act", lambda: ACT.copy(stg.t[:, 0:256], bk.t[:, 0:256]), reads=[bk], writes=[stg])
                  if t == TPS - 1:
                      S.dma("sp", nvp[:, :], stg.t[:, 0:256], src=stg)
                  else:
                      for b in range(4):
                          S.dma("sp", nvs[b, 96:128, :], stg.t[32 * b:32 * b + 32, 0:256], src=stg, more=(b > 0))
                  out_bufs.append(stg)
          wdone()
          chk("v")
          attention_prompt(st, list(range(TPS)))
          for t in range(TPS, ntiles):
              kts = []
              for b in range(4):
                  kts.append(([kcTe, kcTo], (lambda g, b=b: kcTe.t[:, b, g, :]), (lambda g, b=b: kcTo.t[:, b, g, :]), vcaug, (lambda g, b=b: vcaug.t[:, b, g, :]), 5 + b))
              kts.append(([kTe, kTo], (lambda g, t=t: kTe.t[:, g, (t + 1) * 128:(t + 2) * 128]), (lambda g, t=t: kTo.t[:, g, (t + 1) * 128:(t + 2) * 128]), vaug, (lambda g, t=t: vaug.t[:, t + 1, g, :]), 4))
              attention_tile(t * 128, kts)
          chk("attn")
          if not is_last:
              S.op("dve", lambda: V.tensor_copy(kTe.t[0:64, :, 0:128], kTe.t[0:64, :, TP:TP + 128]), reads=[kTe], writes=[kTe])
              S.op("dve", lambda: V.tensor_copy(kTo.t[64:128, :, 0:128], kTo.t[64:128, :, TP:TP + 128]), reads=[kTo], writes=[kTo])
              S.op("dve", lambda: V.tensor_copy(vaug.t[:, 0, :, :], vaug.t[:, TPS, :, :]), reads=[vaug], writes=[vaug])
          for hf, nm in enumerate(["woA", "woB"]):
              wb, wvw = wget(nm)
              for dcl in range(4):
                  dc = hf * 4 + dcl
                  for (t0, n) in groups:
                      bk = A()
                      for kc in range(8):
                          S.op("pe", lambda: PE.matmul(bk.t[:, 0:n], wvw[:, kc, dcl * 128:(dcl + 1) * 128], oT.t[:, kc, t0:t0 + n], start=(kc == 0), stop=(kc == 7)), reads=[wb, oT], writes=[bk], inc=(kc == 7))
                      S.op("dve", lambda: V.tensor_tensor(hT[dc].t[:, t0:t0 + n], bk.t[:, 0:n], hT[dc].t[:, t0:t0 + n], ALU.add), reads=[bk, hT[dc]], writes=[hT[dc]])
              wdone()
          chk("wo")
          S.inherit(ffn_bufs, cur_view); cur_view = ffn_bufs
          ple_load(0, st, ntiles)
          ffn(0, groups, mid=lambda: ple_prep(0, st, ntiles))
          chk("ffn0")
          ple(0, st, groups, ntiles, is_last)
          chk("ple0")

          S.inherit(gm_bufs, cur_view); cur_view = gm_bufs
          S.phase = "Gnorm"
          rmsnorm(h_main, groups, 8, nTc, n_main, rstd, r_main)
          S.phase = "Gv"
          S.op("dve", lambda: V.memset(ssq.t[:], 0.0), writes=[ssq])
          for j in range(6):
              wb, wvw = wget(f"wvv{j}")
              for t in range(ntiles):
                  bk = A()
                  for kc in range(8):
                      S.op("pe", lambda: PE.matmul(bk.t[:, 0:512], nT.t[:, kc, t * 128:(t + 1) * 128], wvw[:, kc, :], start=(kc == 0), stop=(kc == 7)), reads=[wb, nTc[kc]], writes=[bk], inc=(kc == 7))
                  S.op("act", lambda: ACT.activation(gv.t[:, t, j * 512:(j + 1) * 512], bk.t[:, 0:512], AF.Gelu_apprx_tanh), reads=[bk], writes=[gvt[t]])
                  sq = nxt("sq", sqr)
                  S.op("dve", lambda: V.scalar_tensor_tensor(sq.t[:, 0:512], gv.t[:, t, j * 512:(j + 1) * 512], 1.0, gv.t[:, t, j * 512:(j + 1) * 512], ALU.mult, ALU.mult, accum_out=ssq.t[:, t * 6 + j:t * 6 + j + 1]), reads=[gvt[t]], writes=[sq, ssq])
              wdone()
          for t in range(ntiles):
              S.op("dve", lambda: V.reduce_sum(rsv.t[:, t:t + 1], ssq.t[:, t * 6:(t + 1) * 6], axis=AX.X), reads=[ssq], writes=[rsv])
          S.op("act", lambda: ACT.activation(rsv.t[:, 0:ntiles], rsv.t[:, 0:ntiles], AF.Ln, bias=epsb.t[:], scale=1.0 / 3072), reads=[rsv, epsb], writes=[rsv])
          S.op("act", lambda: ACT.activation(rsv.t[:, 0:ntiles], rsv.t[:, 0:ntiles], AF.Exp, scale=-0.5), reads=[rsv], writes=[rsv])
          for t in range(ntiles):
              if t == TPS:
                  for q in range(3):
                      stg = nxt("st", stage)
                      S.op("dve", lambda: V.scalar_tensor_tensor(stg.t[:, :], gv.t[:, t, q * 1024:(q + 1) * 1024], rsv.t[:, t:t + 1], vgb.t[:, q * 1024:(q + 1) * 1024], ALU.mult, ALU.mult), reads=[gvt[t], rsv, vgb], writes=[stg])
                      S.dma("sp", gvs[:, q * 1024:(q + 1) * 1024], stg.t[:, :], src=stg)
                      out_bufs.append(stg)
              S.op("dve", lambda: V.scalar_tensor_tensor(gv.t[:, t, :], gv.t[:, t, :], rsv.t[:, t:t + 1], vgb.t[:, :], ALU.mult, ALU.mult), reads=[gvt[t], rsv, vgb], writes=[gvt[t]])
          S.phase = "Gu"
          sp_pend = []

          def spatial(g):
              sp_pend.extend((g, t) for t in range(ntiles))

          def sp_flush(k):
              for _ in range(min(k, len(sp_pend))):
                  spatial1(*sp_pend.pop(0))

          def spatial1(g, t):
              if True:
                  wset = 1 if t == TPS else 0
                  bk = nxt("B", poolB)
                  for j in range(3):
                      cc = g * 3 + j
                      S.op("pe", lambda: PE.matmul(bk.t[:, j * 128:(j + 1) * 128], gv.t[:, t, cc * 128:(cc + 1) * 128], WtT.t[:, wset, g, :], start=True, stop=True), reads=[gvt[t], WtT], writes=[bk], inc=(j == 2))
                  sg = nxt("sq", sqr)
                  if wset == 0:
                      S.op("dve", lambda: V.tensor_tensor(sg.t[:, 0:384].rearrange("p (j i) -> p j i", j=3), bk.t[:, 0:384].rearrange("p (j i) -> p j i", j=3), bbc.t[:, g * 128:(g + 1) * 128].unsqueeze(1).to_broadcast([128, 3, 128]), ALU.add), reads=[bk, bbc], writes=[sg])
                  else:
                      S.op("dve", lambda: V.tensor_tensor(sg.t[:, 0:384].rearrange("p (j r i) -> p j r i", j=3, r=4), bk.t[:, 0:384].rearrange("p (j r i) -> p j r i", j=3, r=4), bbc.t[:, g * 128:g * 128 + 32].unsqueeze(1).unsqueeze(1).to_broadcast([128, 3, 4, 32]), ALU.add), reads=[bk, bbc], writes=[sg])
                  S.op("dve", lambda: V.tensor_tensor(uT.t[:, g * 3:(g + 1) * 3, t * 128:(t + 1) * 128], uT.t[:, g * 3:(g + 1) * 3, t * 128:(t + 1) * 128], sg.t[:, 0:384].rearrange("p (j i) -> p j i", j=3), ALU.mult), reads=[ug[g], sg], writes=[ug[g]])
          gdone = 0
          for j in range(6):
              wb, wvw = wget(f"wu{j}")
              for jj in range(4):
                  uc = j * 4 + jj
                  for (t0, n) in groups:
                      bk = A()
                      for kc in range(8):
                          S.op("pe", lambda: PE.matmul(bk.t[:, 0:n], wvw[:, kc, jj * 128:(jj + 1) * 128], nT.t[:, kc, t0:t0 + n], start=(kc == 0), stop=(kc == 7)), reads=[wb, nTc[kc]], writes=[bk], inc=(kc == 7))
                      S.op("act", lambda: ACT.activation(uT.t[:, uc, t0:t0 + n], bk.t[:, 0:n], AF.Gelu_apprx_tanh), reads=[bk], writes=[ug[uc // 3]])
                  while gdone < 8 and 3 * gdone + 2 < uc - 8:
                      spatial(gdone)
                      gdone += 1
                  sp_flush(2)
              wdone()
          while gdone < 8:
              spatial(gdone)
              gdone += 1
          sp_flush(len(sp_pend))
          S.phase = "Gout"
          acc_phase(lambda half, rb: f"wout{half}{rb}", 3, lambda f: ug[f // 3], lambda f, t0, n: uT.t[:, f, t0:t0 + n], groups)
          chk("gmlp")
          S.inherit(ffn_bufs, cur_view); cur_view = ffn_bufs
          ple_load(1, st, ntiles)
          ffn(1, groups, mid=lambda: ple_prep(1, st, ntiles))
          chk("ffn1")
          ple(1, st, groups, ntiles, is_last)
          chk("ple1")
          S.phase = "out"
          for t_ in range(XIN):
              x_issue(st + 1, t_)
          for t in range(ntiles):
              stg = nxt("st", stage)
              for half in range(2):
                  bk = A()
                  for j in range(4):
                      c = half * 4 + j
                      S.op("pe", lambda: PE.transpose(bk.t[:, j * 128:(j + 1) * 128], hT[c].t[:, t * 128:(t + 1) * 128], identf.t[:]), reads=[hT[c], identf], writes=[bk], inc=(j == 3))
                  if half == 0:
                      S.op("act", lambda: ACT.copy(stg.t[:, 0:512], bk.t[:, 0:512]), reads=[bk], writes=[stg])
                  else:
                      S.op("dve", lambda: V.tensor_copy(stg.t[:, 512:1024], bk.t[:, 0:512]), reads=[bk], writes=[stg])
              if t < TPS:
                  S.dma("sp", ym[st * TP + t * 128: st * TP + (t + 1) * 128, :], stg.t[:, :], src=stg)
              else:
                  S.dma("sp", ys[:, :], stg.t[:, :], src=stg)
              out_bufs.append(stg)

    except _Stop:
        pass
    if stop is None:
        assert wstate["used"] == len(wspecs), (wstate, len(wspecs))
    S.finish("sp", out_bufs)
    stats = (S.n_inst, S.n_wait, len(S.sem))
    nc._pe_phase = S.pe_phase
    nc._pe_waits = S.pe_waits
    S.close()
    es.close()
    return nc, stats


def _const_inputs(is_second_half):
    kk = np.arange(128)[:, None]
    qq = np.arange(128)[None, :]
    own = np.where((kk // 64) <= (qq // 64), -np.abs(qq - kk), NEG).astype(np.float32)
    prev = np.where((qq < 64) | (kk >= 64), -(qq + 128 - kk), NEG).astype(np.float32)
    prev0 = prev if is_second_half else np.full((128, 128), NEG, np.float32)
    s_own = np.where((kk // 32) == (qq // 32), -np.abs(qq % 32 - kk % 32), NEG).astype(np.float32)
    sc = [np.where((qq // 32) == b, -((qq % 32) + 128 - kk), NEG).astype(np.float32) for b in range(4)]
    cbias = np.stack([prev, own, prev0, own, s_own] + sc).astype(np.float32)
    tril = (np.arange(128)[None, :] <= np.arange(128)[:, None]).astype(np.float32)
    p = np.arange(128)[:, None]
    c = np.arange(8)[None, :]
    h = 2 * c + p // 64
    qscale = (1.0 / (8.0 * np.array(SLOPES, np.float64)[h])).astype(np.float32)
    return cbias, tril, qscale


_PROG = {}


def make_in_maps(inp, n_cores, NST, TPS, seq):
    f = lambda a: np.ascontiguousarray(np.asarray(a, dtype=np.float32))
    half = seq // 2
    NTP = NST * TPS * 128
    assert NTP == half
    xp = f(inp["x_prompt"]); xs_ = f(inp["x_sample"]); pp = f(inp["p_prompt"]); ps_ = f(inp["p_sample"])
    ckk = f(inp["cache_k"]); cvv = f(inp["cache_v"])
    gvec = np.concatenate([f(inp["g_mix"]).reshape(16, 128), f(inp["g_ffn"]).reshape(16, 128), f(inp["g_ple"]).reshape(16, 128)], 0)
    shared = dict(
        gvec=np.ascontiguousarray(gvec), qg=f(inp["attn_q_norm"])[0], kg=f(inp["attn_k_norm"])[0],
        sinks=f(inp["attn_sinks"])[0], w_qkv=f(inp["attn_w_qkv"])[0], w_o=f(inp["attn_w_o"])[0],
        w_uv=f(inp["gmlp_w_uv"])[0], vg=f(inp["gmlp_v_norm"])[0], w_s=f(inp["gmlp_w_s"])[0], b_s=f(inp["gmlp_b_s"])[0],
        w_out=f(inp["gmlp_w_out"])[0], w1=f(inp["ffn_w1"]), w2=f(inp["ffn_w2"]), wp=f(inp["ple_w_proj"]), wg=f(inp["ple_w_gate"]),
    )
    maps = []
    for core in range(n_cores):
        b, hf = core // 2, core % 2
        cb, tril, qscale = _const_inputs(hf == 1)
        m = dict(shared)
        m["xm"] = np.ascontiguousarray(xp[b, hf * half:(hf + 1) * half])
        m["xh"] = np.ascontiguousarray(xp[b, half - 128:half]) if hf == 1 else np.zeros((128, 1024), np.float32)
        m["xs"] = np.ascontiguousarray(xs_[4 * core:4 * core + 4].reshape(128, 1024))
        m["pm"] = np.ascontiguousarray(pp[:, b, hf * half:(hf + 1) * half])
        m["psm"] = np.ascontiguousarray(ps_[:, 4 * core:4 * core + 4].reshape(2, 128, 256))
        m["ck"] = np.ascontiguousarray(ckk[0, 4 * core:4 * core + 4].reshape(4, 128, 256))
        m["cv"] = np.ascontiguousarray(cvv[0, 4 * core:4 * core + 4].reshape(4, 128, 256))
        m["cbias"] = cb; m["tril"] = tril; m["qscale"] = qscale
        maps.append(m)
    return maps


def assemble(results, n_cores, seq):
    nb = n_cores // 2
    half = seq // 2
    y_prompt = np.zeros((nb, seq, 1024), np.float32)
    y_sample = np.zeros((4 * n_cores, 32, 1024), np.float32)
    nkp = np.zeros((1, nb, 128, 4, 64), np.float32); nvp = np.zeros_like(nkp)
    nks = np.zeros((1, 4 * n_cores, 128, 4, 64), np.float32); nvs = np.zeros_like(nks)
    gvs = np.zeros((1, 4 * n_cores, 32, 3072), np.float32)
    for core in range(n_cores):
        r = results[core]
        b, hf = core // 2, core % 2
        y_prompt[b, hf * half:(hf + 1) * half] = r["ym"]
        y_sample[4 * core:4 * core + 4] = r["ys"].reshape(4, 32, 1024)
        if hf == 1:
            nkp[0, b] = r["nkp"].reshape(128, 4, 64)
            nvp[0, b] = r["nvp"].reshape(128, 4, 64)
        nks[0, 4 * core:4 * core + 4] = r["nks"].reshape(4, 128, 4, 64)
        nvs[0, 4 * core:4 * core + 4] = r["nvs"].reshape(4, 128, 4, 64)
        gvs[0, 4 * core:4 * core + 4] = r["gvs"].reshape(4, 32, 3072)
    return (y_prompt, y_sample, nkp, nvp, nks, nvs, gvs)


def kernel(**inputs):
    n_cores = 8
    NST, TPS, seq = 4, 4, 4096
    key = (NST, TPS)
    if key not in _PROG:
        _PROG[key] = build_program(NST, TPS)[0]
    nc = _PROG[key]
    in_maps = make_in_maps(inputs, n_cores, NST, TPS, seq)
    res = run_bass_kernel_spmd(nc, in_maps, core_ids=list(range(n_cores)))
    return assemble(res.results, n_cores, seq)
```

```python
from contextlib import ExitStack
import numpy as np
import concourse.bass as bass
import concourse.mybir as mybir
from concourse.bass_utils import run_bass_kernel_spmd

F32 = mybir.dt.float32
BF16 = mybir.dt.bfloat16
AF = mybir.ActivationFunctionType
ALU = mybir.AluOpType
AX = mybir.AxisListType

EPS = 1e-6
N_HEADS = 16
SLOPES = [2.0 ** (-8.0 * (h + 1) / N_HEADS) for h in range(N_HEADS)]
NEG = -32768.0


class Buf:
    __slots__ = ("name", "t", "lw", "rd", "dsem", "psum")

    def __init__(self, name, t=None, psum=False):
        self.name = name
        self.t = t
        self.psum = psum
        self.lw = []
        self.rd = []
        self.dsem = None


class Sched:
    def __init__(self, nc):
        self.nc = nc
        self.engs = {"pe": nc.tensor, "act": nc.scalar, "dve": nc.vector, "pool": nc.gpsimd, "sp": nc.sync}
        self.sem, self.cnt, self.unit = {}, {}, {}
        self.known = {e: {} for e in self.engs}
        self.clock = {}
        self.stack = []
        for e in self.engs:
            self._mksrc(e, 1)
        self.n_wait = 0
        self.n_inst = 0
        self.phase = "setup"
        self.pe_phase = []
        self.pe_pending = []
        self.pe_waits = []

    def _mksrc(self, name, unit):
        cm = self.nc.semaphore("s_" + name)
        self.sem[name] = cm.__enter__()
        self.stack.append(cm)
        self.cnt[name] = 0
        self.unit[name] = unit

    def close(self):
        for cm in reversed(self.stack):
            cm.__exit__(None, None, None)

    def _need(self, eng, deps):
        kn = self.known[eng]
        best = {}
        for (s, i) in deps:
            if kn.get(s, 0) >= i:
                continue
            if best.get(s, 0) < i:
                best[s] = i
        for s, i in sorted(best.items(), key=lambda x: -x[1]):
            if kn.get(s, 0) >= i:
                continue
            self.engs[eng].wait_ge(self.sem[s], i * self.unit[s])
            self.n_wait += 1
            if eng == "pe":
                self.pe_pending.append(s)
            kn[s] = i
            ck = self.clock.get((s, i))
            if ck:
                for s2, i2 in ck.items():
                    if kn.get(s2, 0) < i2:
                        kn[s2] = i2

    def _deps(self, eng, reads, writes):
        deps = []
        for b in reads:
            deps.extend(b.lw)
            if b.psum:
                deps.extend(d for d in b.rd if d[0] != eng)
        for b in writes:
            deps.extend(b.lw)
            deps.extend(b.rd)
        if eng == "pe":
            deps = [d for d in deps if d[0] != "pe"]
        return deps

    def op(self, eng, fn, reads=(), writes=(), inc=True):
        self._need(eng, self._deps(eng, reads, writes))
        ins = fn()
        self.n_inst += 1
        if eng == "pe":
            self.pe_phase.append(self.phase)
            self.pe_waits.append(tuple(self.pe_pending))
            self.pe_pending = []
        idx = self.cnt[eng] + 1
        if inc:
            ins.then_inc(self.sem[eng], 1)
            self.cnt[eng] = idx
            self.clock[(eng, idx)] = dict(self.known[eng])
        tag = (eng, idx)
        for b in reads:
            b.rd.append(tag)
        for b in writes:
            b.lw = [tag]
            b.rd = []
        return ins

    def dma(self, q, out_ap, in_ap, dst=None, src=None, more=False):
        owner = dst if dst is not None else src
        reads = [src] if src is not None else []
        writes = [dst] if dst is not None else []
        if not more:
            self._need(q, self._deps(q, reads, writes))
        if owner.dsem is None:
            owner.dsem = "d_" + owner.name
            self._mksrc(owner.dsem, 16)
        s = owner.dsem
        ins = self.engs[q].dma_start(out=out_ap, in_=in_ap)
        ins.then_inc(self.sem[s], 16)
        self.n_inst += 1
        idx = self.cnt[s] + 1
        self.cnt[s] = idx
        self.clock[(s, idx)] = dict(self.known[q])
        tag = (s, idx)
        for b in reads:
            b.rd.append(tag)
        for b in writes:
            b.lw = [tag]
            if not more:
                b.rd = []
        return ins

    def inherit(self, new_bufs, old_bufs):
        acc = []
        for b in old_bufs:
            acc.extend(b.lw)
            acc.extend(b.rd)
        for nb in new_bufs:
            nb.rd = list(nb.rd) + acc

    def finish(self, eng, bufs):
        deps = []
        for b in bufs:
            deps.extend(b.lw)
            deps.extend(b.rd)
        self._need(eng, deps)


class _Stop(Exception):
    pass


XIN = 2
SCR_MOD = 2


def build_program(NST, TPS, stop=None, skip=()):
    nc = bass.Bass("TRN2", target_bir_lowering=False)
    NTP = NST * TPS * 128
    TP = TPS * 128
    TMAX = TP + 128
    KSL = TMAX + 128

    def din(name, shape):
        return nc.dram_tensor(name, list(shape), F32, kind="ExternalInput").ap()

    def dout(name, shape):
        return nc.dram_tensor(name, list(shape), F32, kind="ExternalOutput").ap()

    xm = din("xm", [NTP, 1024]); xh = din("xh", [128, 1024]); xs = din("xs", [128, 1024])
    pm = din("pm", [2, NTP, 256]); psm = din("psm", [2, 128, 256])
    ck = din("ck", [4, 128, 256]); cv = din("cv", [4, 128, 256])
    gvec = din("gvec", [48, 128]); qg = din("qg", [64]); kg = din("kg", [64])
    qscale = din("qscale", [128, 8]); sinks = din("sinks", [16])
    w_qkv = din("w_qkv", [1024, 1536]); w_o = din("w_o", [1024, 1024])
    w_uv = din("w_uv", [1024, 6144]); vg = din("vg", [3072])
    w_s = din("w_s", [8, 128, 128]); b_s = din("b_s", [8, 128]); w_out = din("w_out", [3072, 1024])
    w1 = din("w1", [2, 1024, 4096]); w2 = din("w2", [2, 4096, 1024])
    wp = din("wp", [2, 256, 1024]); wg = din("wg", [2, 1024, 1024])
    tril = din("tril", [128, 128]); cbias = din("cbias", [9, 128, 128])

    ym = dout("ym", [NTP, 1024]); ys = dout("ys", [128, 1024])
    nkp = dout("nkp", [128, 256]); nvp = dout("nvp", [128, 256])
    nks = dout("nks", [4, 128, 256]); nvs = dout("nvs", [4, 128, 256])
    gvs = dout("gvs", [128, 3072])

    S = Sched(nc)
    es = ExitStack()

    def sb(name, shape, dt):
        return Buf(name, es.enter_context(nc.sbuf_tensor(name, list(shape), dt)))

    hT = [sb(f"hT{c}", [128, TMAX], F32) for c in range(8)]
    nT = sb("nT", [128, 8, TMAX], BF16)
    nTc = [Buf(f"nTc{c}", nT.t) for c in range(8)]
    sqr = [sb(f"sq{i}", [128, 512], BF16) for i in range(3)]
    rstd = sb("rstd", [128, TMAX], F32)
    rq = [sb(f"rq{i}", [128, 512], F32) for i in range(2)]
    stage = [sb(f"stage{i}", [128, 1024], F32) for i in range(3)]
    xin = [sb(f"xin{i}", [128, 1024], F32) for i in range(XIN)]
    pstage = [sb(f"pstage{i}", [128, 256], F32) for i in range(TPS + 1)]
    kTe = sb("kTe", [128, 4, KSL], BF16)
    kTo = sb("kTo", [128, 4, KSL], BF16)
    vaug = sb("vaug", [128, TPS + 2, 4, 65], BF16)
    rcs = [sb(f"rcs{i}", [128, 4], F32) for i in range(3)]
    pT = sb("pT", [128, 2, TMAX], BF16)
    identf = sb("identf", [128, 128], F32)
    identb = sb("identb", [128, 128], BF16)
    ones_d = sb("ones_d", [128, 128], BF16)
    ones_b = sb("ones_b", [128, 128], BF16)
    blk1 = sb("blk1", [128, 128], BF16)
    epsb = sb("epsb", [128, 1], F32)
    gT = sb("gT", [128, 48], F32)
    gq = sb("gq", [128, 8], F32)
    qgk = sb("qgk", [128, 2], F32)
    qsc = sb("qsc", [128, 8], F32)
    esink = sb("esink", [128, 16], F32)
    vgb = sb("vgb", [128, 3072], F32)
    bias4 = sb("bias4", [128, 9 * 128], BF16)
    WtT = sb("WtT", [128, 2, 8, 128], BF16)
    bbc = sb("bbc", [128, 1024], F32)
    ssq = sb("ssq", [128, (TPS + 1) * 6], F32)
    rsv = sb("rsv", [128, TPS + 1], F32)
    NWB = 5
    wring = [sb(f"wr{i}", [128, 4096], BF16) for i in range(NWB)]
    arena = es.enter_context(nc.sbuf_tensor("arena", [128, 30 * 1024], BF16))
    AW = 30 * 1024

    class Carve:
        def __init__(self):
            self.off = 0

        def take(self, name, shape, dt):
            n = int(np.prod(shape[1:]))
            if dt == F32:
                self.off = (self.off + 1) // 2 * 2
                ap = arena[:, self.off:self.off + 2 * n].bitcast(F32)
                self.off += 2 * n
            else:
                ap = arena[:, self.off:self.off + n]
                self.off += n
            assert self.off <= AW, (name, self.off)
            if len(shape) == 3:
                ap = ap.rearrange("p (a b) -> p a b", a=shape[1])
            elif len(shape) == 4:
                ap = ap.rearrange("p (a b c) -> p a b c", a=shape[1], b=shape[2])
            return Buf(name, ap)

    cA = Carve()
    qT = cA.take("qT", [128, 8, TMAX], BF16)
    oT = cA.take("oT", [128, 8, TMAX], BF16)
    Pb = [cA.take(f"P{i}", [128, 512], BF16) for i in range(5)]
    Pu = [cA.take(f"Pu{i}", [128, 2, 512], BF16) for i in range(4)]
    otok = [cA.take(f"otok{i}", [128, 1024], BF16) for i in range(2)]
    kf32 = cA.take("kf32", [128, 2, 4, 128], F32)
    hTh = cA.take("hTh", [128, 8, 128], F32)
    nTh = cA.take("nTh", [128, 8, 128], BF16)
    kcTe = cA.take("kcTe", [128, 4, 4, 128], BF16)
    kcTo = cA.take("kcTo", [128, 4, 4, 128], BF16)
    vcaug = cA.take("vcaug", [128, 4, 4, 65], BF16)
    attn_bufs = [qT, oT] + Pb + Pu + otok + [kf32, hTh, nTh, kcTe, kcTo, vcaug]
    cF = Carve()
    hidT = cF.take("hidT", [128, 32, TMAX], BF16)
    hq = [Buf(f"hq{j}", hidT.t) for j in range(8)]
    gsb = [cF.take(f"gsb{i}", [128, 512], F32) for i in range(2)]
    ffn_bufs = hq + gsb
    cG = Carve()
    uT = cG.take("uT", [128, 24, TMAX], BF16)
    gv = cG.take("gv", [128, TPS + 1, 3072], BF16)
    ug = [Buf(f"ug{g}", uT.t) for g in range(8)]
    gvt = [Buf(f"gvt{t}", gv.t) for t in range(TPS + 1)]
    gm_bufs = ug + gvt
    cS = Carve()
    wsf = cS.take("wsf", [128, 2, 8, 128], F32)
    cbf = cS.take("cbf", [128, 9, 128], F32)
    gvs_ = cS.take("gvs_", [128, 128], F32)
    trl = cS.take("trl", [128, 128], F32)
    sinkb = cS.take("sinkb", [128, 16], F32)
    setup_bufs = [wsf, cbf, gvs_, trl, sinkb]

    pbank = [Buf(f"pb{i}", es.enter_context(nc.psum_tensor(f"pb{i}", [128, 512], F32)), psum=True) for i in range(8)]
    poolA = pbank[0:3]
    poolB = pbank[3:7]
    bankC = pbank[7]
    rr = {"A": 0, "B": 0, "sq": 0, "rq": 0, "st": 0, "pst": 0, "P": 0, "rcs": 0, "otok": 0, "gsb": 0, "QK": 0, "QS": 0, "xin": 0, "sqk": 0}

    def nxt(key, lst):
        i = rr[key]
        rr[key] = i + 1
        return lst[i % len(lst)]

    def A():
        return nxt("A", poolA)

    V = nc.vector
    ACT = nc.scalar
    PE = nc.tensor
    POOL = nc.gpsimd

    wspecs = []

    def wv(ap2d, c0, c1, r0=None, r1=None):
        v = ap2d.rearrange("(c p) f -> p c f", p=128)
        if r0 is not None:
            v = v[:, r0:r1, :]
        return v[:, :, c0:c1]

    def plan_weights():
        for st in range(NST):
            wspecs.append(("wqA", [(None, wv(w_qkv, 0, 512))], 8, 512))
            wspecs.append(("wqB", [(None, wv(w_qkv, 512, 1024))], 8, 512))
            wspecs.append(("wkD", "dupk", 8, 512))
            wspecs.append(("wvD", [(None, wv(w_qkv, 1280, 1536))], 8, 256))
            wspecs.append(("woA", [(None, wv(w_o, 0, 512))], 8, 512))
            wspecs.append(("woB", [(None, wv(w_o, 512, 1024))], 8, 512))
            for l in range(2):
                if l == 1:
                    for j in range(6):
                        wspecs.append((f"wvv{j}", [(None, wv(w_uv, 3072 + j * 512, 3072 + (j + 1) * 512))], 8, 512))
                    for j in range(6):
                        wspecs.append((f"wu{j}", [(None, wv(w_uv, j * 512, (j + 1) * 512))], 8, 512))
                    for half in range(2):
                        for rb in range(3):
                            wspecs.append((f"wout{half}{rb}", [(None, wv(w_out, half * 512, (half + 1) * 512, rb * 8, rb * 8 + 8))], 8, 512))
                for j in range(8):
                    wspecs.append((f"w1_{l}_{j}", [(None, wv(w1[l], j * 512, (j + 1) * 512))], 8, 512))
                for half in range(2):
                    for rb in range(4):
                        wspecs.append((f"w2_{l}_{half}{rb}", [(None, wv(w2[l], half * 512, (half + 1) * 512, rb * 8, rb * 8 + 8))], 8, 512))
                wspecs.append((f"wp_{l}", [(None, wv(wp[l], 0, 1024))], 2, 1024))
                wspecs.append((f"wgA_{l}", [(None, wv(wg[l], 0, 512))], 8, 512))
                wspecs.append((f"wgB_{l}", [(None, wv(wg[l], 512, 1024))], 8, 512))

    plan_weights()
    wstate = {"issued": 0, "used": 0, "rel": 0}
    NBLK = len(wspecs) // NST
    wscr = nc.dram_tensor("wscr", [NBLK, 128, 4096], BF16).ap()
    scr = [Buf(f"scr{b}") for b in range(NBLK)]
    USE_SCR = NST > 1

    def scr_on(b):
        return USE_SCR and (b % SCR_MOD != 0)

    def w_issue(i):
        tag, srcs, kc, cols = wspecs[i]
        buf = wring[i % NWB]
        view = buf.t[:, 0:kc * cols].rearrange("p (c f) -> p c f", c=kc)
        if i >= NBLK and scr_on(i % NBLK):
            b = i % NBLK
            S.dma("pool", buf.t[:, 0:kc * cols], wscr[b, :, 0:kc * cols], dst=buf, src=scr[b])
        elif srcs == "dupk" or srcs == "dupv":
            base = 1024 if srcs == "dupk" else 1280
            src = w_qkv.rearrange("(c p) f -> p c f", p=128)[:, :, base:base + 256].rearrange("p c (g d) -> p c g d", g=4)
            v4 = view.rearrange("p c (g two d) -> p c g two d", g=4, two=2)
            for kc_ in range(8):
                for two in range(2):
                    S.dma("pool", v4[:, kc_, :, two, :], src[:, kc_, :, :], dst=buf, more=not (kc_ == 0 and two == 0))
        else:
            first = True
            for _, sap in srcs:
                S.dma("pool", view, sap, dst=buf, more=not first)
                first = False

    def w_pump():
        while wstate["issued"] < min(len(wspecs), wstate["rel"] + NWB):
            w_issue(wstate["issued"])
            wstate["issued"] += 1

    def wdone(n=1):
        for r in range(wstate["rel"], wstate["rel"] + n):
            if r < NBLK and scr_on(r):
                _, _, kc, cols = wspecs[r]
                buf = wring[r % NWB]
                S.dma("sp", wscr[r, :, 0:kc * cols], buf.t[:, 0:kc * cols], dst=scr[r], src=buf)
        wstate["rel"] += n
        assert wstate["rel"] <= wstate["used"]
        w_pump()

    def wget(tag):
        i = wstate["used"]
        assert wspecs[i][0] == tag, (wspecs[i][0], tag)
        wstate["used"] = i + 1
        w_pump()
        assert wstate["issued"] > i, (tag, wstate)
        _, _, kc, cols = wspecs[i]
        buf = wring[i % NWB]
        return buf, buf.t[:, 0:kc * cols].rearrange("p (c f) -> p c f", c=kc)

    xpre = {}

    def x_issue(st_, t):
        if (st_, t) in xpre or st_ >= NST or t >= TPS:
            return
        stg = nxt("xin", xin)
        S.dma("sp", stg.t[:], xm[st_ * TP + t * 128: st_ * TP + (t + 1) * 128, :], dst=stg)
        xpre[(st_, t)] = stg

    for t_ in range(XIN):
        x_issue(0, t_)
    S.op("pool", lambda: POOL.memset(identf.t[:], 0.0), writes=[identf])
    S.op("pool", lambda: POOL.affine_select(out=identf.t[:], in_=identf.t[:], pattern=[[-1, 128]], compare_op=ALU.not_equal, fill=1.0, base=0, channel_multiplier=1), reads=[identf], writes=[identf])
    S.op("dve", lambda: V.tensor_copy(identb.t[:], identf.t[:]), reads=[identf], writes=[identb])
    S.op("dve", lambda: V.memset(ones_d.t[:], 1.0 / 1024), writes=[ones_d])
    S.op("dve", lambda: V.memset(ones_b.t[:], 1.0), writes=[ones_b])
    S.op("dve", lambda: V.memset(blk1.t[:], 0.0), writes=[blk1])
    S.op("dve", lambda: V.memset(blk1.t[0:64, 0:64], 1.0 / 64), writes=[blk1])
    S.op("dve", lambda: V.memset(blk1.t[64:128, 64:128], 1.0 / 64), writes=[blk1])
    S.op("dve", lambda: V.memset(epsb.t[:], EPS), writes=[epsb])
    S.op("dve", lambda: V.memset(kTe.t[:], 0.0), writes=[kTe])
    S.op("dve", lambda: V.memset(kTo.t[:], 0.0), writes=[kTo])
    S.op("dve", lambda: V.memset(vaug.t[:], 1.0), writes=[vaug])
    S.dma("sp", gvs_.t[0:48, :], gvec, dst=gvs_)
    S.dma("sp", trl.t[:], tril, dst=trl)
    S.dma("sp", cbf.t[:], cbias.rearrange("k p q -> p k q"), dst=cbf)
    S.dma("sp", qsc.t[:], qscale, dst=qsc)
    S.dma("sp", qgk.t[0:64, 0:1], qg.rearrange("(p o) -> p o", o=1), dst=qgk)
    S.dma("sp", qgk.t[64:128, 0:1], qg.rearrange("(p o) -> p o", o=1), dst=qgk, more=True)
    S.dma("sp", qgk.t[0:64, 1:2], kg.rearrange("(p o) -> p o", o=1), dst=qgk, more=True)
    S.dma("sp", qgk.t[64:128, 1:2], kg.rearrange("(p o) -> p o", o=1), dst=qgk, more=True)
    S.dma("sp", sinkb.t[:], sinks.partition_broadcast(128), dst=sinkb)
    bk = A()
    S.op("pe", lambda: PE.transpose(bk.t[:, 0:48], gvs_.t[0:48, :], identf.t[0:48, 0:48]), reads=[gvs_, identf], writes=[bk])
    S.op("dve", lambda: V.tensor_copy(gT.t[:], bk.t[:, 0:48]), reads=[bk], writes=[gT])
    S.op("dve", lambda: V.tensor_scalar(gq.t[:], qsc.t[:], qgk.t[:, 0:1], None, ALU.mult), reads=[qsc, qgk], writes=[gq])
    S.op("act", lambda: ACT.activation(esink.t[:], sinkb.t[:], AF.Exp), reads=[sinkb], writes=[esink])
    S.op("dve", lambda: V.tensor_copy(bias4.t[:].rearrange("p (k q) -> p k q", k=9), cbf.t[:]), reads=[cbf], writes=[bias4])

    def setup_spatial_dma():
        S.dma("sp", wsf.t[:, 0, :, :], w_s.rearrange("g i j -> i g j"), dst=wsf)
        S.op("dve", lambda: V.memset(wsf.t[:, 1, :, :], 0.0), writes=[wsf])
        for b in range(4):
            S.dma("sp", wsf.t[32 * b:32 * b + 32, 1, :, 32 * b:32 * b + 32], w_s[:, 0:32, 0:32].rearrange("g i j -> i g j"), dst=wsf, more=(b > 0))
        S.dma("sp", bbc.t[:, :], b_s.rearrange("g i -> (g i)").partition_broadcast(128), dst=bbc)
        S.dma("sp", vgb.t[:], vg.partition_broadcast(128), dst=vgb)

    def setup_spatial():
        for st_ in range(2):
            for g in range(8):
                S.op("dve", lambda: V.tensor_tensor(wsf.t[:, st_, g, :], wsf.t[:, st_, g, :], trl.t[:], ALU.mult), reads=[wsf, trl], writes=[wsf])
            for q in range(2):
                bk = A()
                for j in range(4):
                    g = q * 4 + j
                    S.op("pe", lambda: PE.transpose(bk.t[:, j * 128:(j + 1) * 128], wsf.t[:, st_, g, :], identf.t[:]), reads=[wsf, identf], writes=[bk], inc=(j == 3))
                S.op("dve", lambda: V.tensor_copy(WtT.t[:, st_, q * 4:(q + 1) * 4, :], bk.t[:].rearrange("p (g i) -> p g i", g=4)), reads=[bk], writes=[WtT])

    setup_spatial_dma()
    dk = Buf("dk"); dv = Buf("dv")
    S.dma("sp", nks[:, 0:96, :], ck[:, 32:128, :], dst=dk)
    S.dma("sp", nvs[:, 0:96, :], cv[:, 32:128, :], dst=dv)
    out_bufs = [dk, dv]

    def load_tiles_T(src_rows_fn, ntiles, dst_fn, pre=None):
        for t in range(ntiles):
            if pre is not None and t < TPS:
                x_issue(pre, t)
                stg = xpre.pop((pre, t))
            else:
                stg = nxt("xin", xin)
                S.dma("sp", stg.t[:], src_rows_fn(t), dst=stg)
            for half in range(2):
                bk = A()
                for j in range(4):
                    c = half * 4 + j
                    S.op("pe", lambda: PE.transpose(bk.t[:, j * 128:(j + 1) * 128], stg.t[:, c * 128:(c + 1) * 128], identf.t[:]), reads=[stg, identf], writes=[bk], inc=(j == 3))
                for j in range(4):
                    c = half * 4 + j
                    db, dap = dst_fn(c, t)
                    eng = "act" if half == 0 else "dve"
                    if eng == "act":
                        S.op("act", lambda: ACT.copy(dap, bk.t[:, j * 128:(j + 1) * 128]), reads=[bk], writes=[db])
                    else:
                        S.op("dve", lambda: V.tensor_copy(dap, bk.t[:, j * 128:(j + 1) * 128]), reads=[bk], writes=[db])

    def rmsnorm(h_fn, groups, gcol, dbuf, d_fn, rbuf, r_fn):
        for (t0, n) in groups:
            bk = bankC
            for c in range(8):
                hb, hap = h_fn(c, t0, n)
                sq = nxt("sq", sqr)
                S.op("act", lambda: ACT.activation(sq.t[:, 0:n], hap, AF.Square), reads=[hb], writes=[sq])
                S.op("pe", lambda: PE.matmul(bk.t[:, 0:n], ones_d.t[:], sq.t[:, 0:n], start=(c == 0), stop=(c == 7)), reads=[ones_d, sq], writes=[bk], inc=True)
            S.op("act", lambda: ACT.activation(r_fn(t0, n), bk.t[:, 0:n], AF.Ln, bias=epsb.t[:], scale=1.0), reads=[bk, epsb], writes=[rbuf])
            S.op("act", lambda: ACT.activation(r_fn(t0, n), r_fn(t0, n), AF.Exp, scale=-0.5), reads=[rbuf], writes=[rbuf])
            for c in range(8):
                hb, hap = h_fn(c, t0, n)
                S.op("dve", lambda: V.scalar_tensor_tensor(d_fn(c, t0, n), hap, gT.t[:, gcol + c:gcol + c + 1], r_fn(t0, n), ALU.mult, ALU.mult), reads=[hb, gT, rbuf], writes=[dbuf[c]])

    def rmsnorm_deferred(h_fn, groups, gcol, dbuf, d_fn, rbuf, r_fn):
        for (t0, n) in groups:
            bk = bankC
            for c in range(8):
                hb, hap = h_fn(c, t0, n)
                S.op("act", lambda: ACT.activation(d_fn(c, t0, n), hap, AF.Copy, scale=gT.t[:, gcol + c:gcol + c + 1]), reads=[hb, gT], writes=[dbuf[c]])
                sq = nxt("sq", sqr)
                S.op("act", lambda: ACT.activation(sq.t[:, 0:n], hap, AF.Square), reads=[hb], writes=[sq])
                S.op("pe", lambda: PE.matmul(bk.t[:, 0:n], ones_d.t[:], sq.t[:, 0:n], start=(c == 0), stop=(c == 7)), reads=[ones_d, sq], writes=[bk], inc=True)
            S.op("act", lambda: ACT.activation(r_fn(t0, n), bk.t[:, 0:n], AF.Ln, bias=epsb.t[:], scale=1.0), reads=[bk, epsb], writes=[rbuf])
            S.op("act", lambda: ACT.activation(r_fn(t0, n), r_fn(t0, n), AF.Exp, scale=-0.5), reads=[rbuf], writes=[rbuf])

    def h_main(c, t0, n):
        return hT[c], hT[c].t[:, t0:t0 + n]

    def n_main(c, t0, n):
        return nT.t[:, c, t0:t0 + n]

    def r_main(t0, n):
        return rstd.t[:, t0:t0 + n]

    def acc_phase(tag_fn, nrb, rhs_buf, rhs_fn, groups):
        for half in range(2):
            first_c = True
            for rb in range(nrb):
                wb, wvw = wget(tag_fn(half, rb))
                for dcl in range(4):
                    for fc in range(8):
                        for gi, (t0, n) in enumerate(groups):
                            if t0 != TP:
                                bk = poolB[dcl]
                                oap = bk.t[:, 0:n]
                                st = (rb == 0 and fc == 0)
                                sk = False
                            else:
                                bk = bankC
                                oap = bk.t[:, dcl * 128:dcl * 128 + n]
                                st = first_c
                                first_c = False
                                sk = True
                            last = (rb == nrb - 1 and fc == 7)
                            S.op("pe", lambda: PE.matmul(oap, wvw[:, fc, dcl * 128:(dcl + 1) * 128], rhs_fn(rb * 8 + fc, t0, n), start=st, stop=last, skip_group_check=sk),
                                 reads=[wb, rhs_buf(rb * 8 + fc)], writes=[bk], inc=last)
                wdone()
            for dcl in range(4):
                dc = half * 4 + dcl
                for gi, (t0, n) in enumerate(groups):
                    if t0 != TP:
                        bk = poolB[dcl]; iap = bk.t[:, 0:n]
                    else:
                        bk = bankC; iap = bk.t[:, dcl * 128:dcl * 128 + n]
                    S.op("dve", lambda: V.tensor_tensor(hT[dc].t[:, t0:t0 + n], iap, hT[dc].t[:, t0:t0 + n], ALU.add), reads=[bk, hT[dc]], writes=[hT[dc]])

    def ffn(l, groups, mid=None):
        S.phase = f"F{l}norm"
        rmsnorm_deferred(h_main, groups, 16 + l * 8, nTc, n_main, rstd, r_main)
        S.phase = f"F{l}p1"
        for j in range(8):
            wb, wvw = wget(f"w1_{l}_{j}")
            for jj in range(4):
                f = j * 4 + jj
                for (t0, n) in groups:
                    bk = A()
                    for kc in range(8):
                        S.op("pe", lambda: PE.matmul(bk.t[:, 0:n], wvw[:, kc, jj * 128:(jj + 1) * 128], nT.t[:, kc, t0:t0 + n], start=(kc == 0), stop=(kc == 7)), reads=[wb, nTc[kc]], writes=[bk], inc=(kc == 7))
                    rl = nxt("gsb", gsb)
                    S.op("act", lambda: ACT.activation(rl.t[:, 0:n], bk.t[:, 0:n], AF.Relu), reads=[bk], writes=[rl])
                    S.op("dve", lambda: V.tensor_tensor(rl.t[:, 0:n], rl.t[:, 0:n], rstd.t[:, t0:t0 + n], ALU.mult), reads=[rl, rstd], writes=[rl])
                    S.op("dve", lambda: V.tensor_tensor(hidT.t[:, f, t0:t0 + n], rl.t[:, 0:n], rl.t[:, 0:n], ALU.mult), reads=[rl], writes=[hq[j]])
            wdone()
        if mid is not None:
            mid()
        S.phase = f"F{l}p2"
        acc_phase(lambda half, rb: f"w2_{l}_{half}{rb}", 4, lambda f: hq[f // 4], lambda f, t0, n: hidT.t[:, f, t0:t0 + n], groups)

    def ple_load(l, st, ntiles):
        for t in range(ntiles):
            if t < TPS:
                src = pm[l, st * TP + t * 128: st * TP + (t + 1) * 128, :]
            else:
                src = psm[l, :, :]
            S.dma("sp", pstage[t].t[:], src, dst=pstage[t])

    def ple_prep(l, st, ntiles):
        for t in range(ntiles):
            pst = pstage[t]
            bk = A()
            for j in range(2):
                S.op("pe", lambda: PE.transpose(bk.t[:, j * 128:(j + 1) * 128], pst.t[:, j * 128:(j + 1) * 128], identf.t[:]), reads=[pst, identf], writes=[bk], inc=(j == 1))
            S.op("act", lambda: ACT.copy(pT.t[:, :, t * 128:(t + 1) * 128], bk.t[:, 0:256].rearrange("p (c t) -> p c t", c=2)), reads=[bk], writes=[pT])

    def ple(l, st, groups, ntiles, is_last):
        S.phase = f"P{l}"
        rmsnorm(h_main, groups, 32 + l * 8, nTc, n_main, rstd, r_main)
        pbanks = pbank[0:7]
        pr = [0]

        def PB():
            b = pbanks[pr[0] % 7]
            pr[0] += 1
            return b
        wpb, wpv = wget(f"wp_{l}")
        for hf, nm in enumerate(["wgA", "wgB"]):
            wb, wvw = wget(f"{nm}_{l}")
            for dcl in range(4):
                dc = hf * 4 + dcl
                for (t0, n) in groups:
                    bp = PB()
                    for kc in range(2):
                        S.op("pe", lambda: PE.matmul(bp.t[:, 0:n], wpv[:, kc, dc * 128:(dc + 1) * 128], pT.t[:, kc, t0:t0 + n], start=(kc == 0), stop=(kc == 1)), reads=[wpb, pT], writes=[bp], inc=(kc == 1))
                    bg = PB()
                    for kc in range(8):
                        S.op("pe", lambda: PE.matmul(bg.t[:, 0:n], wvw[:, kc, dcl * 128:(dcl + 1) * 128], nT.t[:, kc, t0:t0 + n], start=(kc == 0), stop=(kc == 7)), reads=[wb, nTc[kc]], writes=[bg], inc=(kc == 7))
                    gs = nxt("gsb", gsb)
                    S.op("act", lambda: ACT.activation(gs.t[:, 0:n], bg.t[:, 0:n], AF.Sigmoid), reads=[bg], writes=[gs])
                    S.op("dve", lambda: V.tensor_tensor(gs.t[:, 0:n], bp.t[:, 0:n], gs.t[:, 0:n], ALU.mult), reads=[bp, gs], writes=[gs])
                    S.op("dve", lambda: V.tensor_tensor(hT[dc].t[:, t0:t0 + n], hT[dc].t[:, t0:t0 + n], gs.t[:, 0:n], ALU.add), reads=[hT[dc], gs], writes=[hT[dc]])
        wdone(3)

    def qk_chunk_proj(wb, lhs_fn, rhs_buf, rhs_fn, n):
        bk = nxt("QK", pbank[0:5])
        for kc in range(8):
            S.op("pe", lambda: PE.matmul(bk.t[:, 0:n], lhs_fn(kc), rhs_fn(kc), start=(kc == 0), stop=(kc == 7)), reads=[wb, rhs_buf(kc)], writes=[bk], inc=(kc == 7))
        sq = nxt("sq", sqr)
        S.op("act", lambda: ACT.activation(sq.t[:, 0:n], bk.t[:, 0:n], AF.Square), reads=[bk], writes=[sq])
        return (bk, sq)

    def qk_chunk_norm(bksq, n, writes_fn):
        bk, sq = bksq
        b2 = nxt("QS", pbank[5:7])
        S.op("pe", lambda: PE.matmul(b2.t[:, 0:n], blk1.t[:], sq.t[:, 0:n], start=True, stop=True), reads=[blk1, sq], writes=[b2])
        r = nxt("rq", rq)
        S.op("act", lambda: ACT.activation(r.t[:, 0:n], b2.t[:, 0:n], AF.Ln, bias=epsb.t[:], scale=1.0), reads=[b2, epsb], writes=[r])
        S.op("act", lambda: ACT.activation(r.t[:, 0:n], r.t[:, 0:n], AF.Exp, scale=-0.5), reads=[r], writes=[r])
        writes_fn(bk, r)

    def pipeline(items, proj_fn, norm_fn, depth=2):
        pend = []
        for it in items:
            pend.append((it, proj_fn(it)))
            if len(pend) > depth:
                norm_fn(*pend.pop(0))
        while pend:
            norm_fn(*pend.pop(0))

    def finish_tile(ot, tq, bank):
        bv = bank.t[:].bitcast(BF16)
        for c in range(8):
            S.op("pe", lambda: PE.transpose(bv[:, c * 128:(c + 1) * 128], ot.t[:, c * 128:(c + 1) * 128], identb.t[:]), reads=[ot, identb], writes=[bank], inc=(c == 7))
        S.op("dve", lambda: V.tensor_copy(oT.t[:, :, tq:tq + 128], bv.rearrange("p (c q) -> p c q", c=8)), reads=[bank], writes=[oT])

    def pv_norm(ob, g, ot, mm_list):
        for hh in range(4):
            n_ = len(mm_list[hh])
            for j, (lap, rap, rb) in enumerate(mm_list[hh]):
                last = (hh == 3 and j == n_ - 1)
                S.op("pe", lambda: PE.matmul(ob.t[:, hh * 65:(hh + 1) * 65], lap, rap, start=(j == 0), stop=(j == n_ - 1)), reads=rb, writes=[ob], inc=last)
        rc = nxt("rcs", rcs)
        o3 = ob.t[:, 0:260].rearrange("p (h e) -> p h e", e=65)
        S.op("dve", lambda: V.tensor_tensor(rc.t[:, 0:4], o3[:, :, 64], esink.t[:, 4 * g:4 * g + 4], ALU.add), reads=[ob, esink], writes=[rc])
        S.op("dve", lambda: V.reciprocal(rc.t[:, 0:4], rc.t[:, 0:4]), reads=[rc], writes=[rc])
        S.op("dve", lambda: V.tensor_tensor(ot.t[:, g * 256:(g + 1) * 256].rearrange("p (h d) -> p h d", h=4), o3[:, :, 0:64], rc.t[:, 0:4].unsqueeze(2).to_broadcast([128, 4, 64]), ALU.mult), reads=[ob, rc], writes=[ot])

    def attention_prompt(st, tiles):
        Sx = [poolA[0], poolA[1], poolA[2], bankC]
        units = [(t, g) for t in tiles for g in range(4)]

        def front(i):
            t, g = units[i]
            tq = t * 128
            pbi = 2 if (st == 0 and t == 0) else 0
            pu = Pu[i % 4]
            for half in range(2):
                sbk = Sx[2 * (i % 2) + half]
                S.op("pe", lambda: PE.matmul(sbk.t[:, 0:512], identb.t[:], bias4.t[:, pbi * 128:(pbi + 2) * 128].unsqueeze(1).to_broadcast([128, 2, 256]), start=True, stop=False), reads=[identb, bias4], writes=[sbk], inc=False)
                for hl in range(2):
                    h = 4 * g + 2 * half + hl
                    c = h // 2
                    kb = kTe if h % 2 == 0 else kTo
                    for kt in range(2):
                        last = (hl == 1 and kt == 1)
                        S.op("pe", lambda: PE.matmul(sbk.t[:, hl * 256 + kt * 128:hl * 256 + (kt + 1) * 128], kb.t[:, g, (t + kt) * 128:(t + kt + 1) * 128], qT.t[:, c, tq:tq + 128], start=False, stop=last), reads=[kTe, kTo, qT], writes=[sbk], inc=last)
                for hl in range(2):
                    h = 4 * g + 2 * half + hl
                    S.op("act", lambda: ACT.activation(pu.t[:, half, hl * 256:(hl + 1) * 256], sbk.t[:, hl * 256:(hl + 1) * 256], AF.Exp, scale=float(SLOPES[h])), reads=[sbk], writes=[pu])

        def back(i):
            t, g = units[i]
            tq = t * 128
            pu = Pu[i % 4]
            ob = poolB[i % 2]
            ot = otok[t % 2]
            mm = []
            for hh in range(4):
                half, hl = hh // 2, hh % 2
                mm.append([(pu.t[:, half, hl * 256 + kt * 128:hl * 256 + (kt + 1) * 128], vaug.t[:, t + kt, g, :], [pu, vaug]) for kt in range(2)])
            pv_norm(ob, g, ot, mm)
            if g == 3:
                finish_tile(ot, tq, poolB[2 + (t % 2)])

        LAG = 2
        for i in range(len(units) + LAG):
            if i < len(units):
                front(i)
            if i >= LAG:
                back(i - LAG)

    def attention_tile(tq, keytiles):
        nkt = len(keytiles)
        for g in range(4):
            Ps = []
            for (kbufs, ke_fn, ko_fn, vbuf, v_fn, bidx) in keytiles:
                sbk = A()
                S.op("pe", lambda: PE.matmul(sbk.t[:, 0:512], identb.t[:], bias4.t[:, bidx * 128:(bidx + 1) * 128].unsqueeze(1).to_broadcast([128, 4, 128]), start=True, stop=False), reads=[identb, bias4], writes=[sbk], inc=False)
                for hh in range(4):
                    h = 4 * g + hh
                    c = h // 2
                    kap = ke_fn(g) if h % 2 == 0 else ko_fn(g)
                    S.op("pe", lambda: PE.matmul(sbk.t[:, hh * 128:(hh + 1) * 128], kap, qT.t[:, c, tq:tq + 128], start=False, stop=(hh == 3)), reads=list(kbufs) + [qT], writes=[sbk], inc=(hh == 3))
                pb_ = nxt("P", Pb)
                for hh in range(4):
                    h = 4 * g + hh
                    S.op("act", lambda: ACT.activation(pb_.t[:, hh * 128:(hh + 1) * 128], sbk.t[:, hh * 128:(hh + 1) * 128], AF.Exp, scale=float(SLOPES[h])), reads=[sbk], writes=[pb_])
                Ps.append(pb_)
            ob = poolB[g % 2]
            mm = []
            for hh in range(4):
                mm.append([(Ps[kt].t[:, hh * 128:(hh + 1) * 128], keytiles[kt][4](g), [Ps[kt], keytiles[kt][3]]) for kt in range(nkt)])
            pv_norm(ob, g, otok[0], mm)
        finish_tile(otok[0], tq, poolB[2])

    def chk(name):
        S.phase = name
        if stop == name:
            raise _Stop()

    cur_view = attn_bufs
    try:
      chk("setup")
      for st in range(NST):
          is_last = (st == NST - 1)
          ntiles = TPS + (1 if is_last else 0)
          T = ntiles * 128
          groups = []
          t0 = 0
          while t0 < TP:
              n = min(512, TP - t0)
              groups.append((t0, n))
              t0 += n
          if is_last:
              groups.append((TP, 128))

          if st > 0:
              S.inherit(attn_bufs, cur_view)
              cur_view = attn_bufs

          def xrows(t):
              if t < TPS:
                  return xm[st * TP + t * 128: st * TP + (t + 1) * 128, :]
              return xs[:, :]
          load_tiles_T(xrows, ntiles, lambda c, t: (hT[c], hT[c].t[:, t * 128:(t + 1) * 128]), pre=st)
          if st == 0:
              setup_spatial()
              S.inherit(attn_bufs, setup_bufs)
          chk("load")
          if st == 0:
              load_tiles_T(lambda t: xh[:, :], 1, lambda c, t: (hTh, hTh.t[:, c, :]))
              rmsnorm(lambda c, t0, n: (hTh, hTh.t[:, c, :]), [(0, 128)], 0, [nTh] * 8, lambda c, t0, n: nTh.t[:, c, :], rq[0], lambda t0, n: rq[0].t[:, 0:128])
          if is_last:
              S.op("dve", lambda: V.memset(kcTe.t[:], 0.0), writes=[kcTe])
              S.op("dve", lambda: V.memset(kcTo.t[:], 0.0), writes=[kcTo])
              S.op("dve", lambda: V.memset(vcaug.t[:], 1.0), writes=[vcaug])
              for b in range(4):
                  S.dma("pool", vcaug.t[:, b, :, 0:64], cv[b].rearrange("r (g d) -> r g d", g=4), dst=vcaug, more=(b > 0))
              for b in range(4):
                  stg = nxt("st", stage)
                  for two in range(2):
                      S.dma("sp", stg.t[:, 0:512].rearrange("p (g two d) -> p g two d", g=4, two=2)[:, :, two, :], ck[b].rearrange("r (g d) -> r g d", g=4), dst=stg, more=(two > 0))
                  bk = A()
                  for g in range(4):
                      S.op("pe", lambda: PE.transpose(bk.t[:, g * 128:(g + 1) * 128], stg.t[:, g * 128:(g + 1) * 128], identf.t[:]), reads=[stg, identf], writes=[bk], inc=(g == 3))
                  S.op("dve", lambda: V.tensor_copy(kcTe.t[0:64, b, :, :], bk.t[0:64, :].rearrange("p (g k) -> p g k", g=4)), reads=[bk], writes=[kcTe])
                  S.op("act", lambda: ACT.copy(kcTo.t[64:128, b, :, :], bk.t[64:128, :].rearrange("p (g k) -> p g k", g=4)), reads=[bk], writes=[kcTo])

          chk("prep")
          rmsnorm(h_main, groups, 0, nTc, n_main, rstd, r_main)
          chk("norm0")
          wq = {}
          items = [(c, t0, n) for c in range(8) for (t0, n) in groups]

          def qproj(it):
              c, t0, n = it
              if c == 0 and "a" not in wq:
                  wq["a"] = wget("wqA")
              if c == 4 and "b" not in wq:
                  wdone()
                  wq["b"] = wget("wqB")
              wb, wvw = wq["a"] if c < 4 else wq["b"]
              return qk_chunk_proj(wb, lambda kc: wvw[:, kc, (c % 4) * 128:(c % 4 + 1) * 128], (lambda kc: nTc[kc]), lambda kc: nT.t[:, kc, t0:t0 + n], n)

          def qnorm(it, bk):
              c, t0, n = it

              def wr(bk, r):
                  S.op("dve", lambda: V.scalar_tensor_tensor(qT.t[:, c, t0:t0 + n], bk.t[:, 0:n], gq.t[:, c:c + 1], r.t[:, 0:n], ALU.mult, ALU.mult), reads=[bk, gq, r], writes=[qT])
              qk_chunk_norm(bk, n, wr)
          pipeline(items, qproj, qnorm)
          wdone()
          chk("q")
          wkb, wkv = wget("wkD")
          kgroups = [("m", t0, n) for (t0, n) in groups] + ([("h", 0, 128)] if st == 0 else [])
          items = [(g, kind, t0, n) for g in range(4) for (kind, t0, n) in kgroups]
          need_kout = is_last

          def kproj(it):
              g, kind, t0, n = it
              if kind == "m":
                  return qk_chunk_proj(wkb, lambda kc: wkv[:, kc, g * 128:(g + 1) * 128], (lambda kc: nTc[kc]), lambda kc: nT.t[:, kc, t0:t0 + n], n)
              return qk_chunk_proj(wkb, lambda kc: wkv[:, kc, g * 128:(g + 1) * 128], (lambda kc: nTh), lambda kc: nTh.t[:, kc, :], n)

          def knorm(it, bk):
              g, kind, t0, n = it
              k0 = 128 + t0 if kind == "m" else 0

              def wr(bk, r):
                  S.op("dve", lambda: V.scalar_tensor_tensor(kTe.t[0:64, g, k0:k0 + n], bk.t[0:64, 0:n], qgk.t[0:64, 1:2], r.t[0:64, 0:n], ALU.mult, ALU.mult), reads=[bk, qgk, r], writes=[kTe])
                  S.op("dve", lambda: V.scalar_tensor_tensor(kTo.t[64:128, g, k0:k0 + n], bk.t[64:128, 0:n], qgk.t[64:128, 1:2], r.t[64:128, 0:n], ALU.mult, ALU.mult), reads=[bk, qgk, r], writes=[kTo])
                  if need_kout and kind == "m":
                      outs = []
                      if t0 <= (TPS - 1) * 128 < t0 + n:
                          outs.append(((TPS - 1) * 128 - t0, 0))
                      if t0 == TP:
                          outs.append((0, 1))
                      for (off, which) in outs:
                          S.op("dve", lambda: V.scalar_tensor_tensor(kf32.t[:, which, g, :], bk.t[:, off:off + 128], qgk.t[:, 1:2], r.t[:, off:off + 128], ALU.mult, ALU.mult), reads=[bk, qgk, r], writes=[kf32])
              qk_chunk_norm(bk, n, wr)

          pipeline(items, kproj, knorm)
          wdone()
          if need_kout:
              for which in range(2):
                  bk = A()
                  for g in range(4):
                      S.op("pe", lambda: PE.transpose(bk.t[:, g * 128:(g + 1) * 128], kf32.t[:, which, g, :], identf.t[:]), reads=[kf32, identf], writes=[bk], inc=(g == 3))
                  stg = nxt("st", stage)
                  S.op("act", lambda: ACT.copy(stg.t[:, 0:256].rearrange("p (g d) -> p g d", g=4), bk.t[:].rearrange("p (g x) -> p g x", g=4)[:, :, 0:64]), reads=[bk], writes=[stg])
                  if which == 0:
                      S.dma("sp", nkp[:, :], stg.t[:, 0:256], src=stg)
                  else:
                      for b in range(4):
                          S.dma("sp", nks[b, 96:128, :], stg.t[32 * b:32 * b + 32, 0:256], src=stg, more=(b > 0))
                  out_bufs.append(stg)
          chk("k")
          wvb, wvv_ = wget("wvD")
          vt = [("m", t) for t in range(ntiles)] + ([("h", 0)] if st == 0 else [])
          for (kind, t) in vt:
              bk = A()
              for kc in range(8):
                  lhs = nT.t[:, kc, t * 128:(t + 1) * 128] if kind == "m" else nTh.t[:, kc, :]
                  S.op("pe", lambda: PE.matmul(bk.t[:, 0:256], lhs, wvv_[:, kc, :], start=(kc == 0), stop=(kc == 7)), reads=[wvb, nTc[kc] if kind == "m" else nTh], writes=[bk], inc=(kc == 7))
              slot = t + 1 if kind == "m" else 0
              S.op("act", lambda: ACT.copy(vaug.t[:, slot, :, 0:64], bk.t[:, 0:256].rearrange("p (g d) -> p g d", g=4)), reads=[bk], writes=[vaug])
              if is_last and kind == "m" and t >= TPS - 1:
                  stg = nxt("st", stage)
                  S.op("act", lambda: ACT.copy(stg.t[:, 0:256], bk.t[:, 0:256]), reads=[bk], writes=[stg])
                  if t == TPS - 1:
                      S.dma("sp", nvp[:, :], stg.t[:, 0:256], src=stg)
                  else:
                      for b in range(4):
                          S.dma("sp", nvs[b, 96:128, :], stg.t[32 * b:32 * b + 32, 0:256], src=stg, more=(b > 0))
                  out_bufs.append(stg)
          wdone()
          chk("v")
          attention_prompt(st, list(range(TPS)))
          for t in range(TPS, ntiles):
              kts = []
              for b in range(4):
                  kts.append(([kcTe, kcTo], (lambda g, b=b: kcTe.t[:, b, g, :]), (lambda g, b=b: kcTo.t[:, b, g, :]), vcaug, (lambda g, b=b: vcaug.t[:, b, g, :]), 5 + b))
              kts.append(([kTe, kTo], (lambda g, t=t: kTe.t[:, g, (t + 1) * 128:(t + 2) * 128]), (lambda g, t=t: kTo.t[:, g, (t + 1) * 128:(t + 2) * 128]), vaug, (lambda g, t=t: vaug.t[:, t + 1, g, :]), 4))
              attention_tile(t * 128, kts)
          chk("attn")
          if not is_last:
              S.op("dve", lambda: V.tensor_copy(kTe.t[0:64, :, 0:128], kTe.t[0:64, :, TP:TP + 128]), reads=[kTe], writes=[kTe])
              S.op("dve", lambda: V.tensor_copy(kTo.t[64:128, :, 0:128], kTo.t[64:128, :, TP:TP + 128]), reads=[kTo], writes=[kTo])
              S.op("dve", lambda: V.tensor_copy(vaug.t[:, 0, :, :], vaug.t[:, TPS, :, :]), reads=[vaug], writes=[vaug])
          for hf, nm in enumerate(["woA", "woB"]):
              wb, wvw = wget(nm)
              for dcl in range(4):
                  dc = hf * 4 + dcl
                  for (t0, n) in groups:
                      bk = A()
                      for kc in range(8):
                          S.op("pe", lambda: PE.matmul(bk.t[:, 0:n], wvw[:, kc, dcl * 128:(dcl + 1) * 128], oT.t[:, kc, t0:t0 + n], start=(kc == 0), stop=(kc == 7)), reads=[wb, oT], writes=[bk], inc=(kc == 7))
                      S.op("dve", lambda: V.tensor_tensor(hT[dc].t[:, t0:t0 + n], bk.t[:, 0:n], hT[dc].t[:, t0:t0 + n], ALU.add), reads=[bk, hT[dc]], writes=[hT[dc]])
              wdone()
          chk("wo")
          S.inherit(ffn_bufs, cur_view); cur_view = ffn_bufs
          ple_load(0, st, ntiles)
          ffn(0, groups, mid=lambda: ple_prep(0, st, ntiles))
          chk("ffn0")
          ple(0, st, groups, ntiles, is_last)
          chk("ple0")

          S.inherit(gm_bufs, cur_view); cur_view = gm_bufs
          S.phase = "Gnorm"
          rmsnorm(h_main, groups, 8, nTc, n_main, rstd, r_main)
          S.phase = "Gv"
          S.op("dve", lambda: V.memset(ssq.t[:], 0.0), writes=[ssq])
          for j in range(6):
              wb, wvw = wget(f"wvv{j}")
              for t in range(ntiles):
                  bk = A()
                  for kc in range(8):
                      S.op("pe", lambda: PE.matmul(bk.t[:, 0:512], nT.t[:, kc, t * 128:(t + 1) * 128], wvw[:, kc, :], start=(kc == 0), stop=(kc == 7)), reads=[wb, nTc[kc]], writes=[bk], inc=(kc == 7))
                  S.op("act", lambda: ACT.activation(gv.t[:, t, j * 512:(j + 1) * 512], bk.t[:, 0:512], AF.Gelu_apprx_tanh), reads=[bk], writes=[gvt[t]])
                  sq = nxt("sq", sqr)
                  S.op("dve", lambda: V.scalar_tensor_tensor(sq.t[:, 0:512], gv.t[:, t, j * 512:(j + 1) * 512], 1.0, gv.t[:, t, j * 512:(j + 1) * 512], ALU.mult, ALU.mult, accum_out=ssq.t[:, t * 6 + j:t * 6 + j + 1]), reads=[gvt[t]], writes=[sq, ssq])
              wdone()
          for t in range(ntiles):
              S.op("dve", lambda: V.reduce_sum(rsv.t[:, t:t + 1], ssq.t[:, t * 6:(t + 1) * 6], axis=AX.X), reads=[ssq], writes=[rsv])
          S.op("act", lambda: ACT.activation(rsv.t[:, 0:ntiles], rsv.t[:, 0:ntiles], AF.Ln, bias=epsb.t[:], scale=1.0 / 3072), reads=[rsv, epsb], writes=[rsv])
          S.op("act", lambda: ACT.activation(rsv.t[:, 0:ntiles], rsv.t[:, 0:ntiles], AF.Exp, scale=-0.5), reads=[rsv], writes=[rsv])
          for t in range(ntiles):
              if t == TPS:
                  for q in range(3):
                      stg = nxt("st", stage)
                      S.op("dve", lambda: V.scalar_tensor_tensor(stg.t[:, :], gv.t[:, t, q * 1024:(q + 1) * 1024], rsv.t[:, t:t + 1], vgb.t[:, q * 1024:(q + 1) * 1024], ALU.mult, ALU.mult), reads=[gvt[t], rsv, vgb], writes=[stg])
                      S.dma("sp", gvs[:, q * 1024:(q + 1) * 1024], stg.t[:, :], src=stg)
                      out_bufs.append(stg)
              S.op("dve", lambda: V.scalar_tensor_tensor(gv.t[:, t, :], gv.t[:, t, :], rsv.t[:, t:t + 1], vgb.t[:, :], ALU.mult, ALU.mult), reads=[gvt[t], rsv, vgb], writes=[gvt[t]])
          S.phase = "Gu"
          sp_pend = []

          def spatial(g):
              sp_pend.extend((g, t) for t in range(ntiles))

          def sp_flush(k):
              for _ in range(min(k, len(sp_pend))):
                  spatial1(*sp_pend.pop(0))

          def spatial1(g, t):
              if True:
                  wset = 1 if t == TPS else 0
                  bk = nxt("B", poolB)
                  for j in range(3):
                      cc = g * 3 + j
                      S.op("pe", lambda: PE.matmul(bk.t[:, j * 128:(j + 1) * 128], gv.t[:, t, cc * 128:(cc + 1) * 128], WtT.t[:, wset, g, :], start=True, stop=True), reads=[gvt[t], WtT], writes=[bk], inc=(j == 2))
                  sg = nxt("sq", sqr)
                  if wset == 0:
                      S.op("dve", lambda: V.tensor_tensor(sg.t[:, 0:384].rearrange("p (j i) -> p j i", j=3), bk.t[:, 0:384].rearrange("p (j i) -> p j i", j=3), bbc.t[:, g * 128:(g + 1) * 128].unsqueeze(1).to_broadcast([128, 3, 128]), ALU.add), reads=[bk, bbc], writes=[sg])
                  else:
                      S.op("dve", lambda: V.tensor_tensor(sg.t[:, 0:384].rearrange("p (j r i) -> p j r i", j=3, r=4), bk.t[:, 0:384].rearrange("p (j r i) -> p j r i", j=3, r=4), bbc.t[:, g * 128:g * 128 + 32].unsqueeze(1).unsqueeze(1).to_broadcast([128, 3, 4, 32]), ALU.add), reads=[bk, bbc], writes=[sg])
                  S.op("dve", lambda: V.tensor_tensor(uT.t[:, g * 3:(g + 1) * 3, t * 128:(t + 1) * 128], uT.t[:, g * 3:(g + 1) * 3, t * 128:(t + 1) * 128], sg.t[:, 0:384].rearrange("p (j i) -> p j i", j=3), ALU.mult), reads=[ug[g], sg], writes=[ug[g]])
          gdone = 0
          for j in range(6):
              wb, wvw = wget(f"wu{j}")
              for jj in range(4):
                  uc = j * 4 + jj
                  for (t0, n) in groups:
                      bk = A()
                      for kc in range(8):
                          S.op("pe", lambda: PE.matmul(bk.t[:, 0:n], wvw[:, kc, jj * 128:(jj + 1) * 128], nT.t[:, kc, t0:t0 + n], start=(kc == 0), stop=(kc == 7)), reads=[wb, nTc[kc]], writes=[bk], inc=(kc == 7))
                      S.op("act", lambda: ACT.activation(uT.t[:, uc, t0:t0 + n], bk.t[:, 0:n], AF.Gelu_apprx_tanh), reads=[bk], writes=[ug[uc // 3]])
                  while gdone < 8 and 3 * gdone + 2 < uc - 8:
                      spatial(gdone)
                      gdone += 1
                  sp_flush(2)
              wdone()
          while gdone < 8:
              spatial(gdone)
              gdone += 1
          sp_flush(len(sp_pend))
          S.phase = "Gout"
          acc_phase(lambda half, rb: f"wout{half}{rb}", 3, lambda f: ug[f // 3], lambda f, t0, n: uT.t[:, f, t0:t0 + n], groups)
          chk("gmlp")
          S.inherit(ffn_bufs, cur_view); cur_view = ffn_bufs
          ple_load(1, st, ntiles)
          ffn(1, groups, mid=lambda: ple_prep(1, st, ntiles))
          chk("ffn1")
          ple(1, st, groups, ntiles, is_last)
          chk("ple1")
          S.phase = "out"
          for t_ in range(XIN):
              x_issue(st + 1, t_)
          for t in range(ntiles):
              stg = nxt("st", stage)
              for half in range(2):
                  bk = A()
                  for j in range(4):
                      c = half * 4 + j
                      S.op("pe", lambda: PE.transpose(bk.t[:, j * 128:(j + 1) * 128], hT[c].t[:, t * 128:(t + 1) * 128], identf.t[:]), reads=[hT[c], identf], writes=[bk], inc=(j == 3))
                  if half == 0:
                      S.op("act", lambda: ACT.copy(stg.t[:, 0:512], bk.t[:, 0:512]), reads=[bk], writes=[stg])
                  else:
                      S.op("dve", lambda: V.tensor_copy(stg.t[:, 512:1024], bk.t[:, 0:512]), reads=[bk], writes=[stg])
              if t < TPS:
                  S.dma("sp", ym[st * TP + t * 128: st * TP + (t + 1) * 128, :], stg.t[:, :], src=stg)
              else:
                  S.dma("sp", ys[:, :], stg.t[:, :], src=stg)
              out_bufs.append(stg)

    except _Stop:
        pass
    if stop is None:
        assert wstate["used"] == len(wspecs), (wstate, len(wspecs))
    S.finish("sp", out_bufs)
    stats = (S.n_inst, S.n_wait, len(S.sem))
    nc._pe_phase = S.pe_phase
    nc._pe_waits = S.pe_waits
    S.close()
    es.close()
    return nc, stats


def _const_inputs(is_second_half):
    kk = np.arange(128)[:, None]
    qq = np.arange(128)[None, :]
    own = np.where((kk // 64) <= (qq // 64), -np.abs(qq - kk), NEG).astype(np.float32)
    prev = np.where((qq < 64) | (kk >= 64), -(qq + 128 - kk), NEG).astype(np.float32)
    prev0 = prev if is_second_half else np.full((128, 128), NEG, np.float32)
    s_own = np.where((kk // 32) == (qq // 32), -np.abs(qq % 32 - kk % 32), NEG).astype(np.float32)
    sc = [np.where((qq // 32) == b, -((qq % 32) + 128 - kk), NEG).astype(np.float32) for b in range(4)]
    cbias = np.stack([prev, own, prev0, own, s_own] + sc).astype(np.float32)
    tril = (np.arange(128)[None, :] <= np.arange(128)[:, None]).astype(np.float32)
    p = np.arange(128)[:, None]
    c = np.arange(8)[None, :]
    h = 2 * c + p // 64
    qscale = (1.0 / (8.0 * np.array(SLOPES, np.float64)[h])).astype(np.float32)
    return cbias, tril, qscale


_PROG = {}


def make_in_maps(inp, n_cores, NST, TPS, seq):
    f = lambda a: np.ascontiguousarray(np.asarray(a, dtype=np.float32))
    half = seq // 2
    NTP = NST * TPS * 128
    assert NTP == half
    xp = f(inp["x_prompt"]); xs_ = f(inp["x_sample"]); pp = f(inp["p_prompt"]); ps_ = f(inp["p_sample"])
    ckk = f(inp["cache_k"]); cvv = f(inp["cache_v"])
    gvec = np.concatenate([f(inp["g_mix"]).reshape(16, 128), f(inp["g_ffn"]).reshape(16, 128), f(inp["g_ple"]).reshape(16, 128)], 0)
    shared = dict(
        gvec=np.ascontiguousarray(gvec), qg=f(inp["attn_q_norm"])[0], kg=f(inp["attn_k_norm"])[0],
        sinks=f(inp["attn_sinks"])[0], w_qkv=f(inp["attn_w_qkv"])[0], w_o=f(inp["attn_w_o"])[0],
        w_uv=f(inp["gmlp_w_uv"])[0], vg=f(inp["gmlp_v_norm"])[0], w_s=f(inp["gmlp_w_s"])[0], b_s=f(inp["gmlp_b_s"])[0],
        w_out=f(inp["gmlp_w_out"])[0], w1=f(inp["ffn_w1"]), w2=f(inp["ffn_w2"]), wp=f(inp["ple_w_proj"]), wg=f(inp["ple_w_gate"]),
    )
    maps = []
    for core in range(n_cores):
        b, hf = core // 2, core % 2
        cb, tril, qscale = _const_inputs(hf == 1)
        m = dict(shared)
        m["xm"] = np.ascontiguousarray(xp[b, hf * half:(hf + 1) * half])
        m["xh"] = np.ascontiguousarray(xp[b, half - 128:half]) if hf == 1 else np.zeros((128, 1024), np.float32)
        m["xs"] = np.ascontiguousarray(xs_[4 * core:4 * core + 4].reshape(128, 1024))
        m["pm"] = np.ascontiguousarray(pp[:, b, hf * half:(hf + 1) * half])
        m["psm"] = np.ascontiguousarray(ps_[:, 4 * core:4 * core + 4].reshape(2, 128, 256))
        m["ck"] = np.ascontiguousarray(ckk[0, 4 * core:4 * core + 4].reshape(4, 128, 256))
        m["cv"] = np.ascontiguousarray(cvv[0, 4 * core:4 * core + 4].reshape(4, 128, 256))
        m["cbias"] = cb; m["tril"] = tril; m["qscale"] = qscale
        maps.append(m)
    return maps


def assemble(results, n_cores, seq):
    nb = n_cores // 2
    half = seq // 2
    y_prompt = np.zeros((nb, seq, 1024), np.float32)
    y_sample = np.zeros((4 * n_cores, 32, 1024), np.float32)
    nkp = np.zeros((1, nb, 128, 4, 64), np.float32); nvp = np.zeros_like(nkp)
    nks = np.zeros((1, 4 * n_cores, 128, 4, 64), np.float32); nvs = np.zeros_like(nks)
    gvs = np.zeros((1, 4 * n_cores, 32, 3072), np.float32)
    for core in range(n_cores):
        r = results[core]
        b, hf = core // 2, core % 2
        y_prompt[b, hf * half:(hf + 1) * half] = r["ym"]
        y_sample[4 * core:4 * core + 4] = r["ys"].reshape(4, 32, 1024)
        if hf == 1:
            nkp[0, b] = r["nkp"].reshape(128, 4, 64)
            nvp[0, b] = r["nvp"].reshape(128, 4, 64)
        nks[0, 4 * core:4 * core + 4] = r["nks"].reshape(4, 128, 4, 64)
        nvs[0, 4 * core:4 * core + 4] = r["nvs"].reshape(4, 128, 4, 64)
        gvs[0, 4 * core:4 * core + 4] = r["gvs"].reshape(4, 32, 3072)
    return (y_prompt, y_sample, nkp, nvp, nks, nvs, gvs)


def kernel(**inputs):
    n_cores = 8
    NST, TPS, seq = 4, 4, 4096
    key = (NST, TPS)
    if key not in _PROG:
        _PROG[key] = build_program(NST, TPS)[0]
    nc = _PROG[key]
    in_maps = make_in_maps(inputs, n_cores, NST, TPS, seq)
    res = run_bass_kernel_spmd(nc, in_maps, core_ids=list(range(n_cores)))
    return assemble(res.results, n_cores, seq)
```
